# Optimizing a Trainium2 kernel written in Bass

```python
import math
import jax, jax.numpy as jnp
from jax import lax
import numpy as np

D_MODEL = 1024
BATCH = 2
SEQ = 8192
DEPTH = 2

GRID_W = 64
CTX_LEN = 256
N_EVEN = (DEPTH + 1) // 2
N_ODD = DEPTH // 2
D_FF = -(-8 * D_MODEL // (3 * 256)) * 256
RMS_EPS = 1e-6
FOURIER_CH = D_MODEL // 2
FOURIER_GROUPS = 4
FOURIER_GDIM = FOURIER_CH // FOURIER_GROUPS
HY_CH = D_MODEL - FOURIER_CH
HY_ORDER = 2
HY_SHORT = 3
HY_BANDS = 16
HY_EMB = 1 + 2 * HY_BANDS
HY_FILTER_HID = 64
HY_FAST_DECAY = 0.3
HY_SLOW_DECAY = 1.5
HY_DECAY_TARGET = 1e-2
AB_IN = FOURIER_CH + (HY_ORDER + 1) * HY_CH
NA_HEADS = 16
NA_HEAD_DIM = D_MODEL // NA_HEADS
NA_KH_MAX = 8
NA_KW = 16
NA_QCOLS = NA_KW
NA_KCOLS = 2 * NA_KW

kernel_name = "hybrid_fnet_hyena_natten_dit"


def rms_norm(x, g):
    x32 = x.astype(jnp.float32)
    y = x32 * lax.rsqrt(jnp.mean(x32 * x32, axis=-1, keepdims=True) + RMS_EPS)
    return (y * g.astype(jnp.float32)).astype(x.dtype)


def ada_mod(cond, w, b):
    m = (jax.nn.silu(cond) @ w + b)[..., None, :]
    return jnp.split(m, 6, axis=-1)


def modulate(h, shift, scale):
    return h * (1.0 + scale) + shift


def swiglu(u, w_gate, w_up, w_down):
    return (jax.nn.silu(u @ w_gate) * (u @ w_up)) @ w_down


def fourier_mix(p):
    B_, L, _ = p.shape
    g = p.astype(jnp.float32).reshape(B_, L, FOURIER_GROUPS, FOURIER_GDIM).transpose(0, 2, 1, 3)
    f = jnp.fft.fft2(g, norm="ortho").real
    return f.transpose(0, 2, 1, 3).reshape(B_, L, FOURIER_CH).astype(p.dtype)


def short_conv(u, w, b):
    K, C = w.shape
    pad = K // 2
    y = lax.conv_general_dilated(u, w[:, None, :].astype(u.dtype), (1,), [(pad, K - 1 - pad)],
                                 dimension_numbers=("NWC", "WIO", "NWC"), feature_group_count=C)
    return y + b.astype(u.dtype)


def hyena_filters(L, w1, b1, freq, w2, b2, w3):
    f32 = jnp.float32
    w1, b1, freq, w2, b2, w3 = (a.astype(f32) for a in (w1, b1, freq, w2, b2, w3))
    t01 = jnp.linspace(0.0, 1.0, L, dtype=f32)[:, None]
    bands = jnp.linspace(1e-4, HY_BANDS - 1, HY_BANDS, dtype=f32)[None, :]
    ang = (2.0 * math.pi / L) * jnp.arange(L, dtype=f32)[:, None] * bands
    z = jnp.concatenate([t01, jnp.cos(ang), -jnp.sin(ang)], axis=-1)
    h = jnp.sin(freq * (z @ w1 + b1))
    h = jnp.sin(freq * (h @ w2 + b2))
    h = (h @ w3).reshape(L, 2, HY_ORDER, HY_CH)
    max_decay = math.log(HY_DECAY_TARGET) / HY_FAST_DECAY
    min_decay = math.log(HY_DECAY_TARGET) / HY_SLOW_DECAY
    deltas = jnp.abs(jnp.linspace(min_decay, max_decay, HY_CH, dtype=f32))
    h = h * jnp.exp(-t01 * deltas)[:, None, None, :]
    h = h / jnp.sum(jnp.abs(h), axis=(0, 1), keepdims=True)
    fwd, bwd = h[:, 0], h[:, 1]
    return jnp.concatenate([fwd, jnp.zeros_like(fwd[:1]), bwd[:0:-1]], axis=0)


def long_conv(u, filt, skip):
    L = u.shape[1]
    uf = jnp.fft.rfft(u, n=2 * L, axis=1)
    ff = jnp.fft.rfft(filt, n=2 * L, axis=0)
    y = jnp.fft.irfft(uf * ff[None], n=2 * L, axis=1)[:, :L]
    return y + u * skip


def hyena_mix(p, conv_w, conv_b, w1, b1, freq, w2, b2, w3, skip):
    L = p.shape[1]
    z = short_conv(p, conv_w, conv_b).astype(jnp.float32)
    x1, x2, v = jnp.split(z, 3, axis=-1)
    filt = hyena_filters(L, w1, b1, freq, w2, b2, w3)
    skip = skip.astype(jnp.float32)
    v = x1 * long_conv(v, filt[:, 0], skip[0])
    v = x2 * long_conv(v, filt[:, 1], skip[1])
    return v.astype(p.dtype)


def mixer_ab(u, w_in, w_out, conv_w, conv_b, w1, b1, freq, w2, b2, w3, skip):
    p = u @ w_in
    y_f = fourier_mix(p[..., :FOURIER_CH])
    y_h = hyena_mix(p[..., FOURIER_CH:], conv_w, conv_b, w1, b1, freq, w2, b2, w3, skip)
    return jnp.concatenate([y_f, y_h], axis=-1) @ w_out


def dense_attention(q, k, v):
    s = jnp.einsum("bqhd,bkhd->bhqk", q, k).astype(jnp.float32) * (q.shape[-1] ** -0.5)
    p = jax.nn.softmax(s, axis=-1).astype(v.dtype)
    return jnp.einsum("bhqk,bkhd->bqhd", p, v)


def neighbourhood_attention(q, k, v, kc, vc, rpb):
    B_, L, H, dh = q.shape
    rows = L // GRID_W
    kh = min(NA_KH_MAX, rows)
    scale = dh ** -0.5
    n_chunk = GRID_W // NA_QCOLS
    qcol = np.arange(GRID_W).reshape(n_chunk, NA_QCOLS)
    kstart = np.clip(qcol[:, 0] - NA_KW // 2, 0, GRID_W - NA_KCOLS)
    kcol = kstart[:, None] + np.arange(NA_KCOLS)
    wstart = np.clip(qcol - NA_KW // 2, 0, GRID_W - NA_KW)
    col_ok = (kcol[:, None, :] >= wstart[:, :, None]) & (kcol[:, None, :] < wstart[:, :, None] + NA_KW)
    dc = np.clip(kcol[:, None, :] - qcol[:, :, None] + NA_KW - 1, 0, 2 * NA_KW - 2)
    qg = q.reshape(B_, rows, GRID_W, H, dh)
    kg = k.reshape(B_, rows, GRID_W, H, dh)
    vg = v.reshape(B_, rows, GRID_W, H, dh)
    n_loc = kh * NA_KCOLS

    def row_block(r):
        rs = jnp.clip(r - kh // 2, 0, rows - kh)
        q_r = lax.dynamic_index_in_dim(qg, r, axis=1, keepdims=False).reshape(B_, n_chunk, NA_QCOLS, H, dh)
        k_b = lax.dynamic_slice_in_dim(kg, rs, kh, axis=1)[:, :, kcol]
        v_b = lax.dynamic_slice_in_dim(vg, rs, kh, axis=1)[:, :, kcol]
        dr = rs + jnp.arange(kh) - r + (NA_KH_MAX - 1)
        bias = rpb[:, dr[None, None, :, None], dc[:, :, None, :]].astype(jnp.float32)
        s_loc = jnp.einsum("bcqhd,bicjhd->bhcqij", q_r, k_b).astype(jnp.float32) * scale + bias[None]
        s_loc = jnp.where(col_ok[:, :, None, :], s_loc, -jnp.inf)
        s_ctx = jnp.einsum("bcqhd,bjhd->bhcqj", q_r, kc).astype(jnp.float32) * scale
        s = jnp.concatenate([s_loc.reshape(B_, H, n_chunk, NA_QCOLS, n_loc), s_ctx], axis=-1)
        p = jax.nn.softmax(s, axis=-1).astype(v.dtype)
        p_loc = p[..., :n_loc].reshape(B_, H, n_chunk, NA_QCOLS, kh, NA_KCOLS)
        o = (jnp.einsum("bhcqij,bicjhd->bcqhd", p_loc, v_b)
             + jnp.einsum("bhcqj,bjhd->bcqhd", p[..., n_loc:], vc))
        return o.reshape(B_, GRID_W, H, dh)

    out = lax.map(row_block, jnp.arange(rows))
    return out.transpose(1, 0, 2, 3, 4).reshape(B_, L, H, dh)


def mixer_na(u_lat, u_ctx, w_qkv, w_out, rpb, with_ctx_out):
    B_, L, _ = u_lat.shape
    heads = lambda t: t.reshape(t.shape[0], t.shape[1], NA_HEADS, NA_HEAD_DIM)
    q, k, v = (heads(t) for t in jnp.split(u_lat @ w_qkv, 3, axis=-1))
    kc, vc = (heads(t) for t in jnp.split(u_ctx @ w_qkv[:, D_MODEL:], 2, axis=-1))
    y_lat = neighbourhood_attention(q, k, v, kc, vc, rpb).reshape(B_, L, D_MODEL) @ w_out
    if not with_ctx_out:
        return y_lat, None
    qc = heads(u_ctx @ w_qkv[:, :D_MODEL])
    y_ctx = dense_attention(qc, kc, vc).reshape(u_ctx.shape) @ w_out
    return y_lat, y_ctx


def setup_inputs(seed: int = 0) -> dict:
    key = jax.random.key(seed)
    ks = jax.random.split(key, 26)
    D = D_MODEL

    def nrm(k, shape, s):
        return jax.random.normal(k, shape, jnp.float32) * s

    return {
        "x": nrm(ks[0], (BATCH, SEQ, D), 1.0),
        "c": nrm(ks[1], (BATCH, D), 1.0),
        "ctx": nrm(ks[2], (BATCH, CTX_LEN, D), 1.0),
        "c_ctx": nrm(ks[3], (D,), 1.0),
        "ada_w": nrm(ks[4], (DEPTH, D, 6 * D), 0.5 * D ** -0.5),
        "ada_b": nrm(ks[5], (DEPTH, 6 * D), 0.02),
        "norm1_g": 1.0 + nrm(ks[6], (DEPTH, D), 0.02),
        "norm2_g": 1.0 + nrm(ks[7], (DEPTH, D), 0.02),
        "ffn_w_gate": nrm(ks[8], (DEPTH, D, D_FF), D ** -0.5),
        "ffn_w_up": nrm(ks[9], (DEPTH, D, D_FF), D ** -0.5),
        "ffn_w_down": nrm(ks[10], (DEPTH, D_FF, D), D_FF ** -0.5),
        "ab_w_in": nrm(ks[11], (N_EVEN, D, AB_IN), D ** -0.5),
        "ab_w_out": nrm(ks[12], (N_EVEN, FOURIER_CH + HY_CH, D), D ** -0.5),
        "hy_conv_w": nrm(ks[13], (N_EVEN, HY_SHORT, (HY_ORDER + 1) * HY_CH), HY_SHORT ** -0.5),
        "hy_conv_b": nrm(ks[14], (N_EVEN, (HY_ORDER + 1) * HY_CH), 0.02),
        "hy_f_w1": nrm(ks[15], (N_EVEN, HY_EMB, HY_FILTER_HID), HY_EMB ** -0.5),
        "hy_f_b1": nrm(ks[16], (N_EVEN, HY_FILTER_HID), 0.02),
        "hy_f_freq": 1.0 + nrm(ks[17], (N_EVEN, HY_FILTER_HID), 0.02),
        "hy_f_w2": nrm(ks[18], (N_EVEN, HY_FILTER_HID, HY_FILTER_HID), HY_FILTER_HID ** -0.5),
        "hy_f_b2": nrm(ks[19], (N_EVEN, HY_FILTER_HID), 0.02),
        "hy_f_w3": nrm(ks[20], (N_EVEN, HY_FILTER_HID, 2 * HY_ORDER * HY_CH), HY_FILTER_HID ** -0.5),
        "hy_bias": nrm(ks[21], (N_EVEN, HY_ORDER, HY_CH), 0.5),
        "na_w_qkv": nrm(ks[22], (N_ODD, D, 3 * D), D ** -0.5),
        "na_w_out": nrm(ks[23], (N_ODD, D, D), D ** -0.5),
        "na_rpb": nrm(ks[24], (N_ODD, NA_HEADS, 2 * NA_KH_MAX - 1, 2 * NA_KW - 1), 0.1),
        "final_g": 1.0 + nrm(ks[25], (D,), 0.02),
    }


def reference(x, c, ctx, c_ctx, ada_w, ada_b, norm1_g, norm2_g, ffn_w_gate, ffn_w_up, ffn_w_down,
              ab_w_in, ab_w_out, hy_conv_w, hy_conv_b, hy_f_w1, hy_f_b1, hy_f_freq, hy_f_w2, hy_f_b2,
              hy_f_w3, hy_bias, na_w_qkv, na_w_out, na_rpb, final_g):
    h_lat, h_ctx = x, ctx
    for i in range(DEPTH):
        last = i == DEPTH - 1
        even = i % 2 == 0
        j = i // 2
        need_ctx = (not last) or (not even)
        sh1, sc1, g1, sh2, sc2, g2 = ada_mod(c, ada_w[i], ada_b[i])
        u_lat = modulate(rms_norm(h_lat, norm1_g[i]), sh1, sc1)
        if need_ctx:
            csh1, csc1, cg1, csh2, csc2, cg2 = ada_mod(c_ctx, ada_w[i], ada_b[i])
            u_ctx = modulate(rms_norm(h_ctx, norm1_g[i]), csh1, csc1)
        if even:
            ab = (ab_w_in[j], ab_w_out[j], hy_conv_w[j], hy_conv_b[j], hy_f_w1[j], hy_f_b1[j],
                  hy_f_freq[j], hy_f_w2[j], hy_f_b2[j], hy_f_w3[j], hy_bias[j])
            y_lat = mixer_ab(u_lat, *ab)
            y_ctx = None if last else mixer_ab(u_ctx, *ab)
        else:
            y_lat, y_ctx = mixer_na(u_lat, u_ctx, na_w_qkv[j], na_w_out[j], na_rpb[j], not last)
        h_lat = h_lat + g1 * y_lat
        h_lat = h_lat + g2 * swiglu(modulate(rms_norm(h_lat, norm2_g[i]), sh2, sc2),
                                    ffn_w_gate[i], ffn_w_up[i], ffn_w_down[i])
        if not last:
            h_ctx = h_ctx + cg1 * y_ctx
            h_ctx = h_ctx + cg2 * swiglu(modulate(rms_norm(h_ctx, norm2_g[i]), csh2, csc2),
                                         ffn_w_gate[i], ffn_w_up[i], ffn_w_down[i])
    return rms_norm(h_lat, final_g)
```

```python
import numpy as np
from contextlib import ExitStack
import ml_dtypes
import concourse.bass as bass
import concourse.mybir as mybir
from concourse.bass_utils import run_bass_kernel_spmd

F32 = mybir.dt.float32
BF16 = mybir.dt.bfloat16
AF = mybir.ActivationFunctionType
ALU = mybir.AluOpType
AX = mybir.AxisListType
NPBF = ml_dtypes.bfloat16

D = 1024
DFF = 2816
NCORE = 8
EPS = 1e-6
SAME_ENGINE_SYNC = True


class Dep:
    __slots__ = ("w", "r")

    def __init__(self):
        self.w = []
        self.r = []


class Tl:
    def __init__(self, h):
        self.h = h
        self.dep = Dep()

    def __getitem__(self, idx):
        return self.h[idx]


def _dep(x):
    return x.dep if isinstance(x, Tl) else x


class KB:
    def __init__(self):
        self.nc = bass.Bass("TRN2", target_bir_lowering=False)
        nc = self.nc
        self.es = ExitStack()
        self.scopes = [self.es]
        self.eng = dict(pe=nc.tensor, act=nc.scalar, dve=nc.vector, pool=nc.gpsimd, sp=nc.sync)
        self.sems = {}
        self.epoch = {e: 0 for e in self.eng}
        for e in self.eng:
            self.sems[(e, 0)] = self.es.enter_context(nc.semaphore("sem_" + e))
        self.cnt = {e: 0 for e in self.eng}
        self.seen = {e: {} for e in self.eng}
        self.dq = {}
        for q, n in (("sp", 20), ("pool", 20)):
            keys = []
            for i in range(n):
                key = ("d", q, i)
                self.sems[key] = self.es.enter_context(nc.semaphore("dsem_%s_%d" % (q, i)))
                keys.append(key)
            self.dq[q] = dict(keys=keys, i=0, cnt=[0] * n)
        self.uid = 0

    def name(self, base):
        self.uid += 1
        return "%s_%d" % (base, self.uid)

    def sb(self, name, shape, dt):
        return Tl(self.scopes[-1].enter_context(self.nc.sbuf_tensor(self.name(name), list(shape), dt)))

    def ps(self, name, shape, dt=F32):
        return Tl(self.scopes[-1].enter_context(self.nc.psum_tensor(self.name(name), list(shape), dt)))

    def dram(self, name, shape, dt, kind):
        return Tl(self.nc.dram_tensor(name, list(shape), dt, kind=kind).ap())

    def push(self):
        es = ExitStack()
        self.scopes.append(es)

    def pop(self):
        self.barrier()
        es = self.scopes.pop()
        es.close()

    def _wait(self, e, key, val):
        if self.seen[e].get(key, 0) >= val:
            return
        self.eng[e].wait_ge(self.sems[key], val)
        self.seen[e][key] = val

    def _collect(self, e, reads, writes):
        evs = []
        for d in reads:
            evs += _dep(d).w
        for d in writes:
            d = _dep(d)
            evs += d.w
            evs += d.r
        for (key, val, src) in evs:
            if src == e and (e == "pe" or not SAME_ENGINE_SYNC):
                continue
            self._wait(e, key, val)

    def _record(self, ev, reads, writes):
        for d in writes:
            d = _dep(d)
            d.w = [ev]
            d.r = []
        for d in reads:
            d = _dep(d)
            d.r = [x for x in d.r if x[0] != ev[0]] + [ev]

    EPOCH = 3000

    def op(self, e, fn, reads=(), writes=()):
        if self.cnt[e] >= self.EPOCH:
            self.epoch[e] += 1
            self.sems[(e, self.epoch[e])] = self.es.enter_context(
                self.nc.semaphore("sem_%s_%d" % (e, self.epoch[e])))
            self.cnt[e] = 0
        self._collect(e, reads, writes)
        ins = fn(self.eng[e])
        self.cnt[e] += 1
        key = (e, self.epoch[e])
        ins.then_inc(self.sems[key], 1)
        self._record((key, self.cnt[e], e), reads, writes)
        return ins

    def dma(self, q, out, in_, reads=(), writes=(), **kw):
        self._collect(q, reads, writes)
        d = self.dq[q]
        i = d["i"]
        d["i"] = (i + 1) % len(d["keys"])
        key = d["keys"][i]
        prev = d["cnt"][i]
        if prev > 0:
            self._wait(q, key, prev)
        ins = self.eng[q].dma_start(out=out, in_=in_, **kw)
        ins.then_inc(self.sems[key], 16)
        d["cnt"][i] = prev + 16
        self._record((key, prev + 16, "dma"), reads, writes)
        return ins

    def barrier(self):
        for e in self.eng:
            for e2 in self.eng:
                if e2 != e:
                    for ep in range(self.epoch[e2] + 1):
                        c = self.cnt[e2] if ep == self.epoch[e2] else self.EPOCH
                        if c > 0:
                            self._wait(e, (e2, ep), c)
            for q, d in self.dq.items():
                for key, c in zip(d["keys"], d["cnt"]):
                    if c > 0:
                        self._wait(e, key, c)

    def finish(self):
        self.barrier()
        self.es.close()
        return self.nc

    def mm(self, out, lhsT, rhs, start, stop, reads, writes):
        return self.op("pe", lambda e: e.matmul(out, lhsT=lhsT, rhs=rhs, start=start, stop=stop),
                       reads=reads, writes=writes)

    def act(self, out, in_, func, reads, writes, eng="act", **kw):
        return self.op(eng, lambda e: e.activation(out=out, in_=in_, func=func, **kw), reads=reads, writes=writes)

    def tt(self, eng, out, in0, in1, op, reads, writes):
        return self.op(eng, lambda e: e.tensor_tensor(out=out, in0=in0, in1=in1, op=op), reads=reads, writes=writes)

    def ts(self, eng, out, in0, s1, s2, op0, op1, reads, writes):
        if s2 is None:
            return self.op(eng, lambda e: e.tensor_scalar(out=out, in0=in0, scalar1=s1, scalar2=None, op0=op0),
                           reads=reads, writes=writes)
        return self.op(eng, lambda e: e.tensor_scalar(out=out, in0=in0, scalar1=s1, scalar2=s2, op0=op0, op1=op1),
                       reads=reads, writes=writes)

    def stt(self, out, in0, scalar, in1, op0, op1, reads, writes):
        return self.op("dve", lambda e: e.scalar_tensor_tensor(out=out, in0=in0, scalar=scalar, in1=in1,
                                                                op0=op0, op1=op1), reads=reads, writes=writes)

    def copy(self, eng, out, in_, reads, writes):
        if eng == "act":
            return self.op(eng, lambda e: e.copy(out=out, in_=in_), reads=reads, writes=writes)
        return self.op(eng, lambda e: e.tensor_copy(out=out, in_=in_), reads=reads, writes=writes)

    def memset(self, eng, ap, val, writes):
        return self.op(eng, lambda e: e.memset(ap, val), reads=(), writes=writes)


def chunked(ap2d):
    return ap2d.rearrange("(c p) t -> p c t", p=128)


class Ctx:
    def __init__(self, k, npsum=8):
        self.k = k
        self.psb = [k.ps("psb", [128, 512], F32) for _ in range(npsum)]
        self.pi = 0
        self.ones = k.sb("ones", [128, 128], BF16)
        k.memset("dve", self.ones[:], 1.0, [self.ones])
        self.epsb = k.sb("epsb", [128, 1], F32)
        k.memset("dve", self.epsb[:], EPS, [self.epsb])

    def psum(self):
        p = self.psb[self.pi]
        self.pi = (self.pi + 1) % len(self.psb)
        return p


def rms_rstd(k, cx, x, xdeps, n, rstd, sq2):
    ps = cx.psum()
    for c in range(8):
        sq = sq2[c % 2]
        k.act(sq[:, :n], x(c), AF.Square, reads=[xdeps[c]], writes=[sq])
        k.mm(ps[:, :n], cx.ones[:], sq[:, :n], c == 0, c == 7, reads=[sq, cx.ones], writes=[ps])
    k.act(rstd[:, :n], ps[:, :n], AF.Sqrt, reads=[ps, cx.epsb], writes=[rstd], bias=cx.epsb[:], scale=1.0 / D)
    k.op("dve", lambda e: e.reciprocal(out=rstd[:, :n], in_=rstd[:, :n]), reads=[rstd], writes=[rstd])


def norm_mod(k, cx, x, xdeps, n, gmod, shift, out, odeps, rstd, sq2, tmp2, moddeps=()):
    rms_rstd(k, cx, x, xdeps, n, rstd, sq2)
    for c in range(8):
        if shift is None:
            k.stt(out(c), x(c), gmod(c), rstd[:, :n], ALU.mult, ALU.mult,
                  reads=[xdeps[c], rstd] + list(moddeps), writes=[odeps[c]])
        else:
            tmp = tmp2[c % 2]
            k.stt(tmp[:, :n], x(c), gmod(c), rstd[:, :n], ALU.mult, ALU.mult,
                  reads=[xdeps[c], rstd] + list(moddeps), writes=[tmp])
            k.act(out(c), tmp[:, :n], AF.Identity, reads=[tmp] + list(moddeps), writes=[odeps[c]],
                  bias=shift(c), scale=1.0)


def load_w_bf16(k, dst, dst_dep, w_ap, kchunks, ncols, col0=0, q="pool"):
    src = w_ap.rearrange("(c p) f -> p c f", p=128)
    step = 8
    for c0 in range(0, kchunks, step):
        c1 = min(kchunks, c0 + step)
        k.dma(q, dst[:, c0:c1, :ncols], src[:, c0:c1, col0:col0 + ncols], reads=(), writes=[dst_dep])


def build_ada():
    k = KB()
    condT = k.dram("condT", [128, 8, 3], F32, "ExternalInput")
    w = k.dram("w", [2, 1024, 768], F32, "ExternalInput")
    b = k.dram("b", [2, 3, 768], F32, "ExternalInput")
    out = k.dram("out", [2, 3, 768], F32, "ExternalOutput")
    c_sb = k.sb("c", [128, 8, 3], F32)
    s_sb = k.sb("s", [128, 8, 3], F32)
    w_sb = k.sb("w", [128, 2, 8, 768], F32)
    b_sb = k.sb("b", [3, 2, 768], F32)
    o_sb = k.sb("o", [3, 2, 768], F32)
    ps = [k.ps("ps", [128, 512], F32) for _ in range(4)]
    k.dma("sp", c_sb[:], condT[:], writes=[c_sb])
    for l in range(2):
        k.dma("sp", w_sb[:, l, :, :], w[l].rearrange("(c p) f -> p c f", p=128), writes=[w_sb])
        k.dma("sp", b_sb[:, l, :], b[l], writes=[b_sb])
    k.act(s_sb[:], c_sb[:], AF.Silu, reads=[c_sb], writes=[s_sb])
    for l in range(2):
        for hf in range(2):
            p = ps[l * 2 + hf]
            for c in range(8):
                k.mm(p[:3, :384], s_sb[:, c, :], w_sb[:, l, c, hf * 384:(hf + 1) * 384], c == 0, c == 7,
                     reads=[s_sb, w_sb], writes=[p])
            k.tt("dve", o_sb[:, l, hf * 384:(hf + 1) * 384], p[:3, :384], b_sb[:, l, hf * 384:(hf + 1) * 384],
                 ALU.add, reads=[p, b_sb], writes=[o_sb])
    k.dma("sp", out.h.rearrange("l c f -> c l f"), o_sb[:], reads=[o_sb], writes=[out])
    return k.finish()


def build_tail(TT, segs, final):
    k = KB()
    hT = k.dram("hT", [D, TT], F32, "ExternalInput")
    yT = k.dram("yT", [D, TT], BF16, "ExternalInput")
    nseg = 1 + max(s[2] for g in segs for s in g)
    modT = k.dram("modT", [128, nseg, 6, 8], F32, "ExternalInput")
    ng = k.dram("ng", [128, 2, 8], F32, "ExternalInput")
    w_mo = k.dram("w_mo", [D, D], F32, "ExternalInput")
    w_g = k.dram("w_g", [D, DFF], F32, "ExternalInput")
    w_u = k.dram("w_u", [D, DFF], F32, "ExternalInput")
    w_d = k.dram("w_d", [DFF, D], F32, "ExternalInput")
    oT = k.dram("oT", [D, TT], F32, "ExternalOutput")
    cx = Ctx(k)
    GM = max(sum(s[1] for s in g) for g in segs)
    NF = DFF // 128
    wmo_sb = k.sb("wmo", [128, 8, D], BF16)
    wd_sb = k.sb("wd", [128, NF, D], BF16)
    load_w_bf16(k, wmo_sb, wmo_sb, w_mo.h, 8, D)
    load_w_bf16(k, wd_sb, wd_sb, w_d.h, NF, D)
    mod_sb = k.sb("mod", [128, nseg, 6, 8], F32)
    ng_sb = k.sb("ng", [128, 2, 8], F32)
    k.dma("sp", mod_sb[:], modT[:], writes=[mod_sb])
    k.dma("sp", ng_sb[:], ng[:], writes=[ng_sb])
    gm2 = k.sb("gm2", [128, nseg, 8], F32)
    for s in range(nseg):
        k.ts("dve", gm2[:, s, :], mod_sb[:, s, 4, :], 1.0, None, ALU.add, None, reads=[mod_sb], writes=[gm2])
        k.tt("dve", gm2[:, s, :], gm2[:, s, :], ng_sb[:, 0, :], ALU.mult, reads=[gm2, ng_sb], writes=[gm2])
    h_sb = k.sb("h", [128, 8, GM], F32)
    y_sb = k.sb("y", [128, 8, GM], BF16)
    u_sb = k.sb("u", [128, 8, GM], BF16)
    a_sb = k.sb("a", [128, NF, GM], BF16)
    hd = [Dep() for _ in range(8)]
    yd = [Dep() for _ in range(8)]
    ud = [Dep() for _ in range(8)]
    ad = [Dep() for _ in range(NF)]
    rstd = k.sb("rstd", [128, 512], F32)
    sq2 = [k.sb("sq", [128, 512], BF16) for _ in range(2)]
    tmp2 = [k.sb("tmp", [128, 512], F32) for _ in range(2)]
    sg2 = [k.sb("sg", [128, 512], BF16) for _ in range(2)]
    wg2 = [k.sb("wg", [128, 8, 128], BF16) for _ in range(2)]
    wu2 = [k.sb("wu", [128, 8, 128], BF16) for _ in range(2)]
    hTc = chunked(hT.h)
    yTc = chunked(yT.h)
    oTc = chunked(oT.h)
    wgv = w_g.h.rearrange("(c p) f -> p c f", p=128)
    wuv = w_u.h.rearrange("(c p) f -> p c f", p=128)
    for g in segs:
        g0 = g[0][0]
        G = sum(s[1] for s in g)
        for c in range(8):
            k.dma("sp", h_sb[:, c, :G], hTc[:, c, g0:g0 + G], writes=[hd[c]])
            k.dma("sp", y_sb[:, c, :G], yTc[:, c, g0:g0 + G], writes=[yd[c]])
        for (s0, n, sg) in g:
            o = s0 - g0
            for c in range(8):
                p = cx.psum()
                for kc in range(8):
                    k.mm(p[:, :n], wmo_sb[:, kc, c * 128:(c + 1) * 128], y_sb[:, kc, o:o + n], kc == 0, kc == 7,
                         reads=[wmo_sb, yd[kc]], writes=[p])
                k.stt(h_sb[:, c, o:o + n], p[:, :n], mod_sb[:, sg, 2, c:c + 1], h_sb[:, c, o:o + n],
                      ALU.mult, ALU.add, reads=[p, mod_sb, hd[c]], writes=[hd[c]])
        for (s0, n, sg) in g:
            o = s0 - g0
            norm_mod(k, cx, lambda c: h_sb[:, c, o:o + n], hd, n,
                     lambda c: gm2[:, sg, c:c + 1], lambda c: mod_sb[:, sg, 3, c:c + 1],
                     lambda c: u_sb[:, c, o:o + n], ud, rstd, sq2, tmp2, moddeps=[gm2, mod_sb])
        for f in range(NF):
            wg = wg2[f % 2]
            wu = wu2[f % 2]
            k.dma("pool", wg[:], wgv[:, :, f * 128:(f + 1) * 128], writes=[wg])
            k.dma("pool", wu[:], wuv[:, :, f * 128:(f + 1) * 128], writes=[wu])
            for (s0, n, sg) in g:
                o = s0 - g0
                pg = cx.psum()
                pu = cx.psum()
                for kc in range(8):
                    k.mm(pg[:, :n], wg[:, kc, :], u_sb[:, kc, o:o + n], kc == 0, kc == 7, reads=[wg, ud[kc]], writes=[pg])
                for kc in range(8):
                    k.mm(pu[:, :n], wu[:, kc, :], u_sb[:, kc, o:o + n], kc == 0, kc == 7, reads=[wu, ud[kc]], writes=[pu])
                sgt = sg2[f % 2]
                k.act(sgt[:, :n], pg[:, :n], AF.Silu, reads=[pg], writes=[sgt])
                k.tt("dve", a_sb[:, f, o:o + n], sgt[:, :n], pu[:, :n], ALU.mult, reads=[sgt, pu], writes=[ad[f]])
        for (s0, n, sg) in g:
            o = s0 - g0
            for c in range(8):
                p = cx.psum()
                for f in range(NF):
                    k.mm(p[:, :n], wd_sb[:, f, c * 128:(c + 1) * 128], a_sb[:, f, o:o + n], f == 0, f == NF - 1,
                         reads=[wd_sb, ad[f]], writes=[p])
                k.stt(h_sb[:, c, o:o + n], p[:, :n], mod_sb[:, sg, 5, c:c + 1], h_sb[:, c, o:o + n],
                      ALU.mult, ALU.add, reads=[p, mod_sb, hd[c]], writes=[hd[c]])
        if final:
            for (s0, n, sg) in g:
                o = s0 - g0
                norm_mod(k, cx, lambda c: h_sb[:, c, o:o + n], hd, n,
                         lambda c: ng_sb[:, 1, c:c + 1], None,
                         lambda c: h_sb[:, c, o:o + n], hd, rstd, sq2, tmp2, moddeps=[ng_sb])
        for c in range(8):
            k.dma("sp", oTc[:, c, g0:g0 + G], h_sb[:, c, :G], reads=[hd[c]], writes=[oT])
    return k.finish()


_NC_CACHE = {}


def _get(name, fn, *args):
    key = (name,) + tuple(str(a) for a in args)
    if key not in _NC_CACHE:
        _NC_CACHE[key] = fn(*args)
    return _NC_CACHE[key]


def _launch(nc, maps):
    res = run_bass_kernel_spmd(nc, maps, core_ids=list(range(len(maps))))
    return res.results


def fm_chunks(v):
    v = np.asarray(v)
    lead = v.shape[:-1]
    r = v.reshape(lead + (8, 128))
    return np.ascontiguousarray(np.moveaxis(r, -1, 0))


def run_ada(I):
    cond = np.stack([I["c"][0], I["c"][1], I["c_ctx"]]).astype(np.float32)
    condT = np.ascontiguousarray(cond.reshape(3, 8, 128).transpose(2, 1, 0))
    maps = []
    for i in range(NCORE):
        sl = slice(768 * i, 768 * (i + 1))
        maps.append(dict(condT=condT, w=np.ascontiguousarray(I["ada_w"][:, :, sl]),
                         b=np.ascontiguousarray(np.broadcast_to(I["ada_b"][:, None, sl], (2, 3, 768)))))
    res = _launch(_get("ada", build_ada), maps)
    return np.concatenate([r["out"] for r in res], axis=2)


SEGS_L0 = [[(0, 512, 0), (512, 512, 0)], [(1024, 512, 0), (1536, 512, 0), (2048, 64, 1)]]
SEGS_L1 = [[(0, 512, 0), (512, 512, 0)], [(1024, 512, 0), (1536, 512, 0)]]


def run_tail(layer, hT_list, yT_list, mod, I, final):
    TT = hT_list[0].shape[1]
    segs = SEGS_L0 if TT == 2112 else SEGS_L1
    ng = np.stack([fm_chunks(I["norm2_g"][layer]), fm_chunks(I["final_g"])], axis=1).astype(np.float32)
    if layer == 0:
        w_mo = I["ab_w_out"][0]
    else:
        w_mo = I["na_w_out"][0]
    maps = []
    for i in range(NCORE):
        b = i // 4
        conds = [b, 2] if TT == 2112 else [b]
        modT = np.stack([fm_chunks(mod[layer, cnd].reshape(6, 1024)) for cnd in conds], axis=1)
        maps.append(dict(hT=np.ascontiguousarray(hT_list[i], dtype=np.float32),
                         yT=np.ascontiguousarray(yT_list[i]).astype(NPBF) if yT_list[i].dtype != NPBF else np.ascontiguousarray(yT_list[i]),
                         modT=np.ascontiguousarray(modT, dtype=np.float32), ng=ng,
                         w_mo=np.ascontiguousarray(w_mo), w_g=np.ascontiguousarray(I["ffn_w_gate"][layer]),
                         w_u=np.ascontiguousarray(I["ffn_w_up"][layer]), w_d=np.ascontiguousarray(I["ffn_w_down"][layer])))
    res = _launch(_get("tail", build_tail, TT, segs, final), maps)
    return [r["oT"] for r in res]


def build_mix1(L):
    NHI = L // 128
    TW = min(512, L)
    k = KB()
    xT = k.dram("xT", [D, L], F32, "ExternalInput")
    modT = k.dram("modT", [128, 6, 8], F32, "ExternalInput")
    n1g = k.dram("n1g", [128, 8], F32, "ExternalInput")
    w_in = k.dram("w_in", [D, 512], F32, "ExternalInput")
    cw = k.dram("cw", [128, 3, 3], F32, "ExternalInput")
    cb = k.dram("cb", [128, 3], F32, "ExternalInput")
    cs128 = k.dram("cs128", [128, 256], F32, "ExternalInput")
    tbr = k.dram("tbr", [NHI, 2 * NHI], F32, "ExternalInput")
    tbi = k.dram("tbi", [NHI, 2 * NHI], F32, "ExternalInput")
    ec = k.dram("ec", [128, L], F32, "ExternalInput")
    es = k.dram("es", [128, L], F32, "ExternalInput")
    yfT = k.dram("yfT", [128, L], BF16, "ExternalOutput")
    zT = k.dram("zT", [3, 128, L], BF16, "ExternalOutput")
    cx = Ctx(k)
    mod_sb = k.sb("mod", [128, 6, 8], F32)
    n1g_sb = k.sb("n1g", [128, 8], F32)
    cw_sb = k.sb("cw", [128, 3, 3], F32)
    cb_sb = k.sb("cb", [128, 3], F32)
    k.dma("sp", mod_sb[:], modT[:], writes=[mod_sb])
    k.dma("sp", n1g_sb[:], n1g[:], writes=[n1g_sb])
    k.dma("sp", cw_sb[:], cw[:], writes=[cw_sb])
    k.dma("sp", cb_sb[:], cb[:], writes=[cb_sb])
    gm = k.sb("gm", [128, 8], F32)
    k.ts("dve", gm[:], mod_sb[:, 1, :], 1.0, None, ALU.add, None, reads=[mod_sb], writes=[gm])
    k.tt("dve", gm[:], gm[:], n1g_sb[:], ALU.mult, reads=[gm, n1g_sb], writes=[gm])
    gT = k.sb("gT", [128, 128, NHI], BF16)
    k.push()
    w_sb = k.sb("w", [128, 8, 512], BF16)
    load_w_bf16(k, w_sb, w_sb, w_in.h, 8, 512)
    pT = k.sb("pT", [128, 3, L + 2], BF16)
    for s in range(3):
        k.memset("pool", pT[:, s, 0:1], 0.0, [pT])
        k.memset("pool", pT[:, s, L + 1:L + 2], 0.0, [pT])
    x2 = [k.sb("x", [128, 8, TW], F32) for _ in range(2)]
    u2 = [k.sb("u", [128, 8, TW], BF16) for _ in range(2)]
    xd = [[Dep() for _ in range(8)] for _ in range(2)]
    ud = [[Dep() for _ in range(8)] for _ in range(2)]
    rstd = k.sb("rstd", [128, 512], F32)
    sq2 = [k.sb("sq", [128, 512], BF16) for _ in range(2)]
    tmp2 = [k.sb("tmp", [128, 512], F32) for _ in range(2)]
    xTc = chunked(xT.h)
    pdep = [Dep() for _ in range(L // TW)]
    for t in range(L // TW):
        t0 = t * TW
        xs, us = x2[t % 2], u2[t % 2]
        for c in range(8):
            k.dma("sp", xs[:, c, :], xTc[:, c, t0:t0 + TW], writes=[xd[t % 2][c]])
        norm_mod(k, cx, lambda c: xs[:, c, :], xd[t % 2], TW, lambda c: gm[:, c:c + 1],
                 lambda c: mod_sb[:, 0, c:c + 1], lambda c: us[:, c, :], ud[t % 2], rstd, sq2, tmp2,
                 moddeps=[gm, mod_sb])
        for s in range(4):
            p = cx.psum()
            for kc in range(8):
                k.mm(p[:, :TW], w_sb[:, kc, s * 128:(s + 1) * 128], us[:, kc, :], kc == 0, kc == 7,
                     reads=[w_sb, ud[t % 2][kc]], writes=[p])
            if s == 0:
                nh = TW // 128
                h0 = t0 // 128
                k.copy("act", gT[:, :, h0:h0 + nh].rearrange("p l h -> p h l"),
                       p[:, :TW].rearrange("p (h l) -> p h l", l=128), reads=[p], writes=[gT])
            else:
                k.copy("act", pT[:, s - 1, 1 + t0:1 + t0 + TW], p[:, :TW], reads=[p], writes=[pdep[t], pT])
    CBK = min(2048, L)
    acc2 = [k.sb("acc", [128, CBK], F32) for _ in range(2)]
    zb2 = [k.sb("zb", [128, CBK], BF16) for _ in range(2)]
    i = 0
    for s in range(3):
        for c0 in range(0, L, CBK):
            acc, zb = acc2[i % 2], zb2[i % 2]
            i += 1
            k.ts("dve", acc[:], pT[:, s, 1 + c0:1 + c0 + CBK], cw_sb[:, s, 1:2], cb_sb[:, s:s + 1], ALU.mult, ALU.add,
                 reads=[pT, cw_sb, cb_sb], writes=[acc])
            k.stt(acc[:], pT[:, s, c0:c0 + CBK], cw_sb[:, s, 0:1], acc[:], ALU.mult, ALU.add,
                  reads=[pT, cw_sb, acc], writes=[acc])
            k.stt(zb[:], pT[:, s, 2 + c0:2 + c0 + CBK], cw_sb[:, s, 2:3], acc[:], ALU.mult, ALU.add,
                  reads=[pT, cw_sb, acc], writes=[zb])
            k.dma("sp", zT[s, :, c0:c0 + CBK], zb[:], reads=[zb], writes=[zT])
    k.pop()
    k.push()
    cs_sb = k.sb("cs", [128, 256], BF16)
    tbr_sb = k.sb("tbr", [NHI, 2 * NHI], BF16)
    tbi_sb = k.sb("tbi", [NHI, 2 * NHI], BF16)
    ec_sb = k.sb("ec", [128, NHI, 128], BF16)
    es_sb = k.sb("es", [128, NHI, 128], BF16)
    k.dma("pool", cs_sb[:], cs128[:], writes=[cs_sb])
    k.dma("pool", tbr_sb[:], tbr[:], writes=[tbr_sb])
    k.dma("pool", tbi_sb[:], tbi[:], writes=[tbi_sb])
    ecv = ec.h.rearrange("p (a b) -> p a b", b=128)
    esv = es.h.rearrange("p (a b) -> p a b", b=128)
    st = max(1, NHI // 4)
    for a0 in range(0, NHI, st):
        k.dma("pool", ec_sb[:, a0:a0 + st, :], ecv[:, a0:a0 + st, :], writes=[ec_sb])
        k.dma("pool", es_sb[:, a0:a0 + st, :], esv[:, a0:a0 + st, :], writes=[es_sb])
    A1 = k.sb("A1", [NHI, 2, 128, 128], BF16)
    for l0 in range(0, 128, 2):
        p = cx.psum()
        for j in range(2):
            k.mm(p[:NHI, j * 256:(j + 1) * 256], gT[:, l0 + j, :], cs_sb[:], True, True, reads=[gT, cs_sb], writes=[p])
        k.copy("dve" if (l0 // 2) % 2 == 0 else "act",
               A1[:, :, :, l0:l0 + 2].rearrange("p r c l -> p l r c"),
               p[:NHI, :512].rearrange("p (l r c) -> p l r c", l=2, r=2), reads=[p], writes=[A1])
    B1 = k.sb("B1", [128, 2, NHI, 128], BF16)
    W2 = 2 * NHI
    per = max(1, 512 // W2)
    per = min(per, 128)
    for c0 in range(0, 128, per):
        p = cx.psum()
        for j in range(per):
            kc_ = c0 + j
            k.mm(p[:, j * W2:(j + 1) * W2], A1[:, 0, kc_, :], tbr_sb[:], True, False, reads=[A1, tbr_sb], writes=[p])
            k.mm(p[:, j * W2:(j + 1) * W2], A1[:, 1, kc_, :], tbi_sb[:], False, True, reads=[A1, tbi_sb], writes=[p])
        k.copy("dve" if (c0 // per) % 2 == 0 else "act",
               B1[:, :, :, c0:c0 + per].rearrange("p r k c -> p c r k"),
               p[:, :per * W2].rearrange("p (c r k) -> p c r k", c=per, r=2), reads=[p], writes=[B1])
    yf = k.sb("yf", [128, 128, NHI], BF16)
    scale = 1.0 / float(np.sqrt(L * 128.0))
    per = min(4, NHI)
    for q0 in range(0, NHI, per):
        p = cx.psum()
        for j in range(per):
            kl = q0 + j
            k.mm(p[:, j * 128:(j + 1) * 128], B1[:, 0, kl, :], ec_sb[:, kl, :], True, False, reads=[B1, ec_sb], writes=[p])
            k.mm(p[:, j * 128:(j + 1) * 128], B1[:, 1, kl, :], es_sb[:, kl, :], False, True, reads=[B1, es_sb], writes=[p])
        k.act(yf[:, :, q0:q0 + per].rearrange("p h l -> p l h"),
              p[:, :per * 128].rearrange("p (l h) -> p l h", l=per), AF.Copy, reads=[p], writes=[yf], scale=scale)
    k.dma("sp", yfT[:, :], yf[:].rearrange("p h l -> p (h l)"), reads=[yf], writes=[yfT])
    k.pop()
    return k.finish()


def dft_tables_mix1(L):
    NHI = L // 128
    c = np.arange(128)
    ang = 2 * np.pi * np.outer(c, c) / 128.0
    cs128 = np.concatenate([np.cos(ang), -np.sin(ang)], 1)
    h = np.arange(NHI)
    angb = 2 * np.pi * np.outer(h, h) / NHI
    tbr = np.concatenate([np.cos(angb), -np.sin(angb)], 1)
    tbi = np.concatenate([np.sin(angb), np.cos(angb)], 1)
    nlo = np.arange(128)[:, None, None]
    klo = np.arange(NHI)[None, :, None]
    khi = np.arange(128)[None, None, :]
    ange = 2 * np.pi * ((nlo * (klo + NHI * khi)) % L) / L
    ec = np.cos(ange).reshape(128, L)
    es = np.sin(ange).reshape(128, L)
    f = lambda a: np.ascontiguousarray(a, dtype=np.float32)
    return dict(cs128=f(cs128), tbr=f(tbr), tbi=f(tbi), ec=f(ec), es=f(es))


def run_mix1(L, xT_list, mod_list, I):
    tabs = _get("tab1", dft_tables_mix1, L)
    n1g = fm_chunks(I["norm1_g"][0]).astype(np.float32)
    maps = []
    for i in range(NCORE):
        g = i % 4
        cols = np.concatenate([np.arange(128) + 128 * g] + [512 + s * 512 + 128 * g + np.arange(128) for s in range(3)])
        w = np.ascontiguousarray(I["ab_w_in"][0][:, cols])
        hc = [s * 512 + 128 * g + np.arange(128) for s in range(3)]
        cw = np.stack([I["hy_conv_w"][0][:, c_].T for c_ in hc], axis=1)
        cb = np.stack([I["hy_conv_b"][0][c_] for c_ in hc], axis=1)
        m = dict(xT=np.ascontiguousarray(xT_list[i], dtype=np.float32),
                 modT=np.ascontiguousarray(fm_chunks(mod_list[i].reshape(6, 1024)), dtype=np.float32),
                 n1g=n1g, w_in=w, cw=np.ascontiguousarray(cw, dtype=np.float32),
                 cb=np.ascontiguousarray(cb, dtype=np.float32))
        m.update(tabs)
        maps.append(m)
    res = _launch(_get("mix1", build_mix1, L), maps)
    return [r["yfT"] for r in res], [r["zT"] for r in res]


PI = float(np.pi)


def build_mix2(L):
    NHI = L // 128
    NK = 2 * NHI
    CB = 32
    NB = 128 // CB
    TW = min(512, L)
    k = KB()
    zT = k.dram("zT", [3, 128, L], BF16, "ExternalInput")
    zf = k.dram("zf", [33, L], F32, "ExternalInput")
    zb = k.dram("zb", [33, L], F32, "ExternalInput")
    w1 = k.dram("w1", [33, 64], F32, "ExternalInput")
    w2 = k.dram("w2", [64, 64], F32, "ExternalInput")
    w3 = k.dram("w3", [64, 2, 2, 128], F32, "ExternalInput")
    fb = k.dram("fb", [64, 3], F32, "ExternalInput")
    decf = k.dram("decf", [NHI, 128, 128], F32, "ExternalInput")
    decb = k.dram("decb", [NHI, 128, 128], F32, "ExternalInput")
    skipb = k.dram("skipb", [128, 2, 128], F32, "ExternalInput")
    t1f = k.dram("t1f", [NHI, 3 * NK], BF16, "ExternalInput")
    t1b = k.dram("t1b", [NHI, 3 * NK], BF16, "ExternalInput")
    e2c = k.dram("e2c", [128, NK, 128], BF16, "ExternalInput")
    e2s = k.dram("e2s", [128, NK, 128], BF16, "ExternalInput")
    tir = k.dram("tir", [128, 256], BF16, "ExternalInput")
    tii = k.dram("tii", [128, 256], BF16, "ExternalInput")
    eic = k.dram("eic", [NK, 128, NHI], BF16, "ExternalInput")
    eis = k.dram("eis", [NK, 128, NHI], BF16, "ExternalInput")
    yhT = k.dram("yhT", [128, L], BF16, "ExternalOutput")
    cx = Ctx(k)
    w1_sb = k.sb("w1", [33, 64], F32)
    w2_sb = k.sb("w2", [64, 64], F32)
    w3_sb = k.sb("w3", [64, 2, 2, 128], BF16)
    fb_sb = k.sb("fb", [64, 3], F32)
    skip_sb = k.sb("skip", [128, 2, 128], F32)
    t1f_sb = k.sb("t1f", [NHI, 3 * NK], BF16)
    t1b_sb = k.sb("t1b", [NHI, 3 * NK], BF16)
    tir_sb = k.sb("tir", [128, 256], BF16)
    tii_sb = k.sb("tii", [128, 256], BF16)
    onesf = k.sb("onesf", [128, 128], F32)
    k.memset("dve", onesf[:], 1.0, [onesf])
    negpi = k.sb("negpi", [128, 1], F32)
    k.memset("dve", negpi[:], 0.0, [negpi])
    for dst, src in ((w1_sb, w1), (w2_sb, w2), (fb_sb, fb), (skip_sb, skipb), (t1f_sb, t1f), (t1b_sb, t1b),
                     (tir_sb, tir), (tii_sb, tii)):
        k.dma("sp", dst[:], src[:], writes=[dst])
    k.dma("pool", w3_sb[:], w3[:], writes=[w3_sb])
    fbb = k.sb("fbb", [64, 2], F32)
    k.ts("dve", fbb[:], fb_sb[:, 1:3], fb_sb[:, 0:1], None, ALU.mult, None, reads=[fb_sb], writes=[fbb])
    h2 = [k.sb("h2", [64, 128, NHI], BF16) for _ in range(2)]
    k.push()
    zt2 = [k.sb("zt", [33, TW], F32) for _ in range(2)]
    y1 = k.sb("y1", [64, TW], F32)
    mk = k.sb("mk", [64, TW], F32)
    h1 = k.sb("h1", [64, TW], F32)

    def sin_layer(ps, col, out_ap, out_dep, in_view=None):
        k.ts("dve", y1[:], ps[:64, :TW], fb_sb[:, 0:1], fbb[:, col:col + 1], ALU.mult, ALU.add,
             reads=[ps, fb_sb, fbb], writes=[y1])
        k.ts("dve", mk[:], y1[:], -PI, None, ALU.is_lt, None, reads=[y1], writes=[mk])
        k.stt(y1[:], mk[:], 2 * PI, y1[:], ALU.mult, ALU.add, reads=[mk, y1], writes=[y1])
        k.ts("dve", mk[:], y1[:], PI, None, ALU.is_gt, None, reads=[y1], writes=[mk])
        k.stt(y1[:], mk[:], -2 * PI, y1[:], ALU.mult, ALU.add, reads=[mk, y1], writes=[y1])
        k.act(out_ap, y1[:] if in_view is None else in_view(y1), AF.Sin, reads=[y1], writes=[out_dep])

    it = 0
    for d, zsrc in enumerate((zf, zb)):
        for t in range(L // TW):
            t0 = t * TW
            zt = zt2[it % 2]
            it += 1
            k.dma("sp", zt[:], zsrc[:, t0:t0 + TW], writes=[zt])
            p = cx.psum()
            k.mm(p[:64, :TW], w1_sb[:], zt[:], True, True, reads=[w1_sb, zt], writes=[p])
            sin_layer(p, 0, h1[:], h1)
            p = cx.psum()
            k.mm(p[:64, :TW], w2_sb[:], h1[:], True, True, reads=[w2_sb, h1], writes=[p])
            nh = TW // 128
            h0 = t0 // 128
            sin_layer(p, 1, h2[d][:, :, h0:h0 + nh].rearrange("p l h -> p h l"), h2[d],
                      in_view=lambda y: y[:].rearrange("p (h l) -> p h l", l=128))
    k.pop()
    S1f = k.sb("S1f", [128, NK, 3, CB], BF16)
    S1v = k.sb("S1v", [128, NK, 3, CB], BF16)
    Y = k.sb("Y", [128, 2, CB, NK], BF16)
    FZ = k.sb("FZ", [128, 2 * 128 * CB], BF16)
    ftv = FZ[:NHI, :].rearrange("p (d c l) -> p d c l", d=2, c=CB)
    Zv = FZ[:NK, :].rearrange("p (r a c) -> p r a c", r=2, a=128)
    tin = k.sb("tin", [NHI, CB, 128], BF16)
    tx1 = k.sb("tx1", [NHI, CB, 128], BF16)
    tv1 = k.sb("tv1", [NHI, CB, 128], BF16)
    x2b = k.sb("x2b", [CB, NHI, 128], BF16)
    yhb = k.sb("yhb", [CB, NHI, 128], BF16)
    GN = 8
    dec2 = [k.sb("dec", [NHI, GN, CB], F32) for _ in range(2)]
    hd2 = [k.sb("hd", [NHI, GN, CB], F32) for _ in range(2)]
    acc = k.sb("acc", [NHI, GN, CB], F32)
    hab2 = [k.sb("hab", [NHI, GN, CB], F32) for _ in range(2)]
    rn = k.sb("rn", [128, CB], F32)
    rn8 = k.sb("rn8", [128, 8, CB], F32)
    KG = min(4, NK)
    e2c2 = [k.sb("e2c", [128, KG, 128], BF16) for _ in range(2)]
    e2s2 = [k.sb("e2s", [128, KG, 128], BF16) for _ in range(2)]
    Hg2 = [k.sb("Hg", [128, KG, 2, CB], F32) for _ in range(2)]
    m4 = [k.sb("m4", [128, KG, CB], F32) for _ in range(4)]
    AG = 512 // CB
    AG2 = min(512 // NHI, 128) if NHI <= 512 else 1
    AG2 = min(AG2, 8)
    eic2 = [k.sb("eic", [NK, AG, NHI], BF16) for _ in range(2)]
    eis2 = [k.sb("eis", [NK, AG, NHI], BF16) for _ in range(2)]
    zTv = zT.h
    cnt = [0]

    def alt():
        cnt[0] += 1
        return "dve" if cnt[0] % 2 == 0 else "act"

    def stage1(dst, srcs):
        for c in range(CB):
            p = cx.psum()
            for i, (vf, tab, deps) in enumerate(srcs):
                k.mm(p[:, :3 * NK], vf(c), tab[:], i == 0, i == len(srcs) - 1, reads=deps + [tab], writes=[p])
            k.copy(alt(), dst[:, :, :, c], p[:, :3 * NK].rearrange("p (r q) -> p q r", r=3), reads=[p], writes=[dst])

    for cb in range(NB):
        c0 = cb * CB
        k.dma("sp", tin[:], zTv[2, c0:c0 + CB, :].rearrange("c (h l) -> h c l", l=128), writes=[tin])
        k.dma("sp", tx1[:], zTv[0, c0:c0 + CB, :].rearrange("c (h l) -> h c l", l=128), writes=[tx1])
        k.dma("sp", x2b[:], zTv[1, c0:c0 + CB, :].rearrange("c (h l) -> c h l", l=128), writes=[x2b])
        for o in range(2):
            sig = tin if o == 0 else tv1
            k.memset("dve", acc[:], 0.0, [acc])
            gi = 0
            for d, dsrc in enumerate((decf, decb)):
                for l0 in range(0, 128, GN):
                    dec, hd = dec2[gi % 2], hd2[gi % 2]
                    gi += 1
                    k.dma("sp", dec[:], dsrc[:, l0:l0 + GN, c0:c0 + CB], writes=[dec])
                    p = cx.psum()
                    for j in range(GN):
                        k.mm(p[:NHI, j * CB:(j + 1) * CB], h2[d][:, l0 + j, :], w3_sb[:, d, o, c0:c0 + CB], True, True,
                             reads=[h2[d], w3_sb], writes=[p])
                    k.tt("dve", hd[:], p[:NHI, :GN * CB].rearrange("p (l c) -> p l c", l=GN), dec[:], ALU.mult,
                         reads=[p, dec], writes=[hd])
                    k.copy("pool", ftv[:, d, :, l0:l0 + GN].rearrange("p c l -> p l c"), hd[:], reads=[hd], writes=[FZ])
                    hab = hab2[gi % 2]
                    k.act(hab[:], hd[:], AF.Abs, reads=[hd], writes=[hab])
                    k.tt("pool", acc[:], acc[:], hab[:], ALU.add, reads=[hab, acc], writes=[acc])
            k.memset("pool", ftv[0:1, 1, :, 0:1], 0.0, [FZ])
            p = cx.psum()
            k.mm(p[:, :GN * CB], onesf[:NHI, :], acc[:].rearrange("p l c -> p (l c)"), True, True,
                 reads=[onesf, acc], writes=[p])
            k.op("dve", lambda e: e.tensor_reduce(out=rn[:], in_=p[:, :GN * CB].rearrange("p (l c) -> p c l", l=GN),
                                                  axis=AX.X, op=ALU.add), reads=[p], writes=[rn])
            k.op("dve", lambda e: e.reciprocal(out=rn[:], in_=rn[:]), reads=[rn], writes=[rn])
            for j in range(8):
                k.copy("pool", rn8[:, j, :], rn[:], reads=[rn], writes=[rn8])
            stage1(S1f, [(lambda c: ftv[:, 0, c, :], t1f_sb, [FZ]), (lambda c: ftv[:, 1, c, :], t1b_sb, [FZ])])
            stage1(S1v, [(lambda c: sig[:, c, :], t1f_sb, [sig])])
            for gq, q0 in enumerate(range(0, NK, KG)):
                ecg, esg = e2c2[gq % 2], e2s2[gq % 2]
                k.dma("sp", ecg[:], e2c[:, q0:q0 + KG, :], writes=[ecg])
                k.dma("sp", esg[:], e2s[:, q0:q0 + KG, :], writes=[esg])
                ph = cx.psum()
                px = cx.psum()
                for (pp, S1) in ((ph, S1f), (px, S1v)):
                    for j in range(KG):
                        kl = q0 + j
                        k.mm(pp[:, j * 2 * CB:(j + 1) * 2 * CB], ecg[:, j, :],
                             S1[:, kl, 0:2, :].rearrange("p r c -> p (r c)"), True, False, reads=[ecg, S1], writes=[pp])
                        k.mm(pp[:, j * 2 * CB:(j + 1) * 2 * CB], esg[:, j, :],
                             S1[:, kl, 1:3, :].rearrange("p r c -> p (r c)"), False, True, reads=[esg, S1], writes=[pp])
                Hg = Hg2[gq % 2]
                k.tt("dve", Hg[:].rearrange("p q r c -> p (q r) c"),
                     ph[:, :KG * 2 * CB].rearrange("p (q c) -> p q c", c=CB), rn8[:, :KG * 2, :], ALU.mult,
                     reads=[ph, rn8], writes=[Hg])
                for j in range(KG):
                    k.tt("pool", Hg[:, j, 0, :], Hg[:, j, 0, :], skip_sb[:, o, c0:c0 + CB], ALU.add,
                         reads=[Hg, skip_sb], writes=[Hg])
                pxv = px[:, :KG * 2 * CB].rearrange("p (q r c) -> p q r c", q=KG, r=2)
                k.tt("dve", m4[0][:], pxv[:, :, 0, :], Hg[:, :, 0, :], ALU.mult, reads=[px, Hg], writes=[m4[0]])
                k.tt("dve", m4[1][:], pxv[:, :, 1, :], Hg[:, :, 1, :], ALU.mult, reads=[px, Hg], writes=[m4[1]])
                k.tt("dve", m4[2][:], pxv[:, :, 0, :], Hg[:, :, 1, :], ALU.mult, reads=[px, Hg], writes=[m4[2]])
                k.tt("dve", m4[3][:], pxv[:, :, 1, :], Hg[:, :, 0, :], ALU.mult, reads=[px, Hg], writes=[m4[3]])
                k.tt("pool", Y[:, 0, :, q0:q0 + KG].rearrange("p c q -> p q c"), m4[0][:], m4[1][:], ALU.subtract,
                     reads=[m4[0], m4[1]], writes=[Y])
                k.tt("pool", Y[:, 1, :, q0:q0 + KG].rearrange("p c q -> p q c"), m4[2][:], m4[3][:], ALU.add,
                     reads=[m4[2], m4[3]], writes=[Y])
            for c in range(0, CB, 2):
                p = cx.psum()
                for j in range(2):
                    k.mm(p[:NK, j * 256:(j + 1) * 256], Y[:, 0, c + j, :], tir_sb[:], True, False, reads=[Y, tir_sb], writes=[p])
                    k.mm(p[:NK, j * 256:(j + 1) * 256], Y[:, 1, c + j, :], tii_sb[:], False, True, reads=[Y, tii_sb], writes=[p])
                k.copy(alt(), Zv[:, :, :, c:c + 2].rearrange("p r a c -> p c r a"),
                       p[:NK, :512].rearrange("p (c r a) -> p c r a", c=2, r=2), reads=[p], writes=[FZ])
            if o == 0:
                for ga, a0 in enumerate(range(0, 128, AG)):
                    ec_, es_ = eic2[ga % 2], eis2[ga % 2]
                    k.dma("sp", ec_[:], eic[:, a0:a0 + AG, :], writes=[ec_])
                    k.dma("sp", es_[:], eis[:, a0:a0 + AG, :], writes=[es_])
                    p = cx.psum()
                    for j in range(AG):
                        k.mm(p[:NHI, j * CB:(j + 1) * CB], ec_[:, j, :], Zv[:, 0, a0 + j, :], True, False, reads=[ec_, FZ], writes=[p])
                        k.mm(p[:NHI, j * CB:(j + 1) * CB], es_[:, j, :], Zv[:, 1, a0 + j, :], False, True, reads=[es_, FZ], writes=[p])
                    k.tt("dve", tv1[:, :, a0:a0 + AG].rearrange("p c a -> p a c"),
                         p[:NHI, :AG * CB].rearrange("p (a c) -> p a c", a=AG),
                         tx1[:, :, a0:a0 + AG].rearrange("p c a -> p a c"), ALU.mult, reads=[p, tx1], writes=[tv1])
            else:
                per = max(1, min(AG, 512 // NHI))
                for ga, a0 in enumerate(range(0, 128, AG)):
                    ec_, es_ = eic2[ga % 2], eis2[ga % 2]
                    k.dma("sp", ec_[:], eic[:, a0:a0 + AG, :], writes=[ec_])
                    k.dma("sp", es_[:], eis[:, a0:a0 + AG, :], writes=[es_])
                    for a1 in range(0, AG, per):
                        p = cx.psum()
                        for j in range(per):
                            a = a0 + a1 + j
                            k.mm(p[:CB, j * NHI:(j + 1) * NHI], Zv[:, 0, a, :], ec_[:, a1 + j, :], True, False, reads=[ec_, FZ], writes=[p])
                            k.mm(p[:CB, j * NHI:(j + 1) * NHI], Zv[:, 1, a, :], es_[:, a1 + j, :], False, True, reads=[es_, FZ], writes=[p])
                        aa = a0 + a1
                        k.tt("dve", yhb[:, :, aa:aa + per].rearrange("p b a -> p a b"),
                             p[:CB, :per * NHI].rearrange("p (a b) -> p a b", a=per),
                             x2b[:, :, aa:aa + per].rearrange("p b a -> p a b"), ALU.mult, reads=[p, x2b], writes=[yhb])
                k.dma("sp", yhT[c0:c0 + CB, :], yhb[:].rearrange("p b a -> p (b a)"), reads=[yhb], writes=[yhT])
    return k.finish()


def tables_mix2(L, g):
    NHI = L // 128
    NK = 2 * NHI
    N = 2 * L
    f32 = lambda a: np.ascontiguousarray(a, dtype=np.float32)
    bf = lambda a: np.ascontiguousarray(np.asarray(a, dtype=np.float32).astype(NPBF))

    def ztab(t):
        t = t.astype(np.float64)
        t01 = t / (L - 1)
        bands = np.linspace(1e-4, 15.0, 16)
        ang = (2 * np.pi / L) * t[:, None] * bands[None, :]
        return np.concatenate([t01[:, None], np.cos(ang), -np.sin(ang)], axis=1).T

    tf = np.arange(L)
    tb = L - np.arange(L)
    tb[0] = 0
    max_decay = np.log(1e-2) / 0.3
    min_decay = np.log(1e-2) / 1.5
    delta = np.abs(np.linspace(min_decay, max_decay, 512))[g * 128:(g + 1) * 128]

    def dtab(t):
        t01 = t.astype(np.float64) / (L - 1)
        return np.exp(-t01[:, None] * delta[None, :]).reshape(NHI, 128, 128)

    nh = np.arange(NHI)[:, None]
    kl = np.arange(NK)[None, :]
    a = 2 * np.pi * (nh * kl % NK) / NK
    t1f = np.concatenate([np.cos(a), -np.sin(a), -np.cos(a)], 1)
    a = 2 * np.pi * ((nh + NHI) * kl % NK) / NK
    t1b = np.concatenate([np.cos(a), -np.sin(a), -np.cos(a)], 1)
    nlo = np.arange(128)[:, None, None]
    klo = np.arange(NK)[None, :, None]
    khi = np.arange(128)[None, None, :]
    a = 2 * np.pi * ((nlo * (klo + NK * khi)) % N) / N
    e2c, e2s = np.cos(a), np.sin(a)
    kk = np.arange(128)
    a = 2 * np.pi * np.outer(kk, kk) / 128.0
    tir = np.concatenate([np.cos(a), np.sin(a)], 1)
    tii = np.concatenate([-np.sin(a), np.cos(a)], 1)
    klo = np.arange(NK)[:, None, None]
    na = np.arange(128)[None, :, None]
    nb = np.arange(NHI)[None, None, :]
    a = 2 * np.pi * ((klo * (na + 128 * nb)) % N) / N
    eic, eis = np.cos(a) / N, -np.sin(a) / N
    return dict(zf=f32(ztab(tf)), zb=f32(ztab(tb)), decf=f32(dtab(tf)), decb=f32(dtab(tb)),
                t1f=bf(t1f), t1b=bf(t1b), e2c=bf(e2c), e2s=bf(e2s), tir=bf(tir), tii=bf(tii), eic=bf(eic), eis=bf(eis))


def run_mix2(L, zT_list, I):
    maps = []
    for i in range(NCORE):
        g = i % 4
        tabs = _get("tab2", tables_mix2, L, g)
        w3 = I["hy_f_w3"][0].reshape(64, 2, 2, 512)[:, :, :, g * 128:(g + 1) * 128]
        skipb = np.broadcast_to(I["hy_bias"][0][None, :, g * 128:(g + 1) * 128], (128, 2, 128))
        fb = np.stack([I["hy_f_freq"][0], I["hy_f_b1"][0], I["hy_f_b2"][0]], axis=1)
        m = dict(zT=np.ascontiguousarray(zT_list[i]), w1=np.ascontiguousarray(I["hy_f_w1"][0]),
                 w2=np.ascontiguousarray(I["hy_f_w2"][0]), w3=np.ascontiguousarray(w3, dtype=np.float32),
                 fb=np.ascontiguousarray(fb, dtype=np.float32), skipb=np.ascontiguousarray(skipb, dtype=np.float32))
        m.update(tabs)
        maps.append(m)
    res = _launch(_get("mix2", build_mix2, L), maps)
    return [r["yhT"] for r in res]


FLEX_ROWS = [0, 1, 2, 3, 28, 29, 30, 31]
NTAB = 512 + 8 * 768
NEG = -30000.0


def build_attn(nj=8, lrs=None, nbuf=2, ubar=0):
    k = KB()
    lrs = list(range(32)) if lrs is None else list(lrs)
    NT = 2560
    hT = k.dram("hT", [D, NT], F32, "ExternalInput")
    hcT = k.dram("hcT", [D, 256], F32, "ExternalInput")
    modT = k.dram("modT", [128, 2, 6, 8], F32, "ExternalInput")
    n1g = k.dram("n1g", [128, 8], F32, "ExternalInput")
    w_qkv = k.dram("w_qkv", [D, 3 * D], F32, "ExternalInput")
    BT = k.dram("BT", [16, 64, NTAB], F32, "ExternalInput")
    MK = k.dram("MK", [64, NTAB], F32, "ExternalInput")
    ident = k.dram("ident", [64, 64], F32, "ExternalInput")
    attnT = k.dram("attnT", [2048, D], BF16, "ExternalOutput")
    cx = Ctx(k, npsum=3)
    sc2 = [k.ps("sc", [128, 1024], F32) for _ in range(2)]
    ptp = k.ps("pt", [128, 1024], BF16)
    mod_sb = k.sb("mod", [128, 2, 6, 8], F32)
    n1g_sb = k.sb("n1g", [128, 8], F32)
    idf = k.sb("idf", [64, 64], F32)
    idb = k.sb("idb", [64, 64], BF16)
    onesf = k.sb("onesf", [64, 64], F32)
    k.memset("dve", onesf[:], 1.0, [onesf])
    k.dma("sp", mod_sb[:], modT[:], writes=[mod_sb])
    k.dma("sp", n1g_sb[:], n1g[:], writes=[n1g_sb])
    k.dma("sp", idf[:], ident[:], writes=[idf])
    k.dma("pool", idb[:], ident[:], writes=[idb])
    gm = k.sb("gm", [128, 2, 8], F32)
    for s in range(2):
        k.ts("dve", gm[:, s, :], mod_sb[:, s, 1, :], 1.0, None, ALU.add, None, reads=[mod_sb], writes=[gm])
        k.tt("dve", gm[:, s, :], gm[:, s, :], n1g_sb[:], ALU.mult, reads=[gm, n1g_sb], writes=[gm])
    mk_sb = k.sb("mk", [64, NTAB], BF16)
    for qq in range(4):
        k.dma("pool", mk_sb[:, qq * (NTAB // 4):(qq + 1) * (NTAB // 4)], MK[:, qq * (NTAB // 4):(qq + 1) * (NTAB // 4)],
              writes=[mk_sb])
    uT = k.sb("uT", [128, 8, NT + 256], BF16)
    ud = [Dep() for _ in range(8)]
    k.push()
    x2 = [k.sb("x", [128, 8, 512], F32) for _ in range(2)]
    xd = [[Dep() for _ in range(8)] for _ in range(2)]
    rstd = k.sb("rstd", [128, 512], F32)
    sq2 = [k.sb("sq", [128, 512], BF16) for _ in range(2)]
    tmp2 = [k.sb("tmp", [128, 512], F32) for _ in range(2)]
    hTc = chunked(hT.h)
    hcTc = chunked(hcT.h)
    tiles = [(hTc, t * 512, 512, 0, t * 512) for t in range(5)] + [(hcTc, 0, 256, 1, NT)]
    for ti, (src, s0, n, seg, d0) in enumerate(tiles):
        xs = x2[ti % 2]
        for c in range(8):
            k.dma("sp", xs[:, c, :n], src[:, c, s0:s0 + n], writes=[xd[ti % 2][c]])
        norm_mod(k, cx, lambda c: xs[:, c, :n], xd[ti % 2], n, lambda c: gm[:, seg, c:c + 1],
                 lambda c: mod_sb[:, seg, 0, c:c + 1], lambda c: uT[:, c, d0:d0 + n], ud, rstd, sq2, tmp2,
                 moddeps=[gm, mod_sb])
    k.pop()
    w2 = [k.sb("w", [128, 8, 384], BF16) for _ in range(2)]
    qT2 = [k.sb("qT", [64, 2048], BF16) for _ in range(2)]
    kT2 = [k.sb("kT", [64, NT + 256], BF16) for _ in range(2)]
    V = k.sb("V", [64, 44, 2, 65], BF16)
    k.memset("dve", V[:], 1.0, [V])
    tab2 = [k.sb("tab", [64, NTAB], BF16) for _ in range(2)]
    P2 = [k.sb("P", [64, 1024], BF16) for _ in range(2)]
    PT2 = [k.sb("PT", [64, 1024], BF16) for _ in range(2)]
    at2 = [k.sb("at", [64, 32, 64], BF16) for _ in range(2)]
    nmx2 = [k.sb("nmx", [64, 1], F32) for _ in range(2)]
    rs2 = [k.sb("rs", [64, 1], F32) for _ in range(2)]
    dg2 = [k.sb("dg", [64, 64], F32) for _ in range(2)]
    rc2 = [k.sb("rc", [64, 64], F32) for _ in range(2)]
    wv = w_qkv.h.rearrange("(c p) f -> p c f", p=128)
    un = 0
    for j in range(nj):
        w = w2[j % 2]
        for part in range(3):
            k.dma("pool", w[:, :, part * 128:(part + 1) * 128], wv[:, :, part * D + j * 128: part * D + (j + 1) * 128],
                  writes=[w])
        for hh_ in range(2):
            for t in range(4):
                p = cx.psum()
                for kc in range(8):
                    k.mm(p[:64, :512], w[:, kc, hh_ * 64:(hh_ + 1) * 64], uT[:, kc, 256 + t * 512:256 + (t + 1) * 512],
                         kc == 0, kc == 7, reads=[w, ud[kc]], writes=[p])
                k.act(qT2[hh_][:, t * 512:(t + 1) * 512], p[:64, :512], AF.Copy, reads=[p], writes=[qT2[hh_]], scale=0.125)
            for t in range(6):
                n = 512 if t < 5 else 256
                p = cx.psum()
                for kc in range(8):
                    k.mm(p[:64, :n], w[:, kc, 128 + hh_ * 64:128 + (hh_ + 1) * 64], uT[:, kc, t * 512:t * 512 + n],
                         kc == 0, kc == 7, reads=[w, ud[kc]], writes=[p])
                k.copy("act", kT2[hh_][:, t * 512:t * 512 + n], p[:64, :n], reads=[p], writes=[kT2[hh_]])
        for r0 in range(0, 44, 4):
            p = cx.psum()
            for rr in range(4):
                for kc in range(8):
                    k.mm(p[:64, rr * 128:(rr + 1) * 128], uT[:, kc, (r0 + rr) * 64:(r0 + rr + 1) * 64], w[:, kc, 256:384],
                         kc == 0, kc == 7, reads=[w, ud[kc]], writes=[p])
            for hh_ in range(2):
                k.copy("dve", V[:, r0:r0 + 4, hh_, 0:64],
                       p[:64, :512].rearrange("p (r h c) -> p r h c", r=4, h=2)[:, :, hh_, :], reads=[p], writes=[V])
        for hh in range(2):
            h = 2 * j + hh
            qT, kT = qT2[hh], kT2[hh]
            tab = tab2[h % 2]
            for qq in range(4):
                k.dma("pool", tab[:, qq * (NTAB // 4):(qq + 1) * (NTAB // 4)],
                      BT[h, :, qq * (NTAB // 4):(qq + 1) * (NTAB // 4)], writes=[tab])
            k.tt("pool", tab[:], tab[:], mk_sb[:], ALU.add, reads=[tab, mk_sb], writes=[tab])
            at = at2[h % 2]
            if len(lrs) < 32:
                k.memset("pool", at[:], 0.0, [at])
            for lr in lrs:
                flex = lr in FLEX_ROWS
                if flex:
                    f = FLEX_ROWS.index(lr)
                    W0 = 0 if lr < 4 else 28
                    nW = 12
                    tc0 = 512 + f * 768
                else:
                    W0 = lr
                    nW = 8
                    tc0 = 0
                ncol = nW * 64 + 256
                ub = un % nbuf
                sc = sc2[ub]
                P, PT = P2[ub], PT2[ub]
                nmx, rs, dg, rc = nmx2[ub], rs2[ub], dg2[ub], rc2[ub]
                un += 1
                qv = qT[:, lr * 64:(lr + 1) * 64]
                k.mm(sc[:64, 0:512], qv, kT[:, W0 * 64:W0 * 64 + 512], True, False, reads=[qT, kT], writes=[sc])
                k.mm(sc[:64, 0:512], idb[:], tab[:, tc0:tc0 + 512], False, True, reads=[idb, tab], writes=[sc])
                c1 = 512
                if flex:
                    k.mm(sc[:64, 512:768], qv, kT[:, (W0 + 8) * 64:(W0 + 12) * 64], True, False,
                         reads=[qT, kT], writes=[sc])
                    k.mm(sc[:64, 512:768], idb[:], tab[:, tc0 + 512:tc0 + 768], False, True, reads=[idb, tab], writes=[sc])
                    c1 = 768
                k.mm(sc[:64, c1:c1 + 256], qv, kT[:, NT:NT + 256], True, True, reads=[qT, kT], writes=[sc])
                k.op("dve", lambda e: e.tensor_reduce(out=nmx[:], in_=sc[:64, :ncol], axis=AX.X, op=ALU.max, negate=True),
                     reads=[sc], writes=[nmx])
                k.act(P[:, :ncol], sc[:64, :ncol], AF.Exp, reads=[sc, nmx], writes=[P], bias=nmx[:, 0:1], scale=1.0)
                nch = ncol // 64
                for ch in range(nch):
                    k.op("pe", lambda e: e.transpose(ptp[:64, ch * 64:(ch + 1) * 64], P[:, ch * 64:(ch + 1) * 64], idb[:]),
                         reads=[P, idb], writes=[ptp])
                k.copy("act", PT[:, :ncol], ptp[:64, :ncol], reads=[ptp], writes=[PT])
                po = cx.psum()
                for ch in range(nch):
                    vrow = (W0 + ch) if ch < nW else (40 + ch - nW)
                    k.mm(po[:64, 0:65], PT[:, ch * 64:(ch + 1) * 64], V[:, vrow, hh, :], ch == 0, ch == nch - 1,
                         reads=[V, PT], writes=[po])
                k.op("dve", lambda e: e.reciprocal(out=rc[:, 0:1], in_=po[:64, 64:65]), reads=[po], writes=[rc])
                k.ts("dve", at[:, lr, :], po[:64, 0:64], rc[:, 0:1], None, ALU.mult, None, reads=[po, rc], writes=[at])
                if ubar:
                    k.barrier()
            k.dma("sp", attnT[:, h * 64:(h + 1) * 64].rearrange("(r q) d -> q r d", q=64), at[:], reads=[at], writes=[attnT])
    return k.finish()


def attn_tables(rpb, c):
    q = np.arange(64)[:, None]
    col = np.arange(64)[None, :]
    dc = np.clip(col - q + 15, 0, 30)
    wstart = np.clip(q - 8, 0, 48)
    colok = (col >= wstart) & (col < wstart + 16)
    BT = np.zeros((16, 64, NTAB), np.float32)
    MK = np.full((64, NTAB), NEG, np.float32)

    def fill(c0, drs):
        for i, dr in enumerate(drs):
            if dr is None:
                continue
            sl = slice(c0 + i * 64, c0 + (i + 1) * 64)
            BT[:, :, sl] = rpb[:, dr][:, dc] * colok[None]
            MK[:, sl] = np.where(colok, 0.0, NEG)

    fill(0, [i + 3 for i in range(8)])
    for f, lr in enumerate(FLEX_ROWS):
        r = 32 * c + lr
        rs = min(max(r - 4, 0), 120)
        base = (32 * c - 4) if lr < 4 else (32 * c + 24)
        drs = []
        for i in range(12):
            rho = base + i
            drs.append(rho - r + 7 if rs <= rho < rs + 8 else None)
        fill(512 + f * 768, drs)
    return BT, MK


def run_attn(h_lat0, h_ctx0, mod, I, ncore=NCORE, **dbg):
    ident = np.eye(64, dtype=np.float32)
    n1g = fm_chunks(I["norm1_g"][1]).astype(np.float32)
    maps = []
    for i in range(ncore):
        b, c = i // 4, i % 4
        hh = np.zeros((40 * 64, D), np.float32)
        r0 = 32 * c - 4
        lo, hi = max(r0, 0), min(r0 + 40, 128)
        hh[(lo - r0) * 64:(hi - r0) * 64] = h_lat0[b, lo * 64:hi * 64]
        modT = np.stack([fm_chunks(mod[1, cnd].reshape(6, 1024)) for cnd in (b, 2)], axis=1)
        BT, MK = _get("attntab%d" % id(I), attn_tables, I["na_rpb"][0], c) if False else attn_tables(I["na_rpb"][0], c)
        maps.append(dict(hT=np.ascontiguousarray(hh.T), hcT=np.ascontiguousarray(h_ctx0[b].T, dtype=np.float32),
                         modT=np.ascontiguousarray(modT, dtype=np.float32), n1g=n1g,
                         w_qkv=np.ascontiguousarray(I["na_w_qkv"][0]), BT=BT, MK=MK, ident=ident))
    res = _launch(_get("attn", build_attn, *dbg.values()), maps)
    return [r["attnT"] for r in res]


def kernel(**inputs):
    I = {k_: np.asarray(v) for k_, v in inputs.items()}
    x = I["x"].astype(np.float32, copy=False)
    ctx = I["ctx"].astype(np.float32, copy=False)
    mod = run_ada(I)
    xT = [np.ascontiguousarray(x[i // 4].T) for i in range(NCORE)]
    yf, z = run_mix1(8192, xT, [mod[0, i // 4] for i in range(NCORE)], I)
    yh = run_mix2(8192, z, I)
    cT = [np.ascontiguousarray(ctx[i // 4].T) for i in range(NCORE)]
    yfc, zc = run_mix1(256, cT, [mod[0, 2] for _ in range(NCORE)], I)
    yhc = run_mix2(256, zc, I)
    yT = [np.concatenate([yf[b * 4 + g] for g in range(4)] + [yh[b * 4 + g] for g in range(4)], axis=0) for b in range(2)]
    yTc = [np.concatenate([yfc[b * 4 + g] for g in range(4)] + [yhc[b * 4 + g] for g in range(4)], axis=0) for b in range(2)]
    hT_l, yT_l = [], []
    for i in range(NCORE):
        b, j = i // 4, i % 4
        hT_l.append(np.concatenate([xT[i][:, 2048 * j:2048 * (j + 1)], cT[i][:, 64 * j:64 * (j + 1)]], axis=1))
        yT_l.append(np.concatenate([yT[b][:, 2048 * j:2048 * (j + 1)], yTc[b][:, 64 * j:64 * (j + 1)]], axis=1))
    o0 = run_tail(0, hT_l, yT_l, mod, I, False)
    h_lat0 = np.empty((2, 8192, D), np.float32)
    h_ctx0 = np.empty((2, 256, D), np.float32)
    for i in range(NCORE):
        b, j = i // 4, i % 4
        h_lat0[b, 2048 * j:2048 * (j + 1)] = o0[i][:, :2048].T
        h_ctx0[b, 64 * j:64 * (j + 1)] = o0[i][:, 2048:].T
    at = run_attn(h_lat0, h_ctx0, mod, I)
    hT_l = [np.ascontiguousarray(h_lat0[i // 4, 2048 * (i % 4):2048 * (i % 4 + 1)].T) for i in range(NCORE)]
    atT = [np.ascontiguousarray(a.T) for a in at]
    o1 = run_tail(1, hT_l, atT, mod, I, True)
    out = np.empty((2, 8192, D), np.float32)
    for i in range(NCORE):
        b, j = i // 4, i % 4
        out[b, 2048 * j:2048 * (j + 1)] = o1[i].T
    return out
```

```python
import numpy as np
from contextlib import ExitStack
import ml_dtypes
import concourse.bass as bass
import concourse.mybir as mybir
from concourse.bass_utils import run_bass_kernel_spmd

F32 = mybir.dt.float32
BF16 = mybir.dt.bfloat16
AF = mybir.ActivationFunctionType
ALU = mybir.AluOpType
AX = mybir.AxisListType
NPBF = ml_dtypes.bfloat16

D = 1024
DFF = 2816
NCORE = 8
EPS = 1e-6
SAME_ENGINE_SYNC = True


class Dep:
    __slots__ = ("w", "r")

    def __init__(self):
        self.w = []
        self.r = []


class Tl:
    def __init__(self, h):
        self.h = h
        self.dep = Dep()

    def __getitem__(self, idx):
        return self.h[idx]


def _dep(x):
    return x.dep if isinstance(x, Tl) else x


class KB:
    def __init__(self):
        self.nc = bass.Bass("TRN2", target_bir_lowering=False)
        nc = self.nc
        self.es = ExitStack()
        self.scopes = [self.es]
        self.eng = dict(pe=nc.tensor, act=nc.scalar, dve=nc.vector, pool=nc.gpsimd, sp=nc.sync)
        self.sems = {}
        self.epoch = {e: 0 for e in self.eng}
        for e in self.eng:
            self.sems[(e, 0)] = self.es.enter_context(nc.semaphore("sem_" + e))
        self.cnt = {e: 0 for e in self.eng}
        self.seen = {e: {} for e in self.eng}
        self.dq = {}
        for q, n in (("sp", 20), ("pool", 20)):
            keys = []
            for i in range(n):
                key = ("d", q, i)
                self.sems[key] = self.es.enter_context(nc.semaphore("dsem_%s_%d" % (q, i)))
                keys.append(key)
            self.dq[q] = dict(keys=keys, i=0, cnt=[0] * n)
        self.uid = 0

    def name(self, base):
        self.uid += 1
        return "%s_%d" % (base, self.uid)

    def sb(self, name, shape, dt):
        return Tl(self.scopes[-1].enter_context(self.nc.sbuf_tensor(self.name(name), list(shape), dt)))

    def ps(self, name, shape, dt=F32):
        return Tl(self.scopes[-1].enter_context(self.nc.psum_tensor(self.name(name), list(shape), dt)))

    def dram(self, name, shape, dt, kind):
        return Tl(self.nc.dram_tensor(name, list(shape), dt, kind=kind).ap())

    def push(self):
        es = ExitStack()
        self.scopes.append(es)

    def pop(self):
        self.barrier()
        es = self.scopes.pop()
        es.close()

    def _wait(self, e, key, val):
        if self.seen[e].get(key, 0) >= val:
            return
        self.eng[e].wait_ge(self.sems[key], val)
        self.seen[e][key] = val

    def _collect(self, e, reads, writes):
        evs = []
        for d in reads:
            evs += _dep(d).w
        for d in writes:
            d = _dep(d)
            evs += d.w
            evs += d.r
        for (key, val, src) in evs:
            if src == e and (e == "pe" or not SAME_ENGINE_SYNC):
                continue
            self._wait(e, key, val)

    def _record(self, ev, reads, writes):
        for d in writes:
            d = _dep(d)
            d.w = [ev]
            d.r = []
        for d in reads:
            d = _dep(d)
            d.r = [x for x in d.r if x[0] != ev[0]] + [ev]

    EPOCH = 3000

    def op(self, e, fn, reads=(), writes=()):
        if self.cnt[e] >= self.EPOCH:
            self.epoch[e] += 1
            self.sems[(e, self.epoch[e])] = self.es.enter_context(
                self.nc.semaphore("sem_%s_%d" % (e, self.epoch[e])))
            self.cnt[e] = 0
        self._collect(e, reads, writes)
        ins = fn(self.eng[e])
        self.cnt[e] += 1
        key = (e, self.epoch[e])
        ins.then_inc(self.sems[key], 1)
        self._record((key, self.cnt[e], e), reads, writes)
        return ins

    def dma(self, q, out, in_, reads=(), writes=(), **kw):
        self._collect(q, reads, writes)
        d = self.dq[q]
        i = d["i"]
        d["i"] = (i + 1) % len(d["keys"])
        key = d["keys"][i]
        prev = d["cnt"][i]
        if prev > 0:
            self._wait(q, key, prev)
        ins = self.eng[q].dma_start(out=out, in_=in_, **kw)
        ins.then_inc(self.sems[key], 16)
        d["cnt"][i] = prev + 16
        self._record((key, prev + 16, "dma"), reads, writes)
        return ins

    def barrier(self):
        for e in self.eng:
            for e2 in self.eng:
                if e2 != e:
                    for ep in range(self.epoch[e2] + 1):
                        c = self.cnt[e2] if ep == self.epoch[e2] else self.EPOCH
                        if c > 0:
                            self._wait(e, (e2, ep), c)
            for q, d in self.dq.items():
                for key, c in zip(d["keys"], d["cnt"]):
                    if c > 0:
                        self._wait(e, key, c)

    def finish(self):
        self.barrier()
        self.es.close()
        return self.nc

    def mm(self, out, lhsT, rhs, start, stop, reads, writes):
        return self.op("pe", lambda e: e.matmul(out, lhsT=lhsT, rhs=rhs, start=start, stop=stop),
                       reads=reads, writes=writes)

    def act(self, out, in_, func, reads, writes, eng="act", **kw):
        return self.op(eng, lambda e: e.activation(out=out, in_=in_, func=func, **kw), reads=reads, writes=writes)

    def tt(self, eng, out, in0, in1, op, reads, writes):
        return self.op(eng, lambda e: e.tensor_tensor(out=out, in0=in0, in1=in1, op=op), reads=reads, writes=writes)

    def ts(self, eng, out, in0, s1, s2, op0, op1, reads, writes):
        if s2 is None:
            return self.op(eng, lambda e: e.tensor_scalar(out=out, in0=in0, scalar1=s1, scalar2=None, op0=op0),
                           reads=reads, writes=writes)
        return self.op(eng, lambda e: e.tensor_scalar(out=out, in0=in0, scalar1=s1, scalar2=s2, op0=op0, op1=op1),
                       reads=reads, writes=writes)

    def stt(self, out, in0, scalar, in1, op0, op1, reads, writes):
        return self.op("dve", lambda e: e.scalar_tensor_tensor(out=out, in0=in0, scalar=scalar, in1=in1,
                                                                op0=op0, op1=op1), reads=reads, writes=writes)

    def copy(self, eng, out, in_, reads, writes):
        if eng == "act":
            return self.op(eng, lambda e: e.copy(out=out, in_=in_), reads=reads, writes=writes)
        return self.op(eng, lambda e: e.tensor_copy(out=out, in_=in_), reads=reads, writes=writes)

    def memset(self, eng, ap, val, writes):
        return self.op(eng, lambda e: e.memset(ap, val), reads=(), writes=writes)


def chunked(ap2d):
    return ap2d.rearrange("(c p) t -> p c t", p=128)


class Ctx:
    def __init__(self, k, npsum=8):
        self.k = k
        self.psb = [k.ps("psb", [128, 512], F32) for _ in range(npsum)]
        self.pi = 0
        self.ones = k.sb("ones", [128, 128], BF16)
        k.memset("dve", self.ones[:], 1.0, [self.ones])
        self.epsb = k.sb("epsb", [128, 1], F32)
        k.memset("dve", self.epsb[:], EPS, [self.epsb])

    def psum(self):
        p = self.psb[self.pi]
        self.pi = (self.pi + 1) % len(self.psb)
        return p


def rms_rstd(k, cx, x, xdeps, n, rstd, sq2):
    ps = cx.psum()
    for c in range(8):
        sq = sq2[c % 2]
        k.act(sq[:, :n], x(c), AF.Square, reads=[xdeps[c]], writes=[sq])
        k.mm(ps[:, :n], cx.ones[:], sq[:, :n], c == 0, c == 7, reads=[sq, cx.ones], writes=[ps])
    k.act(rstd[:, :n], ps[:, :n], AF.Sqrt, reads=[ps, cx.epsb], writes=[rstd], bias=cx.epsb[:], scale=1.0 / D)
    k.op("dve", lambda e: e.reciprocal(out=rstd[:, :n], in_=rstd[:, :n]), reads=[rstd], writes=[rstd])


def norm_mod(k, cx, x, xdeps, n, gmod, shift, out, odeps, rstd, sq2, tmp2, moddeps=()):
    rms_rstd(k, cx, x, xdeps, n, rstd, sq2)
    for c in range(8):
        if shift is None:
            k.stt(out(c), x(c), gmod(c), rstd[:, :n], ALU.mult, ALU.mult,
                  reads=[xdeps[c], rstd] + list(moddeps), writes=[odeps[c]])
        else:
            tmp = tmp2[c % 2]
            k.stt(tmp[:, :n], x(c), gmod(c), rstd[:, :n], ALU.mult, ALU.mult,
                  reads=[xdeps[c], rstd] + list(moddeps), writes=[tmp])
            k.act(out(c), tmp[:, :n], AF.Identity, reads=[tmp] + list(moddeps), writes=[odeps[c]],
                  bias=shift(c), scale=1.0)


def load_w_bf16(k, dst, dst_dep, w_ap, kchunks, ncols, col0=0, q="pool"):
    src = w_ap.rearrange("(c p) f -> p c f", p=128)
    step = 8
    for c0 in range(0, kchunks, step):
        c1 = min(kchunks, c0 + step)
        k.dma(q, dst[:, c0:c1, :ncols], src[:, c0:c1, col0:col0 + ncols], reads=(), writes=[dst_dep])


def build_ada():
    k = KB()
    condT = k.dram("condT", [128, 8, 3], F32, "ExternalInput")
    w = k.dram("w", [2, 1024, 768], F32, "ExternalInput")
    b = k.dram("b", [2, 3, 768], F32, "ExternalInput")
    out = k.dram("out", [2, 3, 768], F32, "ExternalOutput")
    c_sb = k.sb("c", [128, 8, 3], F32)
    s_sb = k.sb("s", [128, 8, 3], F32)
    w_sb = k.sb("w", [128, 2, 8, 768], F32)
    b_sb = k.sb("b", [3, 2, 768], F32)
    o_sb = k.sb("o", [3, 2, 768], F32)
    ps = [k.ps("ps", [128, 512], F32) for _ in range(4)]
    k.dma("sp", c_sb[:], condT[:], writes=[c_sb])
    for l in range(2):
        k.dma("sp", w_sb[:, l, :, :], w[l].rearrange("(c p) f -> p c f", p=128), writes=[w_sb])
        k.dma("sp", b_sb[:, l, :], b[l], writes=[b_sb])
    k.act(s_sb[:], c_sb[:], AF.Silu, reads=[c_sb], writes=[s_sb])
    for l in range(2):
        for hf in range(2):
            p = ps[l * 2 + hf]
            for c in range(8):
                k.mm(p[:3, :384], s_sb[:, c, :], w_sb[:, l, c, hf * 384:(hf + 1) * 384], c == 0, c == 7,
                     reads=[s_sb, w_sb], writes=[p])
            k.tt("dve", o_sb[:, l, hf * 384:(hf + 1) * 384], p[:3, :384], b_sb[:, l, hf * 384:(hf + 1) * 384],
                 ALU.add, reads=[p, b_sb], writes=[o_sb])
    k.dma("sp", out.h.rearrange("l c f -> c l f"), o_sb[:], reads=[o_sb], writes=[out])
    return k.finish()


def build_tail(TT, segs, final):
    k = KB()
    hT = k.dram("hT", [D, TT], F32, "ExternalInput")
    yT = k.dram("yT", [D, TT], BF16, "ExternalInput")
    nseg = 1 + max(s[2] for g in segs for s in g)
    modT = k.dram("modT", [128, nseg, 6, 8], F32, "ExternalInput")
    ng = k.dram("ng", [128, 2, 8], F32, "ExternalInput")
    w_mo = k.dram("w_mo", [D, D], F32, "ExternalInput")
    w_g = k.dram("w_g", [D, DFF], F32, "ExternalInput")
    w_u = k.dram("w_u", [D, DFF], F32, "ExternalInput")
    w_d = k.dram("w_d", [DFF, D], F32, "ExternalInput")
    oT = k.dram("oT", [D, TT], F32, "ExternalOutput")
    cx = Ctx(k)
    GM = max(sum(s[1] for s in g) for g in segs)
    NF = DFF // 128
    wmo_sb = k.sb("wmo", [128, 8, D], BF16)
    wd_sb = k.sb("wd", [128, NF, D], BF16)
    load_w_bf16(k, wmo_sb, wmo_sb, w_mo.h, 8, D)
    load_w_bf16(k, wd_sb, wd_sb, w_d.h, NF, D)
    mod_sb = k.sb("mod", [128, nseg, 6, 8], F32)
    ng_sb = k.sb("ng", [128, 2, 8], F32)
    k.dma("sp", mod_sb[:], modT[:], writes=[mod_sb])
    k.dma("sp", ng_sb[:], ng[:], writes=[ng_sb])
    gm2 = k.sb("gm2", [128, nseg, 8], F32)
    for s in range(nseg):
        k.ts("dve", gm2[:, s, :], mod_sb[:, s, 4, :], 1.0, None, ALU.add, None, reads=[mod_sb], writes=[gm2])
        k.tt("dve", gm2[:, s, :], gm2[:, s, :], ng_sb[:, 0, :], ALU.mult, reads=[gm2, ng_sb], writes=[gm2])
    h_sb = k.sb("h", [128, 8, GM], F32)
    y_sb = k.sb("y", [128, 8, GM], BF16)
    u_sb = k.sb("u", [128, 8, GM], BF16)
    a_sb = k.sb("a", [128, NF, GM], BF16)
    hd = [Dep() for _ in range(8)]
    yd = [Dep() for _ in range(8)]
    ud = [Dep() for _ in range(8)]
    ad = [Dep() for _ in range(NF)]
    rstd = k.sb("rstd", [128, 512], F32)
    sq2 = [k.sb("sq", [128, 512], BF16) for _ in range(2)]
    tmp2 = [k.sb("tmp", [128, 512], F32) for _ in range(2)]
    sg2 = [k.sb("sg", [128, 512], BF16) for _ in range(2)]
    wg2 = [k.sb("wg", [128, 8, 128], BF16) for _ in range(2)]
    wu2 = [k.sb("wu", [128, 8, 128], BF16) for _ in range(2)]
    hTc = chunked(hT.h)
    yTc = chunked(yT.h)
    oTc = chunked(oT.h)
    wgv = w_g.h.rearrange("(c p) f -> p c f", p=128)
    wuv = w_u.h.rearrange("(c p) f -> p c f", p=128)
    for g in segs:
        g0 = g[0][0]
        G = sum(s[1] for s in g)
        for c in range(8):
            k.dma("sp", h_sb[:, c, :G], hTc[:, c, g0:g0 + G], writes=[hd[c]])
            k.dma("sp", y_sb[:, c, :G], yTc[:, c, g0:g0 + G], writes=[yd[c]])
        for (s0, n, sg) in g:
            o = s0 - g0
            for c in range(8):
                p = cx.psum()
                for kc in range(8):
                    k.mm(p[:, :n], wmo_sb[:, kc, c * 128:(c + 1) * 128], y_sb[:, kc, o:o + n], kc == 0, kc == 7,
                         reads=[wmo_sb, yd[kc]], writes=[p])
                k.stt(h_sb[:, c, o:o + n], p[:, :n], mod_sb[:, sg, 2, c:c + 1], h_sb[:, c, o:o + n],
                      ALU.mult, ALU.add, reads=[p, mod_sb, hd[c]], writes=[hd[c]])
        for (s0, n, sg) in g:
            o = s0 - g0
            norm_mod(k, cx, lambda c: h_sb[:, c, o:o + n], hd, n,
                     lambda c: gm2[:, sg, c:c + 1], lambda c: mod_sb[:, sg, 3, c:c + 1],
                     lambda c: u_sb[:, c, o:o + n], ud, rstd, sq2, tmp2, moddeps=[gm2, mod_sb])
        for f in range(NF):
            wg = wg2[f % 2]
            wu = wu2[f % 2]
            k.dma("pool", wg[:], wgv[:, :, f * 128:(f + 1) * 128], writes=[wg])
            k.dma("pool", wu[:], wuv[:, :, f * 128:(f + 1) * 128], writes=[wu])
            for (s0, n, sg) in g:
                o = s0 - g0
                pg = cx.psum()
                pu = cx.psum()
                for kc in range(8):
                    k.mm(pg[:, :n], wg[:, kc, :], u_sb[:, kc, o:o + n], kc == 0, kc == 7, reads=[wg, ud[kc]], writes=[pg])
                for kc in range(8):
                    k.mm(pu[:, :n], wu[:, kc, :], u_sb[:, kc, o:o + n], kc == 0, kc == 7, reads=[wu, ud[kc]], writes=[pu])
                sgt = sg2[f % 2]
                k.act(sgt[:, :n], pg[:, :n], AF.Silu, reads=[pg], writes=[sgt])
                k.tt("dve", a_sb[:, f, o:o + n], sgt[:, :n], pu[:, :n], ALU.mult, reads=[sgt, pu], writes=[ad[f]])
        for (s0, n, sg) in g:
            o = s0 - g0
            for c in range(8):
                p = cx.psum()
                for f in range(NF):
                    k.mm(p[:, :n], wd_sb[:, f, c * 128:(c + 1) * 128], a_sb[:, f, o:o + n], f == 0, f == NF - 1,
                         reads=[wd_sb, ad[f]], writes=[p])
                k.stt(h_sb[:, c, o:o + n], p[:, :n], mod_sb[:, sg, 5, c:c + 1], h_sb[:, c, o:o + n],
                      ALU.mult, ALU.add, reads=[p, mod_sb, hd[c]], writes=[hd[c]])
        if final:
            for (s0, n, sg) in g:
                o = s0 - g0
                norm_mod(k, cx, lambda c: h_sb[:, c, o:o + n], hd, n,
                         lambda c: ng_sb[:, 1, c:c + 1], None,
                         lambda c: h_sb[:, c, o:o + n], hd, rstd, sq2, tmp2, moddeps=[ng_sb])
        for c in range(8):
            k.dma("sp", oTc[:, c, g0:g0 + G], h_sb[:, c, :G], reads=[hd[c]], writes=[oT])
    return k.finish()


_NC_CACHE = {}


def _get(name, fn, *args):
    key = (name,) + tuple(str(a) for a in args)
    if key not in _NC_CACHE:
        _NC_CACHE[key] = fn(*args)
    return _NC_CACHE[key]


def _launch(nc, maps):
    res = run_bass_kernel_spmd(nc, maps, core_ids=list(range(len(maps))))
    return res.results


def fm_chunks(v):
    v = np.asarray(v)
    lead = v.shape[:-1]
    r = v.reshape(lead + (8, 128))
    return np.ascontiguousarray(np.moveaxis(r, -1, 0))


def run_ada(I):
    cond = np.stack([I["c"][0], I["c"][1], I["c_ctx"]]).astype(np.float32)
    condT = np.ascontiguousarray(cond.reshape(3, 8, 128).transpose(2, 1, 0))
    maps = []
    for i in range(NCORE):
        sl = slice(768 * i, 768 * (i + 1))
        maps.append(dict(condT=condT, w=np.ascontiguousarray(I["ada_w"][:, :, sl]),
                         b=np.ascontiguousarray(np.broadcast_to(I["ada_b"][:, None, sl], (2, 3, 768)))))
    res = _launch(_get("ada", build_ada), maps)
    return np.concatenate([r["out"] for r in res], axis=2)


SEGS_L0 = [[(0, 512, 0), (512, 512, 0)], [(1024, 512, 0), (1536, 512, 0), (2048, 64, 1)]]
SEGS_L1 = [[(0, 512, 0), (512, 512, 0)], [(1024, 512, 0), (1536, 512, 0)]]


def run_tail(layer, hT_list, yT_list, mod, I, final):
    TT = hT_list[0].shape[1]
    segs = SEGS_L0 if TT == 2112 else SEGS_L1
    ng = np.stack([fm_chunks(I["norm2_g"][layer]), fm_chunks(I["final_g"])], axis=1).astype(np.float32)
    if layer == 0:
        w_mo = I["ab_w_out"][0]
    else:
        w_mo = I["na_w_out"][0]
    maps = []
    for i in range(NCORE):
        b = i // 4
        conds = [b, 2] if TT == 2112 else [b]
        modT = np.stack([fm_chunks(mod[layer, cnd].reshape(6, 1024)) for cnd in conds], axis=1)
        maps.append(dict(hT=np.ascontiguousarray(hT_list[i], dtype=np.float32),
                         yT=np.ascontiguousarray(yT_list[i]).astype(NPBF) if yT_list[i].dtype != NPBF else np.ascontiguousarray(yT_list[i]),
                         modT=np.ascontiguousarray(modT, dtype=np.float32), ng=ng,
                         w_mo=np.ascontiguousarray(w_mo), w_g=np.ascontiguousarray(I["ffn_w_gate"][layer]),
                         w_u=np.ascontiguousarray(I["ffn_w_up"][layer]), w_d=np.ascontiguousarray(I["ffn_w_down"][layer])))
    res = _launch(_get("tail", build_tail, TT, segs, final), maps)
    return [r["oT"] for r in res]


def build_mix1(L):
    NHI = L // 128
    TW = min(512, L)
    k = KB()
    xT = k.dram("xT", [D, L], F32, "ExternalInput")
    modT = k.dram("modT", [128, 6, 8], F32, "ExternalInput")
    n1g = k.dram("n1g", [128, 8], F32, "ExternalInput")
    w_in = k.dram("w_in", [D, 512], F32, "ExternalInput")
    cw = k.dram("cw", [128, 3, 3], F32, "ExternalInput")
    cb = k.dram("cb", [128, 3], F32, "ExternalInput")
    cs128 = k.dram("cs128", [128, 256], F32, "ExternalInput")
    tbr = k.dram("tbr", [NHI, 2 * NHI], F32, "ExternalInput")
    tbi = k.dram("tbi", [NHI, 2 * NHI], F32, "ExternalInput")
    ec = k.dram("ec", [128, L], F32, "ExternalInput")
    es = k.dram("es", [128, L], F32, "ExternalInput")
    yfT = k.dram("yfT", [128, L], BF16, "ExternalOutput")
    zT = k.dram("zT", [3, 128, L], BF16, "ExternalOutput")
    cx = Ctx(k)
    mod_sb = k.sb("mod", [128, 6, 8], F32)
    n1g_sb = k.sb("n1g", [128, 8], F32)
    cw_sb = k.sb("cw", [128, 3, 3], F32)
    cb_sb = k.sb("cb", [128, 3], F32)
    k.dma("sp", mod_sb[:], modT[:], writes=[mod_sb])
    k.dma("sp", n1g_sb[:], n1g[:], writes=[n1g_sb])
    k.dma("sp", cw_sb[:], cw[:], writes=[cw_sb])
    k.dma("sp", cb_sb[:], cb[:], writes=[cb_sb])
    gm = k.sb("gm", [128, 8], F32)
    k.ts("dve", gm[:], mod_sb[:, 1, :], 1.0, None, ALU.add, None, reads=[mod_sb], writes=[gm])
    k.tt("dve", gm[:], gm[:], n1g_sb[:], ALU.mult, reads=[gm, n1g_sb], writes=[gm])
    gT = k.sb("gT", [128, 128, NHI], BF16)
    k.push()
    w_sb = k.sb("w", [128, 8, 512], BF16)
    load_w_bf16(k, w_sb, w_sb, w_in.h, 8, 512)
    pT = k.sb("pT", [128, 3, L + 2], BF16)
    for s in range(3):
        k.memset("pool", pT[:, s, 0:1], 0.0, [pT])
        k.memset("pool", pT[:, s, L + 1:L + 2], 0.0, [pT])
    x2 = [k.sb("x", [128, 8, TW], F32) for _ in range(2)]
    u2 = [k.sb("u", [128, 8, TW], BF16) for _ in range(2)]
    xd = [[Dep() for _ in range(8)] for _ in range(2)]
    ud = [[Dep() for _ in range(8)] for _ in range(2)]
    rstd = k.sb("rstd", [128, 512], F32)
    sq2 = [k.sb("sq", [128, 512], BF16) for _ in range(2)]
    tmp2 = [k.sb("tmp", [128, 512], F32) for _ in range(2)]
    xTc = chunked(xT.h)
    pdep = [Dep() for _ in range(L // TW)]
    for t in range(L // TW):
        t0 = t * TW
        xs, us = x2[t % 2], u2[t % 2]
        for c in range(8):
            k.dma("sp", xs[:, c, :], xTc[:, c, t0:t0 + TW], writes=[xd[t % 2][c]])
        norm_mod(k, cx, lambda c: xs[:, c, :], xd[t % 2], TW, lambda c: gm[:, c:c + 1],
                 lambda c: mod_sb[:, 0, c:c + 1], lambda c: us[:, c, :], ud[t % 2], rstd, sq2, tmp2,
                 moddeps=[gm, mod_sb])
        for s in range(4):
            p = cx.psum()
            for kc in range(8):
                k.mm(p[:, :TW], w_sb[:, kc, s * 128:(s + 1) * 128], us[:, kc, :], kc == 0, kc == 7,
                     reads=[w_sb, ud[t % 2][kc]], writes=[p])
            if s == 0:
                nh = TW // 128
                h0 = t0 // 128
                k.copy("act", gT[:, :, h0:h0 + nh].rearrange("p l h -> p h l"),
                       p[:, :TW].rearrange("p (h l) -> p h l", l=128), reads=[p], writes=[gT])
            else:
                k.copy("act", pT[:, s - 1, 1 + t0:1 + t0 + TW], p[:, :TW], reads=[p], writes=[pdep[t], pT])
    CBK = min(2048, L)
    acc2 = [k.sb("acc", [128, CBK], F32) for _ in range(2)]
    zb2 = [k.sb("zb", [128, CBK], BF16) for _ in range(2)]
    i = 0
    for s in range(3):
        for c0 in range(0, L, CBK):
            acc, zb = acc2[i % 2], zb2[i % 2]
            i += 1
            k.ts("dve", acc[:], pT[:, s, 1 + c0:1 + c0 + CBK], cw_sb[:, s, 1:2], cb_sb[:, s:s + 1], ALU.mult, ALU.add,
                 reads=[pT, cw_sb, cb_sb], writes=[acc])
            k.stt(acc[:], pT[:, s, c0:c0 + CBK], cw_sb[:, s, 0:1], acc[:], ALU.mult, ALU.add,
                  reads=[pT, cw_sb, acc], writes=[acc])
            k.stt(zb[:], pT[:, s, 2 + c0:2 + c0 + CBK], cw_sb[:, s, 2:3], acc[:], ALU.mult, ALU.add,
                  reads=[pT, cw_sb, acc], writes=[zb])
            k.dma("sp", zT[s, :, c0:c0 + CBK], zb[:], reads=[zb], writes=[zT])
    k.pop()
    k.push()
    cs_sb = k.sb("cs", [128, 256], BF16)
    tbr_sb = k.sb("tbr", [NHI, 2 * NHI], BF16)
    tbi_sb = k.sb("tbi", [NHI, 2 * NHI], BF16)
    ec_sb = k.sb("ec", [128, NHI, 128], BF16)
    es_sb = k.sb("es", [128, NHI, 128], BF16)
    k.dma("pool", cs_sb[:], cs128[:], writes=[cs_sb])
    k.dma("pool", tbr_sb[:], tbr[:], writes=[tbr_sb])
    k.dma("pool", tbi_sb[:], tbi[:], writes=[tbi_sb])
    ecv = ec.h.rearrange("p (a b) -> p a b", b=128)
    esv = es.h.rearrange("p (a b) -> p a b", b=128)
    st = max(1, NHI // 4)
    for a0 in range(0, NHI, st):
        k.dma("pool", ec_sb[:, a0:a0 + st, :], ecv[:, a0:a0 + st, :], writes=[ec_sb])
        k.dma("pool", es_sb[:, a0:a0 + st, :], esv[:, a0:a0 + st, :], writes=[es_sb])
    A1 = k.sb("A1", [NHI, 2, 128, 128], BF16)
    for l0 in range(0, 128, 2):
        p = cx.psum()
        for j in range(2):
            k.mm(p[:NHI, j * 256:(j + 1) * 256], gT[:, l0 + j, :], cs_sb[:], True, True, reads=[gT, cs_sb], writes=[p])
        k.copy("dve" if (l0 // 2) % 2 == 0 else "act",
               A1[:, :, :, l0:l0 + 2].rearrange("p r c l -> p l r c"),
               p[:NHI, :512].rearrange("p (l r c) -> p l r c", l=2, r=2), reads=[p], writes=[A1])
    B1 = k.sb("B1", [128, 2, NHI, 128], BF16)
    W2 = 2 * NHI
    per = max(1, 512 // W2)
    per = min(per, 128)
    for c0 in range(0, 128, per):
        p = cx.psum()
        for j in range(per):
            kc_ = c0 + j
            k.mm(p[:, j * W2:(j + 1) * W2], A1[:, 0, kc_, :], tbr_sb[:], True, False, reads=[A1, tbr_sb], writes=[p])
            k.mm(p[:, j * W2:(j + 1) * W2], A1[:, 1, kc_, :], tbi_sb[:], False, True, reads=[A1, tbi_sb], writes=[p])
        k.copy("dve" if (c0 // per) % 2 == 0 else "act",
               B1[:, :, :, c0:c0 + per].rearrange("p r k c -> p c r k"),
               p[:, :per * W2].rearrange("p (c r k) -> p c r k", c=per, r=2), reads=[p], writes=[B1])
    yf = k.sb("yf", [128, 128, NHI], BF16)
    scale = 1.0 / float(np.sqrt(L * 128.0))
    per = min(4, NHI)
    for q0 in range(0, NHI, per):
        p = cx.psum()
        for j in range(per):
            kl = q0 + j
            k.mm(p[:, j * 128:(j + 1) * 128], B1[:, 0, kl, :], ec_sb[:, kl, :], True, False, reads=[B1, ec_sb], writes=[p])
            k.mm(p[:, j * 128:(j + 1) * 128], B1[:, 1, kl, :], es_sb[:, kl, :], False, True, reads=[B1, es_sb], writes=[p])
        k.act(yf[:, :, q0:q0 + per].rearrange("p h l -> p l h"),
              p[:, :per * 128].rearrange("p (l h) -> p l h", l=per), AF.Copy, reads=[p], writes=[yf], scale=scale)
    k.dma("sp", yfT[:, :], yf[:].rearrange("p h l -> p (h l)"), reads=[yf], writes=[yfT])
    k.pop()
    return k.finish()


def dft_tables_mix1(L):
    NHI = L // 128
    c = np.arange(128)
    ang = 2 * np.pi * np.outer(c, c) / 128.0
    cs128 = np.concatenate([np.cos(ang), -np.sin(ang)], 1)
    h = np.arange(NHI)
    angb = 2 * np.pi * np.outer(h, h) / NHI
    tbr = np.concatenate([np.cos(angb), -np.sin(angb)], 1)
    tbi = np.concatenate([np.sin(angb), np.cos(angb)], 1)
    nlo = np.arange(128)[:, None, None]
    klo = np.arange(NHI)[None, :, None]
    khi = np.arange(128)[None, None, :]
    ange = 2 * np.pi * ((nlo * (klo + NHI * khi)) % L) / L
    ec = np.cos(ange).reshape(128, L)
    es = np.sin(ange).reshape(128, L)
    f = lambda a: np.ascontiguousarray(a, dtype=np.float32)
    return dict(cs128=f(cs128), tbr=f(tbr), tbi=f(tbi), ec=f(ec), es=f(es))


def run_mix1(L, xT_list, mod_list, I):
    tabs = _get("tab1", dft_tables_mix1, L)
    n1g = fm_chunks(I["norm1_g"][0]).astype(np.float32)
    maps = []
    for i in range(NCORE):
        g = i % 4
        cols = np.concatenate([np.arange(128) + 128 * g] + [512 + s * 512 + 128 * g + np.arange(128) for s in range(3)])
        w = np.ascontiguousarray(I["ab_w_in"][0][:, cols])
        hc = [s * 512 + 128 * g + np.arange(128) for s in range(3)]
        cw = np.stack([I["hy_conv_w"][0][:, c_].T for c_ in hc], axis=1)
        cb = np.stack([I["hy_conv_b"][0][c_] for c_ in hc], axis=1)
        m = dict(xT=np.ascontiguousarray(xT_list[i], dtype=np.float32),
                 modT=np.ascontiguousarray(fm_chunks(mod_list[i].reshape(6, 1024)), dtype=np.float32),
                 n1g=n1g, w_in=w, cw=np.ascontiguousarray(cw, dtype=np.float32),
                 cb=np.ascontiguousarray(cb, dtype=np.float32))
        m.update(tabs)
        maps.append(m)
    res = _launch(_get("mix1", build_mix1, L), maps)
    return [r["yfT"] for r in res], [r["zT"] for r in res]


PI = float(np.pi)


def build_mix2(L):
    NHI = L // 128
    NK = 2 * NHI
    CB = 32 if L > 256 else 128
    NB = 128 // CB
    TW = min(512, L)
    k = KB()
    zT = k.dram("zT", [3, 128, L], BF16, "ExternalInput")
    zf = k.dram("zf", [33, L], F32, "ExternalInput")
    zb = k.dram("zb", [33, L], F32, "ExternalInput")
    w1 = k.dram("w1", [33, 64], F32, "ExternalInput")
    w2 = k.dram("w2", [64, 64], F32, "ExternalInput")
    w3 = k.dram("w3", [64, 2, 2, 128], F32, "ExternalInput")
    fb = k.dram("fb", [64, 3], F32, "ExternalInput")
    decf = k.dram("decf", [NHI, 128, 128], F32, "ExternalInput")
    decb = k.dram("decb", [NHI, 128, 128], F32, "ExternalInput")
    skipb = k.dram("skipb", [128, 2, 128], F32, "ExternalInput")
    t1f = k.dram("t1f", [NHI, 3 * NK], BF16, "ExternalInput")
    t1b = k.dram("t1b", [NHI, 3 * NK], BF16, "ExternalInput")
    e2c = k.dram("e2c", [128, NK, 128], BF16, "ExternalInput")
    e2s = k.dram("e2s", [128, NK, 128], BF16, "ExternalInput")
    tir = k.dram("tir", [128, 256], BF16, "ExternalInput")
    tii = k.dram("tii", [128, 256], BF16, "ExternalInput")
    eic = k.dram("eic", [NK, 128, NHI], BF16, "ExternalInput")
    eis = k.dram("eis", [NK, 128, NHI], BF16, "ExternalInput")
    yhT = k.dram("yhT", [128, L], BF16, "ExternalOutput")
    cx = Ctx(k)
    w1_sb = k.sb("w1", [33, 64], F32)
    w2_sb = k.sb("w2", [64, 64], F32)
    w3_sb = k.sb("w3", [64, 2, 2, 128], BF16)
    fb_sb = k.sb("fb", [64, 3], F32)
    skip_sb = k.sb("skip", [128, 2, 128], F32)
    t1f_sb = k.sb("t1f", [NHI, 3 * NK], BF16)
    t1b_sb = k.sb("t1b", [NHI, 3 * NK], BF16)
    tir_sb = k.sb("tir", [128, 256], BF16)
    tii_sb = k.sb("tii", [128, 256], BF16)
    onesf = k.sb("onesf", [128, 128], F32)
    k.memset("dve", onesf[:], 1.0, [onesf])
    negpi = k.sb("negpi", [128, 1], F32)
    k.memset("dve", negpi[:], 0.0, [negpi])
    for dst, src in ((w1_sb, w1), (w2_sb, w2), (fb_sb, fb), (skip_sb, skipb), (t1f_sb, t1f), (t1b_sb, t1b),
                     (tir_sb, tir), (tii_sb, tii)):
        k.dma("sp", dst[:], src[:], writes=[dst])
    k.dma("pool", w3_sb[:], w3[:], writes=[w3_sb])
    fbb = k.sb("fbb", [64, 2], F32)
    k.ts("dve", fbb[:], fb_sb[:, 1:3], fb_sb[:, 0:1], None, ALU.mult, None, reads=[fb_sb], writes=[fbb])
    h2 = [k.sb("h2", [64, 128, NHI], BF16) for _ in range(2)]
    k.push()
    zt2 = [k.sb("zt", [33, TW], F32) for _ in range(2)]
    y1 = k.sb("y1", [64, TW], F32)
    mk = k.sb("mk", [64, TW], F32)
    h1 = k.sb("h1", [64, TW], F32)

    def sin_layer(ps, col, out_ap, out_dep, in_view=None):
        k.ts("dve", y1[:], ps[:64, :TW], fb_sb[:, 0:1], fbb[:, col:col + 1], ALU.mult, ALU.add,
             reads=[ps, fb_sb, fbb], writes=[y1])
        k.ts("dve", mk[:], y1[:], -PI, None, ALU.is_lt, None, reads=[y1], writes=[mk])
        k.stt(y1[:], mk[:], 2 * PI, y1[:], ALU.mult, ALU.add, reads=[mk, y1], writes=[y1])
        k.ts("dve", mk[:], y1[:], PI, None, ALU.is_gt, None, reads=[y1], writes=[mk])
        k.stt(y1[:], mk[:], -2 * PI, y1[:], ALU.mult, ALU.add, reads=[mk, y1], writes=[y1])
        k.act(out_ap, y1[:] if in_view is None else in_view(y1), AF.Sin, reads=[y1], writes=[out_dep])

    it = 0
    for d, zsrc in enumerate((zf, zb)):
        for t in range(L // TW):
            t0 = t * TW
            zt = zt2[it % 2]
            it += 1
            k.dma("sp", zt[:], zsrc[:, t0:t0 + TW], writes=[zt])
            p = cx.psum()
            k.mm(p[:64, :TW], w1_sb[:], zt[:], True, True, reads=[w1_sb, zt], writes=[p])
            sin_layer(p, 0, h1[:], h1)
            p = cx.psum()
            k.mm(p[:64, :TW], w2_sb[:], h1[:], True, True, reads=[w2_sb, h1], writes=[p])
            nh = TW // 128
            h0 = t0 // 128
            sin_layer(p, 1, h2[d][:, :, h0:h0 + nh].rearrange("p l h -> p h l"), h2[d],
                      in_view=lambda y: y[:].rearrange("p (h l) -> p h l", l=128))
    k.pop()
    S1f = k.sb("S1f", [128, NK, 3, CB], BF16)
    S1v = k.sb("S1v", [128, NK, 3, CB], BF16)
    Y = k.sb("Y", [128, 2, CB, NK], BF16)
    FZ = k.sb("FZ", [128, 2 * 128 * CB], BF16)
    ftv = FZ[:NHI, :].rearrange("p (d c l) -> p d c l", d=2, c=CB)
    Zv = FZ[:NK, :].rearrange("p (r a c) -> p r a c", r=2, a=128)
    tin = k.sb("tin", [NHI, CB, 128], BF16)
    tx1 = k.sb("tx1", [NHI, CB, 128], BF16)
    tv1 = k.sb("tv1", [NHI, CB, 128], BF16)
    x2b = k.sb("x2b", [CB, NHI, 128], BF16)
    yhb = k.sb("yhb", [CB, NHI, 128], BF16)
    GN = min(8, 512 // CB)
    dec2 = [k.sb("dec", [NHI, GN, CB], F32) for _ in range(2)]
    hd2 = [k.sb("hd", [NHI, GN, CB], F32) for _ in range(2)]
    acc = k.sb("acc", [NHI, GN, CB], F32)
    hab2 = [k.sb("hab", [NHI, GN, CB], F32) for _ in range(2)]
    rn = k.sb("rn", [128, CB], F32)
    rn8 = k.sb("rn8", [128, 8, CB], F32)
    KG = min(4, NK, 512 // (2 * CB))
    e2c2 = [k.sb("e2c", [128, KG, 128], BF16) for _ in range(2)]
    e2s2 = [k.sb("e2s", [128, KG, 128], BF16) for _ in range(2)]
    Hg2 = [k.sb("Hg", [128, KG, 2, CB], F32) for _ in range(2)]
    m4 = [k.sb("m4", [128, KG, CB], F32) for _ in range(4)]
    AG = 512 // CB
    AG2 = min(512 // NHI, 128) if NHI <= 512 else 1
    AG2 = min(AG2, 8)
    eic2 = [k.sb("eic", [NK, AG, NHI], BF16) for _ in range(2)]
    eis2 = [k.sb("eis", [NK, AG, NHI], BF16) for _ in range(2)]
    zTv = zT.h
    cnt = [0]

    def alt():
        cnt[0] += 1
        return "dve" if cnt[0] % 2 == 0 else "act"

    def stage1(dst, srcs):
        for c in range(CB):
            p = cx.psum()
            for i, (vf, tab, deps) in enumerate(srcs):
                k.mm(p[:, :3 * NK], vf(c), tab[:], i == 0, i == len(srcs) - 1, reads=deps + [tab], writes=[p])
            k.copy(alt(), dst[:, :, :, c], p[:, :3 * NK].rearrange("p (r q) -> p q r", r=3), reads=[p], writes=[dst])

    for cb in range(NB):
        c0 = cb * CB
        k.dma("sp", tin[:], zTv[2, c0:c0 + CB, :].rearrange("c (h l) -> h c l", l=128), writes=[tin])
        k.dma("sp", tx1[:], zTv[0, c0:c0 + CB, :].rearrange("c (h l) -> h c l", l=128), writes=[tx1])
        k.dma("sp", x2b[:], zTv[1, c0:c0 + CB, :].rearrange("c (h l) -> c h l", l=128), writes=[x2b])
        for o in range(2):
            sig = tin if o == 0 else tv1
            k.memset("dve", acc[:], 0.0, [acc])
            gi = 0
            for d, dsrc in enumerate((decf, decb)):
                for l0 in range(0, 128, GN):
                    dec, hd = dec2[gi % 2], hd2[gi % 2]
                    gi += 1
                    k.dma("sp", dec[:], dsrc[:, l0:l0 + GN, c0:c0 + CB], writes=[dec])
                    p = cx.psum()
                    for j in range(GN):
                        k.mm(p[:NHI, j * CB:(j + 1) * CB], h2[d][:, l0 + j, :], w3_sb[:, d, o, c0:c0 + CB], True, True,
                             reads=[h2[d], w3_sb], writes=[p])
                    k.tt("dve", hd[:], p[:NHI, :GN * CB].rearrange("p (l c) -> p l c", l=GN), dec[:], ALU.mult,
                         reads=[p, dec], writes=[hd])
                    k.copy("pool", ftv[:, d, :, l0:l0 + GN].rearrange("p c l -> p l c"), hd[:], reads=[hd], writes=[FZ])
                    hab = hab2[gi % 2]
                    k.act(hab[:], hd[:], AF.Abs, reads=[hd], writes=[hab])
                    k.tt("pool", acc[:], acc[:], hab[:], ALU.add, reads=[hab, acc], writes=[acc])
            k.memset("pool", ftv[0:1, 1, :, 0:1], 0.0, [FZ])
            p = cx.psum()
            k.mm(p[:, :GN * CB], onesf[:NHI, :], acc[:].rearrange("p l c -> p (l c)"), True, True,
                 reads=[onesf, acc], writes=[p])
            k.op("dve", lambda e: e.tensor_reduce(out=rn[:], in_=p[:, :GN * CB].rearrange("p (l c) -> p c l", l=GN),
                                                  axis=AX.X, op=ALU.add), reads=[p], writes=[rn])
            k.op("dve", lambda e: e.reciprocal(out=rn[:], in_=rn[:]), reads=[rn], writes=[rn])
            for j in range(8):
                k.copy("pool", rn8[:, j, :], rn[:], reads=[rn], writes=[rn8])
            stage1(S1f, [(lambda c: ftv[:, 0, c, :], t1f_sb, [FZ]), (lambda c: ftv[:, 1, c, :], t1b_sb, [FZ])])
            stage1(S1v, [(lambda c: sig[:, c, :], t1f_sb, [sig])])
            for gq, q0 in enumerate(range(0, NK, KG)):
                ecg, esg = e2c2[gq % 2], e2s2[gq % 2]
                k.dma("sp", ecg[:], e2c[:, q0:q0 + KG, :], writes=[ecg])
                k.dma("sp", esg[:], e2s[:, q0:q0 + KG, :], writes=[esg])
                ph = cx.psum()
                px = cx.psum()
                for (pp, S1) in ((ph, S1f), (px, S1v)):
                    for j in range(KG):
                        kl = q0 + j
                        k.mm(pp[:, j * 2 * CB:(j + 1) * 2 * CB], ecg[:, j, :],
                             S1[:, kl, 0:2, :].rearrange("p r c -> p (r c)"), True, False, reads=[ecg, S1], writes=[pp])
                        k.mm(pp[:, j * 2 * CB:(j + 1) * 2 * CB], esg[:, j, :],
                             S1[:, kl, 1:3, :].rearrange("p r c -> p (r c)"), False, True, reads=[esg, S1], writes=[pp])
                Hg = Hg2[gq % 2]
                k.tt("dve", Hg[:].rearrange("p q r c -> p (q r) c"),
                     ph[:, :KG * 2 * CB].rearrange("p (q c) -> p q c", c=CB), rn8[:, :KG * 2, :], ALU.mult,
                     reads=[ph, rn8], writes=[Hg])
                for j in range(KG):
                    k.tt("pool", Hg[:, j, 0, :], Hg[:, j, 0, :], skip_sb[:, o, c0:c0 + CB], ALU.add,
                         reads=[Hg, skip_sb], writes=[Hg])
                pxv = px[:, :KG * 2 * CB].rearrange("p (q r c) -> p q r c", q=KG, r=2)
                k.tt("dve", m4[0][:], pxv[:, :, 0, :], Hg[:, :, 0, :], ALU.mult, reads=[px, Hg], writes=[m4[0]])
                k.tt("dve", m4[1][:], pxv[:, :, 1, :], Hg[:, :, 1, :], ALU.mult, reads=[px, Hg], writes=[m4[1]])
                k.tt("dve", m4[2][:], pxv[:, :, 0, :], Hg[:, :, 1, :], ALU.mult, reads=[px, Hg], writes=[m4[2]])
                k.tt("dve", m4[3][:], pxv[:, :, 1, :], Hg[:, :, 0, :], ALU.mult, reads=[px, Hg], writes=[m4[3]])
                k.tt("pool", Y[:, 0, :, q0:q0 + KG].rearrange("p c q -> p q c"), m4[0][:], m4[1][:], ALU.subtract,
                     reads=[m4[0], m4[1]], writes=[Y])
                k.tt("pool", Y[:, 1, :, q0:q0 + KG].rearrange("p c q -> p q c"), m4[2][:], m4[3][:], ALU.add,
                     reads=[m4[2], m4[3]], writes=[Y])
            for c in range(0, CB, 2):
                p = cx.psum()
                for j in range(2):
                    k.mm(p[:NK, j * 256:(j + 1) * 256], Y[:, 0, c + j, :], tir_sb[:], True, False, reads=[Y, tir_sb], writes=[p])
                    k.mm(p[:NK, j * 256:(j + 1) * 256], Y[:, 1, c + j, :], tii_sb[:], False, True, reads=[Y, tii_sb], writes=[p])
                k.copy(alt(), Zv[:, :, :, c:c + 2].rearrange("p r a c -> p c r a"),
                       p[:NK, :512].rearrange("p (c r a) -> p c r a", c=2, r=2), reads=[p], writes=[FZ])
            if o == 0:
                for ga, a0 in enumerate(range(0, 128, AG)):
                    ec_, es_ = eic2[ga % 2], eis2[ga % 2]
                    k.dma("sp", ec_[:], eic[:, a0:a0 + AG, :], writes=[ec_])
                    k.dma("sp", es_[:], eis[:, a0:a0 + AG, :], writes=[es_])
                    p = cx.psum()
                    for j in range(AG):
                        k.mm(p[:NHI, j * CB:(j + 1) * CB], ec_[:, j, :], Zv[:, 0, a0 + j, :], True, False, reads=[ec_, FZ], writes=[p])
                        k.mm(p[:NHI, j * CB:(j + 1) * CB], es_[:, j, :], Zv[:, 1, a0 + j, :], False, True, reads=[es_, FZ], writes=[p])
                    k.tt("dve", tv1[:, :, a0:a0 + AG].rearrange("p c a -> p a c"),
                         p[:NHI, :AG * CB].rearrange("p (a c) -> p a c", a=AG),
                         tx1[:, :, a0:a0 + AG].rearrange("p c a -> p a c"), ALU.mult, reads=[p, tx1], writes=[tv1])
            else:
                per = max(1, min(AG, 512 // NHI))
                for ga, a0 in enumerate(range(0, 128, AG)):
                    ec_, es_ = eic2[ga % 2], eis2[ga % 2]
                    k.dma("sp", ec_[:], eic[:, a0:a0 + AG, :], writes=[ec_])
                    k.dma("sp", es_[:], eis[:, a0:a0 + AG, :], writes=[es_])
                    for a1 in range(0, AG, per):
                        p = cx.psum()
                        for j in range(per):
                            a = a0 + a1 + j
                            k.mm(p[:CB, j * NHI:(j + 1) * NHI], Zv[:, 0, a, :], ec_[:, a1 + j, :], True, False, reads=[ec_, FZ], writes=[p])
                            k.mm(p[:CB, j * NHI:(j + 1) * NHI], Zv[:, 1, a, :], es_[:, a1 + j, :], False, True, reads=[es_, FZ], writes=[p])
                        aa = a0 + a1
                        k.tt("dve", yhb[:, :, aa:aa + per].rearrange("p b a -> p a b"),
                             p[:CB, :per * NHI].rearrange("p (a b) -> p a b", a=per),
                             x2b[:, :, aa:aa + per].rearrange("p b a -> p a b"), ALU.mult, reads=[p, x2b], writes=[yhb])
                k.dma("sp", yhT[c0:c0 + CB, :], yhb[:].rearrange("p b a -> p (b a)"), reads=[yhb], writes=[yhT])
    return k.finish()


def tables_mix2(L, g):
    NHI = L // 128
    NK = 2 * NHI
    N = 2 * L
    f32 = lambda a: np.ascontiguousarray(a, dtype=np.float32)
    bf = lambda a: np.ascontiguousarray(np.asarray(a, dtype=np.float32).astype(NPBF))

    def ztab(t):
        t = t.astype(np.float64)
        t01 = t / (L - 1)
        bands = np.linspace(1e-4, 15.0, 16)
        ang = (2 * np.pi / L) * t[:, None] * bands[None, :]
        return np.concatenate([t01[:, None], np.cos(ang), -np.sin(ang)], axis=1).T

    tf = np.arange(L)
    tb = L - np.arange(L)
    tb[0] = 0
    max_decay = np.log(1e-2) / 0.3
    min_decay = np.log(1e-2) / 1.5
    delta = np.abs(np.linspace(min_decay, max_decay, 512))[g * 128:(g + 1) * 128]

    def dtab(t):
        t01 = t.astype(np.float64) / (L - 1)
        return np.exp(-t01[:, None] * delta[None, :]).reshape(NHI, 128, 128)

    nh = np.arange(NHI)[:, None]
    kl = np.arange(NK)[None, :]
    a = 2 * np.pi * (nh * kl % NK) / NK
    t1f = np.concatenate([np.cos(a), -np.sin(a), -np.cos(a)], 1)
    a = 2 * np.pi * ((nh + NHI) * kl % NK) / NK
    t1b = np.concatenate([np.cos(a), -np.sin(a), -np.cos(a)], 1)
    nlo = np.arange(128)[:, None, None]
    klo = np.arange(NK)[None, :, None]
    khi = np.arange(128)[None, None, :]
    a = 2 * np.pi * ((nlo * (klo + NK * khi)) % N) / N
    e2c, e2s = np.cos(a), np.sin(a)
    kk = np.arange(128)
    a = 2 * np.pi * np.outer(kk, kk) / 128.0
    tir = np.concatenate([np.cos(a), np.sin(a)], 1)
    tii = np.concatenate([-np.sin(a), np.cos(a)], 1)
    klo = np.arange(NK)[:, None, None]
    na = np.arange(128)[None, :, None]
    nb = np.arange(NHI)[None, None, :]
    a = 2 * np.pi * ((klo * (na + 128 * nb)) % N) / N
    eic, eis = np.cos(a) / N, -np.sin(a) / N
    return dict(zf=f32(ztab(tf)), zb=f32(ztab(tb)), decf=f32(dtab(tf)), decb=f32(dtab(tb)),
                t1f=bf(t1f), t1b=bf(t1b), e2c=bf(e2c), e2s=bf(e2s), tir=bf(tir), tii=bf(tii), eic=bf(eic), eis=bf(eis))


def run_mix2(L, zT_list, I):
    maps = []
    for i in range(NCORE):
        g = i % 4
        tabs = _get("tab2", tables_mix2, L, g)
        w3 = I["hy_f_w3"][0].reshape(64, 2, 2, 512)[:, :, :, g * 128:(g + 1) * 128]
        skipb = np.broadcast_to(I["hy_bias"][0][None, :, g * 128:(g + 1) * 128], (128, 2, 128))
        fb = np.stack([I["hy_f_freq"][0], I["hy_f_b1"][0], I["hy_f_b2"][0]], axis=1)
        m = dict(zT=np.ascontiguousarray(zT_list[i]), w1=np.ascontiguousarray(I["hy_f_w1"][0]),
                 w2=np.ascontiguousarray(I["hy_f_w2"][0]), w3=np.ascontiguousarray(w3, dtype=np.float32),
                 fb=np.ascontiguousarray(fb, dtype=np.float32), skipb=np.ascontiguousarray(skipb, dtype=np.float32))
        m.update(tabs)
        maps.append(m)
    res = _launch(_get("mix2", build_mix2, L), maps)
    return [r["yhT"] for r in res]


FLEX_ROWS = [0, 1, 2, 3, 28, 29, 30, 31]
NTAB = 512 + 8 * 768
NEG = -30000.0


def build_attn(nj=8, lrs=None, nbuf=2, ubar=0):
    k = KB()
    lrs = list(range(32)) if lrs is None else list(lrs)
    NT = 2560
    hT = k.dram("hT", [D, NT], F32, "ExternalInput")
    hcT = k.dram("hcT", [D, 256], F32, "ExternalInput")
    modT = k.dram("modT", [128, 2, 6, 8], F32, "ExternalInput")
    n1g = k.dram("n1g", [128, 8], F32, "ExternalInput")
    w_qkv = k.dram("w_qkv", [D, 3 * D], F32, "ExternalInput")
    BT = k.dram("BT", [16, 64, NTAB], F32, "ExternalInput")
    MK = k.dram("MK", [64, NTAB], F32, "ExternalInput")
    ident = k.dram("ident", [64, 64], F32, "ExternalInput")
    attnT = k.dram("attnT", [2048, D], BF16, "ExternalOutput")
    cx = Ctx(k, npsum=3)
    sc2 = [k.ps("sc", [128, 1024], F32) for _ in range(2)]
    ptp = k.ps("pt", [128, 1024], BF16)
    mod_sb = k.sb("mod", [128, 2, 6, 8], F32)
    n1g_sb = k.sb("n1g", [128, 8], F32)
    idf = k.sb("idf", [64, 64], F32)
    idb = k.sb("idb", [64, 64], BF16)
    onesf = k.sb("onesf", [64, 64], F32)
    k.memset("dve", onesf[:], 1.0, [onesf])
    k.dma("sp", mod_sb[:], modT[:], writes=[mod_sb])
    k.dma("sp", n1g_sb[:], n1g[:], writes=[n1g_sb])
    k.dma("sp", idf[:], ident[:], writes=[idf])
    k.dma("pool", idb[:], ident[:], writes=[idb])
    gm = k.sb("gm", [128, 2, 8], F32)
    for s in range(2):
        k.ts("dve", gm[:, s, :], mod_sb[:, s, 1, :], 1.0, None, ALU.add, None, reads=[mod_sb], writes=[gm])
        k.tt("dve", gm[:, s, :], gm[:, s, :], n1g_sb[:], ALU.mult, reads=[gm, n1g_sb], writes=[gm])
    mk_sb = k.sb("mk", [64, NTAB], BF16)
    for qq in range(4):
        k.dma("pool", mk_sb[:, qq * (NTAB // 4):(qq + 1) * (NTAB // 4)], MK[:, qq * (NTAB // 4):(qq + 1) * (NTAB // 4)],
              writes=[mk_sb])
    uT = k.sb("uT", [128, 8, NT + 256], BF16)
    ud = [Dep() for _ in range(8)]
    k.push()
    x2 = [k.sb("x", [128, 8, 512], F32) for _ in range(2)]
    xd = [[Dep() for _ in range(8)] for _ in range(2)]
    rstd = k.sb("rstd", [128, 512], F32)
    sq2 = [k.sb("sq", [128, 512], BF16) for _ in range(2)]
    tmp2 = [k.sb("tmp", [128, 512], F32) for _ in range(2)]
    hTc = chunked(hT.h)
    hcTc = chunked(hcT.h)
    tiles = [(hTc, t * 512, 512, 0, t * 512) for t in range(5)] + [(hcTc, 0, 256, 1, NT)]
    for ti, (src, s0, n, seg, d0) in enumerate(tiles):
        xs = x2[ti % 2]
        for c in range(8):
            k.dma("sp", xs[:, c, :n], src[:, c, s0:s0 + n], writes=[xd[ti % 2][c]])
        norm_mod(k, cx, lambda c: xs[:, c, :n], xd[ti % 2], n, lambda c: gm[:, seg, c:c + 1],
                 lambda c: mod_sb[:, seg, 0, c:c + 1], lambda c: uT[:, c, d0:d0 + n], ud, rstd, sq2, tmp2,
                 moddeps=[gm, mod_sb])
    k.pop()
    w2 = [k.sb("w", [128, 8, 384], BF16) for _ in range(2)]
    qT2 = [k.sb("qT", [64, 2048], BF16) for _ in range(2)]
    kT2 = [k.sb("kT", [64, NT + 256], BF16) for _ in range(2)]
    V = k.sb("V", [64, 44, 2, 65], BF16)
    k.memset("dve", V[:], 1.0, [V])
    tab2 = [k.sb("tab", [64, NTAB], BF16) for _ in range(2)]
    P2 = [k.sb("P", [64, 1024], BF16) for _ in range(2)]
    PT2 = [k.sb("PT", [64, 1024], BF16) for _ in range(2)]
    at2 = [k.sb("at", [64, 32, 64], BF16) for _ in range(2)]
    nmx2 = [k.sb("nmx", [64, 1], F32) for _ in range(2)]
    rs2 = [k.sb("rs", [64, 1], F32) for _ in range(2)]
    dg2 = [k.sb("dg", [64, 64], F32) for _ in range(2)]
    rc2 = [k.sb("rc", [64, 64], F32) for _ in range(2)]
    wv = w_qkv.h.rearrange("(c p) f -> p c f", p=128)
    un = 0
    for j in range(nj):
        w = w2[j % 2]
        for part in range(3):
            k.dma("pool", w[:, :, part * 128:(part + 1) * 128], wv[:, :, part * D + j * 128: part * D + (j + 1) * 128],
                  writes=[w])
        for hh_ in range(2):
            for t in range(4):
                p = cx.psum()
                for kc in range(8):
                    k.mm(p[:64, :512], w[:, kc, hh_ * 64:(hh_ + 1) * 64], uT[:, kc, 256 + t * 512:256 + (t + 1) * 512],
                         kc == 0, kc == 7, reads=[w, ud[kc]], writes=[p])
                k.act(qT2[hh_][:, t * 512:(t + 1) * 512], p[:64, :512], AF.Copy, reads=[p], writes=[qT2[hh_]], scale=0.125)
            for t in range(6):
                n = 512 if t < 5 else 256
                p = cx.psum()
                for kc in range(8):
                    k.mm(p[:64, :n], w[:, kc, 128 + hh_ * 64:128 + (hh_ + 1) * 64], uT[:, kc, t * 512:t * 512 + n],
                         kc == 0, kc == 7, reads=[w, ud[kc]], writes=[p])
                k.copy("act", kT2[hh_][:, t * 512:t * 512 + n], p[:64, :n], reads=[p], writes=[kT2[hh_]])
        for r0 in range(0, 44, 4):
            p = cx.psum()
            for rr in range(4):
                for kc in range(8):
                    k.mm(p[:64, rr * 128:(rr + 1) * 128], uT[:, kc, (r0 + rr) * 64:(r0 + rr + 1) * 64], w[:, kc, 256:384],
                         kc == 0, kc == 7, reads=[w, ud[kc]], writes=[p])
            for hh_ in range(2):
                k.copy("dve", V[:, r0:r0 + 4, hh_, 0:64],
                       p[:64, :512].rearrange("p (r h c) -> p r h c", r=4, h=2)[:, :, hh_, :], reads=[p], writes=[V])
        units = []
        for hh in range(2):
            for lr in lrs:
                flex = lr in FLEX_ROWS
                if flex:
                    W0, nW, tc0 = (0 if lr < 4 else 28), 12, 512 + FLEX_ROWS.index(lr) * 768
                else:
                    W0, nW, tc0 = lr, 8, 0
                units.append(dict(hh=hh, lr=lr, h=2 * j + hh, flex=flex, W0=W0, nW=nW, tc0=tc0,
                                  ncol=nW * 64 + 256, first=(lr == lrs[0]), last=(lr == lrs[-1])))

        def stage_A(U, ub):
            hh, lr, h = U["hh"], U["lr"], U["h"]
            qT, kT = qT2[hh], kT2[hh]
            tab = tab2[h % 2]
            if U["first"]:
                for qq in range(4):
                    k.dma("pool", tab[:, qq * (NTAB // 4):(qq + 1) * (NTAB // 4)],
                          BT[h, :, qq * (NTAB // 4):(qq + 1) * (NTAB // 4)], writes=[tab])
                k.tt("pool", tab[:], tab[:], mk_sb[:], ALU.add, reads=[tab, mk_sb], writes=[tab])
                if len(lrs) < 32:
                    k.memset("pool", at2[h % 2][:], 0.0, [at2[h % 2]])
            W0, tc0, ncol = U["W0"], U["tc0"], U["ncol"]
            sc, P, nmx = sc2[ub], P2[ub], nmx2[ub]
            qv = qT[:, lr * 64:(lr + 1) * 64]
            k.mm(sc[:64, 0:512], qv, kT[:, W0 * 64:W0 * 64 + 512], True, False, reads=[qT, kT], writes=[sc])
            k.mm(sc[:64, 0:512], idb[:], tab[:, tc0:tc0 + 512], False, True, reads=[idb, tab], writes=[sc])
            c1 = 512
            if U["flex"]:
                k.mm(sc[:64, 512:768], qv, kT[:, (W0 + 8) * 64:(W0 + 12) * 64], True, False, reads=[qT, kT], writes=[sc])
                k.mm(sc[:64, 512:768], idb[:], tab[:, tc0 + 512:tc0 + 768], False, True, reads=[idb, tab], writes=[sc])
                c1 = 768
            k.mm(sc[:64, c1:c1 + 256], qv, kT[:, NT:NT + 256], True, True, reads=[qT, kT], writes=[sc])
            k.op("dve", lambda e: e.tensor_reduce(out=nmx[:], in_=sc[:64, :ncol], axis=AX.X, op=ALU.max, negate=True),
                 reads=[sc], writes=[nmx])
            k.act(P[:, :ncol], sc[:64, :ncol], AF.Exp, reads=[sc, nmx], writes=[P], bias=nmx[:, 0:1], scale=1.0)

        def stage_B(U, ub):
            ncol = U["ncol"]
            P, PT = P2[ub], PT2[ub]
            for ch in range(ncol // 64):
                k.op("pe", lambda e: e.transpose(ptp[:64, ch * 64:(ch + 1) * 64], P[:, ch * 64:(ch + 1) * 64], idb[:]),
                     reads=[P, idb], writes=[ptp])
            k.copy("act", PT[:, :ncol], ptp[:64, :ncol], reads=[ptp], writes=[PT])

        def stage_C(U, ub):
            hh, lr, h, W0, nW = U["hh"], U["lr"], U["h"], U["W0"], U["nW"]
            PT, rc = PT2[ub], rc2[ub]
            at = at2[h % 2]
            nch = U["ncol"] // 64
            po = cx.psum()
            for ch in range(nch):
                vrow = (W0 + ch) if ch < nW else (40 + ch - nW)
                k.mm(po[:64, 0:65], PT[:, ch * 64:(ch + 1) * 64], V[:, vrow, hh, :], ch == 0, ch == nch - 1,
                     reads=[V, PT], writes=[po])
            k.op("dve", lambda e: e.reciprocal(out=rc[:, 0:1], in_=po[:64, 64:65]), reads=[po], writes=[rc])
            k.ts("dve", at[:, lr, :], po[:64, 0:64], rc[:, 0:1], None, ALU.mult, None, reads=[po, rc], writes=[at])
            if U["last"]:
                k.dma("sp", attnT[:, h * 64:(h + 1) * 64].rearrange("(r q) d -> q r d", q=64), at[:],
                      reads=[at], writes=[attnT])

        nu = len(units)
        for st in range(nu + 2):
            if st < nu:
                stage_A(units[st], st % 2)
            if 0 <= st - 1 < nu:
                stage_B(units[st - 1], (st - 1) % 2)
            if 0 <= st - 2 < nu:
                stage_C(units[st - 2], (st - 2) % 2)
    return k.finish()


def attn_tables(rpb, c):
    q = np.arange(64)[:, None]
    col = np.arange(64)[None, :]
    dc = np.clip(col - q + 15, 0, 30)
    wstart = np.clip(q - 8, 0, 48)
    colok = (col >= wstart) & (col < wstart + 16)
    BT = np.zeros((16, 64, NTAB), np.float32)
    MK = np.full((64, NTAB), NEG, np.float32)

    def fill(c0, drs):
        for i, dr in enumerate(drs):
            if dr is None:
                continue
            sl = slice(c0 + i * 64, c0 + (i + 1) * 64)
            BT[:, :, sl] = np.where(colok[None], rpb[:, dr][:, dc], 0.0)
            MK[:, sl] = np.where(colok, 0.0, NEG)

    fill(0, [i + 3 for i in range(8)])
    for f, lr in enumerate(FLEX_ROWS):
        r = 32 * c + lr
        rs = min(max(r - 4, 0), 120)
        base = (32 * c - 4) if lr < 4 else (32 * c + 24)
        drs = []
        for i in range(12):
            rho = base + i
            drs.append(rho - r + 7 if rs <= rho < rs + 8 else None)
        fill(512 + f * 768, drs)
    return BT, MK


def run_attn(h_lat0, h_ctx0, mod, I, ncore=NCORE, **dbg):
    ident = np.eye(64, dtype=np.float32)
    n1g = fm_chunks(I["norm1_g"][1]).astype(np.float32)
    maps = []
    for i in range(ncore):
        b, c = i // 4, i % 4
        hh = np.zeros((40 * 64, D), np.float32)
        r0 = 32 * c - 4
        lo, hi = max(r0, 0), min(r0 + 40, 128)
        hh[(lo - r0) * 64:(hi - r0) * 64] = h_lat0[b, lo * 64:hi * 64]
        modT = np.stack([fm_chunks(mod[1, cnd].reshape(6, 1024)) for cnd in (b, 2)], axis=1)
        BT, MK = _get("attntab%d" % id(I), attn_tables, I["na_rpb"][0], c) if False else attn_tables(I["na_rpb"][0], c)
        maps.append(dict(hT=np.ascontiguousarray(hh.T), hcT=np.ascontiguousarray(h_ctx0[b].T, dtype=np.float32),
                         modT=np.ascontiguousarray(modT, dtype=np.float32), n1g=n1g,
                         w_qkv=np.ascontiguousarray(I["na_w_qkv"][0]), BT=BT, MK=MK, ident=ident))
    res = _launch(_get("attn", build_attn, *dbg.values()), maps)
    return [r["attnT"] for r in res]


def kernel(**inputs):
    I = {k_: np.asarray(v) for k_, v in inputs.items()}
    x = I["x"].astype(np.float32, copy=False)
    ctx = I["ctx"].astype(np.float32, copy=False)
    mod = run_ada(I)
    xT = [np.ascontiguousarray(x[i // 4].T) for i in range(NCORE)]
    yf, z = run_mix1(8192, xT, [mod[0, i // 4] for i in range(NCORE)], I)
    yh = run_mix2(8192, z, I)
    cT = [np.ascontiguousarray(ctx[i // 4].T) for i in range(NCORE)]
    yfc, zc = run_mix1(256, cT, [mod[0, 2] for _ in range(NCORE)], I)
    yhc = run_mix2(256, zc, I)
    yT = [np.concatenate([yf[b * 4 + g] for g in range(4)] + [yh[b * 4 + g] for g in range(4)], axis=0) for b in range(2)]
    yTc = [np.concatenate([yfc[b * 4 + g] for g in range(4)] + [yhc[b * 4 + g] for g in range(4)], axis=0) for b in range(2)]
    hT_l, yT_l = [], []
    for i in range(NCORE):
        b, j = i // 4, i % 4
        hT_l.append(np.concatenate([xT[i][:, 2048 * j:2048 * (j + 1)], cT[i][:, 64 * j:64 * (j + 1)]], axis=1))
        yT_l.append(np.concatenate([yT[b][:, 2048 * j:2048 * (j + 1)], yTc[b][:, 64 * j:64 * (j + 1)]], axis=1))
    o0 = run_tail(0, hT_l, yT_l, mod, I, False)
    h_lat0 = np.empty((2, 8192, D), np.float32)
    h_ctx0 = np.empty((2, 256, D), np.float32)
    for i in range(NCORE):
        b, j = i // 4, i % 4
        h_lat0[b, 2048 * j:2048 * (j + 1)] = o0[i][:, :2048].T
        h_ctx0[b, 64 * j:64 * (j + 1)] = o0[i][:, 2048:].T
    at = run_attn(h_lat0, h_ctx0, mod, I)
    hT_l = [np.ascontiguousarray(h_lat0[i // 4, 2048 * (i % 4):2048 * (i % 4 + 1)].T) for i in range(NCORE)]
    atT = [np.ascontiguousarray(a.T) for a in at]
    o1 = run_tail(1, hT_l, atT, mod, I, True)
    out = np.empty((2, 8192, D), np.float32)
    for i in range(NCORE):
        b, j = i // 4, i % 4
        out[b, 2048 * j:2048 * (j + 1)] = o1[i].T
    return out
```

```python
import numpy as np
from contextlib import ExitStack
import ml_dtypes
import concourse.bass as bass
import concourse.mybir as mybir
from concourse.bass_utils import run_bass_kernel_spmd

F32 = mybir.dt.float32
BF16 = mybir.dt.bfloat16
AF = mybir.ActivationFunctionType
ALU = mybir.AluOpType
AX = mybir.AxisListType
NPBF = ml_dtypes.bfloat16

D = 1024
DFF = 2816
NCORE = 8
EPS = 1e-6
SAME_ENGINE_SYNC = True


class Dep:
    __slots__ = ("w", "r")

    def __init__(self):
        self.w = []
        self.r = []


class Tl:
    def __init__(self, h):
        self.h = h
        self.dep = Dep()

    def __getitem__(self, idx):
        return self.h[idx]


def _dep(x):
    return x.dep if isinstance(x, Tl) else x


class KB:
    def __init__(self):
        self.nc = bass.Bass("TRN2", target_bir_lowering=False)
        nc = self.nc
        self.es = ExitStack()
        self.scopes = [self.es]
        self.eng = dict(pe=nc.tensor, act=nc.scalar, dve=nc.vector, pool=nc.gpsimd, sp=nc.sync)
        self.sems = {}
        self.epoch = {e: 0 for e in self.eng}
        for e in self.eng:
            self.sems[(e, 0)] = self.es.enter_context(nc.semaphore("sem_" + e))
        self.cnt = {e: 0 for e in self.eng}
        self.seen = {e: {} for e in self.eng}
        self.dq = {}
        for q, n in (("sp", 20), ("pool", 20)):
            keys = []
            for i in range(n):
                key = ("d", q, i)
                self.sems[key] = self.es.enter_context(nc.semaphore("dsem_%s_%d" % (q, i)))
                keys.append(key)
            self.dq[q] = dict(keys=keys, i=0, cnt=[0] * n)
        self.uid = 0

    def name(self, base):
        self.uid += 1
        return "%s_%d" % (base, self.uid)

    def sb(self, name, shape, dt):
        return Tl(self.scopes[-1].enter_context(self.nc.sbuf_tensor(self.name(name), list(shape), dt)))

    def ps(self, name, shape, dt=F32):
        return Tl(self.scopes[-1].enter_context(self.nc.psum_tensor(self.name(name), list(shape), dt)))

    def dram(self, name, shape, dt, kind):
        return Tl(self.nc.dram_tensor(name, list(shape), dt, kind=kind).ap())

    def push(self):
        es = ExitStack()
        self.scopes.append(es)

    def pop(self):
        self.barrier()
        es = self.scopes.pop()
        es.close()

    def _wait(self, e, key, val):
        if self.seen[e].get(key, 0) >= val:
            return
        self.eng[e].wait_ge(self.sems[key], val)
        self.seen[e][key] = val

    def _collect(self, e, reads, writes):
        evs = []
        for d in reads:
            evs += _dep(d).w
        for d in writes:
            d = _dep(d)
            evs += d.w
            evs += d.r
        for (key, val, src) in evs:
            if src == e and (e == "pe" or not SAME_ENGINE_SYNC):
                continue
            self._wait(e, key, val)

    def _record(self, ev, reads, writes):
        for d in writes:
            d = _dep(d)
            d.w = [ev]
            d.r = []
        for d in reads:
            d = _dep(d)
            d.r = [x for x in d.r if x[0] != ev[0]] + [ev]

    EPOCH = 3000

    def op(self, e, fn, reads=(), writes=(), inc=True):
        if self.cnt[e] >= self.EPOCH:
            self.epoch[e] += 1
            self.sems[(e, self.epoch[e])] = self.es.enter_context(
                self.nc.semaphore("sem_%s_%d" % (e, self.epoch[e])))
            self.cnt[e] = 0
        self._collect(e, reads, writes)
        ins = fn(self.eng[e])
        key = (e, self.epoch[e])
        if not inc:
            self._record((key, self.cnt[e] + 1, e), reads, writes)
            return ins
        self.cnt[e] += 1
        ins.then_inc(self.sems[key], 1)
        self._record((key, self.cnt[e], e), reads, writes)
        return ins

    def dma(self, q, out, in_, reads=(), writes=(), **kw):
        self._collect(q, reads, writes)
        d = self.dq[q]
        i = d["i"]
        d["i"] = (i + 1) % len(d["keys"])
        key = d["keys"][i]
        prev = d["cnt"][i]
        if prev > 0:
            self._wait(q, key, prev)
        ins = self.eng[q].dma_start(out=out, in_=in_, **kw)
        ins.then_inc(self.sems[key], 16)
        d["cnt"][i] = prev + 16
        self._record((key, prev + 16, "dma"), reads, writes)
        return ins

    def barrier(self):
        for e in self.eng:
            for e2 in self.eng:
                if e2 != e:
                    for ep in range(self.epoch[e2] + 1):
                        c = self.cnt[e2] if ep == self.epoch[e2] else self.EPOCH
                        if c > 0:
                            self._wait(e, (e2, ep), c)
            for q, d in self.dq.items():
                for key, c in zip(d["keys"], d["cnt"]):
                    if c > 0:
                        self._wait(e, key, c)

    def finish(self):
        self.barrier()
        self.es.close()
        return self.nc

    def mm(self, out, lhsT, rhs, start, stop, reads, writes, inc=None):
        return self.op("pe", lambda e: e.matmul(out, lhsT=lhsT, rhs=rhs, start=start, stop=stop),
                       reads=reads, writes=writes, inc=bool(stop) if inc is None else inc)

    def act(self, out, in_, func, reads, writes, eng="act", **kw):
        return self.op(eng, lambda e: e.activation(out=out, in_=in_, func=func, **kw), reads=reads, writes=writes)

    def tt(self, eng, out, in0, in1, op, reads, writes):
        return self.op(eng, lambda e: e.tensor_tensor(out=out, in0=in0, in1=in1, op=op), reads=reads, writes=writes)

    def ts(self, eng, out, in0, s1, s2, op0, op1, reads, writes):
        if s2 is None:
            return self.op(eng, lambda e: e.tensor_scalar(out=out, in0=in0, scalar1=s1, scalar2=None, op0=op0),
                           reads=reads, writes=writes)
        return self.op(eng, lambda e: e.tensor_scalar(out=out, in0=in0, scalar1=s1, scalar2=s2, op0=op0, op1=op1),
                       reads=reads, writes=writes)

    def stt(self, out, in0, scalar, in1, op0, op1, reads, writes):
        return self.op("dve", lambda e: e.scalar_tensor_tensor(out=out, in0=in0, scalar=scalar, in1=in1,
                                                                op0=op0, op1=op1), reads=reads, writes=writes)

    def copy(self, eng, out, in_, reads, writes):
        if eng == "act":
            return self.op(eng, lambda e: e.copy(out=out, in_=in_), reads=reads, writes=writes)
        return self.op(eng, lambda e: e.tensor_copy(out=out, in_=in_), reads=reads, writes=writes)

    def memset(self, eng, ap, val, writes):
        return self.op(eng, lambda e: e.memset(ap, val), reads=(), writes=writes)


def chunked(ap2d):
    return ap2d.rearrange("(c p) t -> p c t", p=128)


class Ctx:
    def __init__(self, k, npsum=8):
        self.k = k
        self.psb = [k.ps("psb", [128, 512], F32) for _ in range(npsum)]
        self.pi = 0
        self.ones = k.sb("ones", [128, 128], BF16)
        k.memset("dve", self.ones[:], 1.0, [self.ones])
        self.epsb = k.sb("epsb", [128, 1], F32)
        k.memset("dve", self.epsb[:], EPS, [self.epsb])

    def psum(self):
        p = self.psb[self.pi]
        self.pi = (self.pi + 1) % len(self.psb)
        return p


def rms_rstd(k, cx, x, xdeps, n, rstd, sq2):
    ps = cx.psum()
    for c in range(8):
        sq = sq2[c % 2]
        k.act(sq[:, :n], x(c), AF.Square, reads=[xdeps[c]], writes=[sq])
        k.mm(ps[:, :n], cx.ones[:], sq[:, :n], c == 0, c == 7, reads=[sq, cx.ones], writes=[ps], inc=True)
    k.act(rstd[:, :n], ps[:, :n], AF.Sqrt, reads=[ps, cx.epsb], writes=[rstd], bias=cx.epsb[:], scale=1.0 / D)
    k.op("dve", lambda e: e.reciprocal(out=rstd[:, :n], in_=rstd[:, :n]), reads=[rstd], writes=[rstd])


def norm_mod(k, cx, x, xdeps, n, gmod, shift, out, odeps, rstd, sq2, tmp2, moddeps=()):
    rms_rstd(k, cx, x, xdeps, n, rstd, sq2)
    for c in range(8):
        if shift is None:
            k.stt(out(c), x(c), gmod(c), rstd[:, :n], ALU.mult, ALU.mult,
                  reads=[xdeps[c], rstd] + list(moddeps), writes=[odeps[c]])
        else:
            tmp = tmp2[c % 2]
            k.stt(tmp[:, :n], x(c), gmod(c), rstd[:, :n], ALU.mult, ALU.mult,
                  reads=[xdeps[c], rstd] + list(moddeps), writes=[tmp])
            k.act(out(c), tmp[:, :n], AF.Identity, reads=[tmp] + list(moddeps), writes=[odeps[c]],
                  bias=shift(c), scale=1.0)


def load_w_bf16(k, dst, dst_dep, w_ap, kchunks, ncols, col0=0, q="pool"):
    src = w_ap.rearrange("(c p) f -> p c f", p=128)
    step = 8
    for c0 in range(0, kchunks, step):
        c1 = min(kchunks, c0 + step)
        k.dma(q, dst[:, c0:c1, :ncols], src[:, c0:c1, col0:col0 + ncols], reads=(), writes=[dst_dep])


def build_ada():
    k = KB()
    condT = k.dram("condT", [128, 8, 3], F32, "ExternalInput")
    w = k.dram("w", [2, 1024, 768], F32, "ExternalInput")
    b = k.dram("b", [2, 3, 768], F32, "ExternalInput")
    out = k.dram("out", [2, 3, 768], F32, "ExternalOutput")
    c_sb = k.sb("c", [128, 8, 3], F32)
    s_sb = k.sb("s", [128, 8, 3], F32)
    w_sb = k.sb("w", [128, 2, 8, 768], F32)
    b_sb = k.sb("b", [3, 2, 768], F32)
    o_sb = k.sb("o", [3, 2, 768], F32)
    ps = [k.ps("ps", [128, 512], F32) for _ in range(4)]
    k.dma("sp", c_sb[:], condT[:], writes=[c_sb])
    for l in range(2):
        k.dma("sp", w_sb[:, l, :, :], w[l].rearrange("(c p) f -> p c f", p=128), writes=[w_sb])
        k.dma("sp", b_sb[:, l, :], b[l], writes=[b_sb])
    k.act(s_sb[:], c_sb[:], AF.Silu, reads=[c_sb], writes=[s_sb])
    for l in range(2):
        for hf in range(2):
            p = ps[l * 2 + hf]
            for c in range(8):
                k.mm(p[:3, :384], s_sb[:, c, :], w_sb[:, l, c, hf * 384:(hf + 1) * 384], c == 0, c == 7,
                     reads=[s_sb, w_sb], writes=[p])
            k.tt("dve", o_sb[:, l, hf * 384:(hf + 1) * 384], p[:3, :384], b_sb[:, l, hf * 384:(hf + 1) * 384],
                 ALU.add, reads=[p, b_sb], writes=[o_sb])
    k.dma("sp", out.h.rearrange("l c f -> c l f"), o_sb[:], reads=[o_sb], writes=[out])
    return k.finish()


def build_tail(TT, segs, final):
    k = KB()
    hT = k.dram("hT", [D, TT], F32, "ExternalInput")
    yT = k.dram("yT", [D, TT], BF16, "ExternalInput")
    nseg = 1 + max(s[2] for g in segs for s in g)
    modT = k.dram("modT", [128, nseg, 6, 8], F32, "ExternalInput")
    ng = k.dram("ng", [128, 2, 8], F32, "ExternalInput")
    w_mo = k.dram("w_mo", [D, D], F32, "ExternalInput")
    w_g = k.dram("w_g", [D, DFF], F32, "ExternalInput")
    w_u = k.dram("w_u", [D, DFF], F32, "ExternalInput")
    w_d = k.dram("w_d", [DFF, D], F32, "ExternalInput")
    oT = k.dram("oT", [D, TT], F32, "ExternalOutput")
    cx = Ctx(k)
    GM = max(sum(s[1] for s in g) for g in segs)
    NF = DFF // 128
    wmo_sb = k.sb("wmo", [128, 8, D], BF16)
    wd_sb = k.sb("wd", [128, NF, D], BF16)
    load_w_bf16(k, wmo_sb, wmo_sb, w_mo.h, 8, D)
    load_w_bf16(k, wd_sb, wd_sb, w_d.h, NF, D)
    mod_sb = k.sb("mod", [128, nseg, 6, 8], F32)
    ng_sb = k.sb("ng", [128, 2, 8], F32)
    k.dma("sp", mod_sb[:], modT[:], writes=[mod_sb])
    k.dma("sp", ng_sb[:], ng[:], writes=[ng_sb])
    gm2 = k.sb("gm2", [128, nseg, 8], F32)
    for s in range(nseg):
        k.ts("dve", gm2[:, s, :], mod_sb[:, s, 4, :], 1.0, None, ALU.add, None, reads=[mod_sb], writes=[gm2])
        k.tt("dve", gm2[:, s, :], gm2[:, s, :], ng_sb[:, 0, :], ALU.mult, reads=[gm2, ng_sb], writes=[gm2])
    h_sb = k.sb("h", [128, 8, GM], F32)
    y_sb = k.sb("y", [128, 8, GM], BF16)
    u_sb = k.sb("u", [128, 8, GM], BF16)
    a_sb = k.sb("a", [128, NF, GM], BF16)
    hd = [Dep() for _ in range(8)]
    yd = [Dep() for _ in range(8)]
    ud = [Dep() for _ in range(8)]
    ad = [Dep() for _ in range(NF)]
    rstd = k.sb("rstd", [128, 512], F32)
    sq2 = [k.sb("sq", [128, 512], BF16) for _ in range(2)]
    tmp2 = [k.sb("tmp", [128, 512], F32) for _ in range(2)]
    sg2 = [k.sb("sg", [128, 512], BF16) for _ in range(2)]
    wg2 = [k.sb("wg", [128, 8, 128], BF16) for _ in range(2)]
    wu2 = [k.sb("wu", [128, 8, 128], BF16) for _ in range(2)]
    hTc = chunked(hT.h)
    yTc = chunked(yT.h)
    oTc = chunked(oT.h)
    wgv = w_g.h.rearrange("(c p) f -> p c f", p=128)
    wuv = w_u.h.rearrange("(c p) f -> p c f", p=128)
    for g in segs:
        g0 = g[0][0]
        G = sum(s[1] for s in g)
        for c in range(8):
            k.dma("sp", h_sb[:, c, :G], hTc[:, c, g0:g0 + G], writes=[hd[c]])
            k.dma("sp", y_sb[:, c, :G], yTc[:, c, g0:g0 + G], writes=[yd[c]])
        for (s0, n, sg) in g:
            o = s0 - g0
            for c in range(8):
                p = cx.psum()
                for kc in range(8):
                    k.mm(p[:, :n], wmo_sb[:, kc, c * 128:(c + 1) * 128], y_sb[:, kc, o:o + n], kc == 0, kc == 7,
                         reads=[wmo_sb, yd[kc]], writes=[p])
                k.stt(h_sb[:, c, o:o + n], p[:, :n], mod_sb[:, sg, 2, c:c + 1], h_sb[:, c, o:o + n],
                      ALU.mult, ALU.add, reads=[p, mod_sb, hd[c]], writes=[hd[c]])
        for (s0, n, sg) in g:
            o = s0 - g0
            norm_mod(k, cx, lambda c: h_sb[:, c, o:o + n], hd, n,
                     lambda c: gm2[:, sg, c:c + 1], lambda c: mod_sb[:, sg, 3, c:c + 1],
                     lambda c: u_sb[:, c, o:o + n], ud, rstd, sq2, tmp2, moddeps=[gm2, mod_sb])
        for f in range(NF):
            wg = wg2[f % 2]
            wu = wu2[f % 2]
            k.dma("pool", wg[:], wgv[:, :, f * 128:(f + 1) * 128], writes=[wg])
            k.dma("pool", wu[:], wuv[:, :, f * 128:(f + 1) * 128], writes=[wu])
            for (s0, n, sg) in g:
                o = s0 - g0
                pg = cx.psum()
                pu = cx.psum()
                for kc in range(8):
                    k.mm(pg[:, :n], wg[:, kc, :], u_sb[:, kc, o:o + n], kc == 0, kc == 7, reads=[wg, ud[kc]], writes=[pg])
                for kc in range(8):
                    k.mm(pu[:, :n], wu[:, kc, :], u_sb[:, kc, o:o + n], kc == 0, kc == 7, reads=[wu, ud[kc]], writes=[pu])
                sgt = sg2[f % 2]
                k.act(sgt[:, :n], pg[:, :n], AF.Silu, reads=[pg], writes=[sgt])
                k.tt("dve", a_sb[:, f, o:o + n], sgt[:, :n], pu[:, :n], ALU.mult, reads=[sgt, pu], writes=[ad[f]])
        for (s0, n, sg) in g:
            o = s0 - g0
            for c in range(8):
                p = cx.psum()
                for f in range(NF):
                    k.mm(p[:, :n], wd_sb[:, f, c * 128:(c + 1) * 128], a_sb[:, f, o:o + n], f == 0, f == NF - 1,
                         reads=[wd_sb, ad[f]], writes=[p])
                k.stt(h_sb[:, c, o:o + n], p[:, :n], mod_sb[:, sg, 5, c:c + 1], h_sb[:, c, o:o + n],
                      ALU.mult, ALU.add, reads=[p, mod_sb, hd[c]], writes=[hd[c]])
        if final:
            for (s0, n, sg) in g:
                o = s0 - g0
                norm_mod(k, cx, lambda c: h_sb[:, c, o:o + n], hd, n,
                         lambda c: ng_sb[:, 1, c:c + 1], None,
                         lambda c: h_sb[:, c, o:o + n], hd, rstd, sq2, tmp2, moddeps=[ng_sb])
        for c in range(8):
            k.dma("sp", oTc[:, c, g0:g0 + G], h_sb[:, c, :G], reads=[hd[c]], writes=[oT])
    return k.finish()


_NC_CACHE = {}


def _get(name, fn, *args):
    key = (name,) + tuple(str(a) for a in args)
    if key not in _NC_CACHE:
        _NC_CACHE[key] = fn(*args)
    return _NC_CACHE[key]


def _launch(nc, maps):
    res = run_bass_kernel_spmd(nc, maps, core_ids=list(range(len(maps))))
    return res.results


def fm_chunks(v):
    v = np.asarray(v)
    lead = v.shape[:-1]
    r = v.reshape(lead + (8, 128))
    return np.ascontiguousarray(np.moveaxis(r, -1, 0))


def run_ada(I):
    cond = np.stack([I["c"][0], I["c"][1], I["c_ctx"]]).astype(np.float32)
    condT = np.ascontiguousarray(cond.reshape(3, 8, 128).transpose(2, 1, 0))
    maps = []
    for i in range(NCORE):
        sl = slice(768 * i, 768 * (i + 1))
        maps.append(dict(condT=condT, w=np.ascontiguousarray(I["ada_w"][:, :, sl]),
                         b=np.ascontiguousarray(np.broadcast_to(I["ada_b"][:, None, sl], (2, 3, 768)))))
    res = _launch(_get("ada", build_ada), maps)
    return np.concatenate([r["out"] for r in res], axis=2)


SEGS_L0 = [[(0, 512, 0), (512, 512, 0)], [(1024, 512, 0), (1536, 512, 0), (2048, 64, 1)]]
SEGS_L1 = [[(0, 512, 0), (512, 512, 0)], [(1024, 512, 0), (1536, 512, 0)]]


def run_tail(layer, hT_list, yT_list, mod, I, final):
    TT = hT_list[0].shape[1]
    segs = SEGS_L0 if TT == 2112 else SEGS_L1
    ng = np.stack([fm_chunks(I["norm2_g"][layer]), fm_chunks(I["final_g"])], axis=1).astype(np.float32)
    if layer == 0:
        w_mo = I["ab_w_out"][0]
    else:
        w_mo = I["na_w_out"][0]
    maps = []
    for i in range(NCORE):
        b = i // 4
        conds = [b, 2] if TT == 2112 else [b]
        modT = np.stack([fm_chunks(mod[layer, cnd].reshape(6, 1024)) for cnd in conds], axis=1)
        maps.append(dict(hT=np.ascontiguousarray(hT_list[i], dtype=np.float32),
                         yT=np.ascontiguousarray(yT_list[i]).astype(NPBF) if yT_list[i].dtype != NPBF else np.ascontiguousarray(yT_list[i]),
                         modT=np.ascontiguousarray(modT, dtype=np.float32), ng=ng,
                         w_mo=np.ascontiguousarray(w_mo), w_g=np.ascontiguousarray(I["ffn_w_gate"][layer]),
                         w_u=np.ascontiguousarray(I["ffn_w_up"][layer]), w_d=np.ascontiguousarray(I["ffn_w_down"][layer])))
    res = _launch(_get("tail", build_tail, TT, segs, final), maps)
    return [r["oT"] for r in res]


def build_mix1(L):
    NHI = L // 128
    TW = min(512, L)
    k = KB()
    xT = k.dram("xT", [D, L], F32, "ExternalInput")
    modT = k.dram("modT", [128, 6, 8], F32, "ExternalInput")
    n1g = k.dram("n1g", [128, 8], F32, "ExternalInput")
    w_in = k.dram("w_in", [D, 512], F32, "ExternalInput")
    cw = k.dram("cw", [128, 3, 3], F32, "ExternalInput")
    cb = k.dram("cb", [128, 3], F32, "ExternalInput")
    cs128 = k.dram("cs128", [128, 256], F32, "ExternalInput")
    tbr = k.dram("tbr", [NHI, 2 * NHI], F32, "ExternalInput")
    tbi = k.dram("tbi", [NHI, 2 * NHI], F32, "ExternalInput")
    ec = k.dram("ec", [128, L], F32, "ExternalInput")
    es = k.dram("es", [128, L], F32, "ExternalInput")
    yfT = k.dram("yfT", [128, L], BF16, "ExternalOutput")
    zT = k.dram("zT", [3, 128, L], BF16, "ExternalOutput")
    cx = Ctx(k)
    mod_sb = k.sb("mod", [128, 6, 8], F32)
    n1g_sb = k.sb("n1g", [128, 8], F32)
    cw_sb = k.sb("cw", [128, 3, 3], F32)
    cb_sb = k.sb("cb", [128, 3], F32)
    k.dma("sp", mod_sb[:], modT[:], writes=[mod_sb])
    k.dma("sp", n1g_sb[:], n1g[:], writes=[n1g_sb])
    k.dma("sp", cw_sb[:], cw[:], writes=[cw_sb])
    k.dma("sp", cb_sb[:], cb[:], writes=[cb_sb])
    gm = k.sb("gm", [128, 8], F32)
    k.ts("dve", gm[:], mod_sb[:, 1, :], 1.0, None, ALU.add, None, reads=[mod_sb], writes=[gm])
    k.tt("dve", gm[:], gm[:], n1g_sb[:], ALU.mult, reads=[gm, n1g_sb], writes=[gm])
    gT = k.sb("gT", [128, 128, NHI], BF16)
    k.push()
    w_sb = k.sb("w", [128, 8, 512], BF16)
    load_w_bf16(k, w_sb, w_sb, w_in.h, 8, 512)
    pT = k.sb("pT", [128, 3, L + 2], BF16)
    for s in range(3):
        k.memset("pool", pT[:, s, 0:1], 0.0, [pT])
        k.memset("pool", pT[:, s, L + 1:L + 2], 0.0, [pT])
    x2 = [k.sb("x", [128, 8, TW], F32) for _ in range(2)]
    u2 = [k.sb("u", [128, 8, TW], BF16) for _ in range(2)]
    xd = [[Dep() for _ in range(8)] for _ in range(2)]
    ud = [[Dep() for _ in range(8)] for _ in range(2)]
    rstd = k.sb("rstd", [128, 512], F32)
    sq2 = [k.sb("sq", [128, 512], BF16) for _ in range(2)]
    tmp2 = [k.sb("tmp", [128, 512], F32) for _ in range(2)]
    xTc = chunked(xT.h)
    pdep = [Dep() for _ in range(L // TW)]
    for t in range(L // TW):
        t0 = t * TW
        xs, us = x2[t % 2], u2[t % 2]
        for c in range(8):
            k.dma("sp", xs[:, c, :], xTc[:, c, t0:t0 + TW], writes=[xd[t % 2][c]])
        norm_mod(k, cx, lambda c: xs[:, c, :], xd[t % 2], TW, lambda c: gm[:, c:c + 1],
                 lambda c: mod_sb[:, 0, c:c + 1], lambda c: us[:, c, :], ud[t % 2], rstd, sq2, tmp2,
                 moddeps=[gm, mod_sb])
        for s in range(4):
            p = cx.psum()
            for kc in range(8):
                k.mm(p[:, :TW], w_sb[:, kc, s * 128:(s + 1) * 128], us[:, kc, :], kc == 0, kc == 7,
                     reads=[w_sb, ud[t % 2][kc]], writes=[p])
            if s == 0:
                nh = TW // 128
                h0 = t0 // 128
                k.copy("act", gT[:, :, h0:h0 + nh].rearrange("p l h -> p h l"),
                       p[:, :TW].rearrange("p (h l) -> p h l", l=128), reads=[p], writes=[gT])
            else:
                k.copy("act", pT[:, s - 1, 1 + t0:1 + t0 + TW], p[:, :TW], reads=[p], writes=[pdep[t], pT])
    CBK = min(2048, L)
    acc2 = [k.sb("acc", [128, CBK], F32) for _ in range(2)]
    zb2 = [k.sb("zb", [128, CBK], BF16) for _ in range(2)]
    i = 0
    for s in range(3):
        for c0 in range(0, L, CBK):
            acc, zb = acc2[i % 2], zb2[i % 2]
            i += 1
            k.ts("dve", acc[:], pT[:, s, 1 + c0:1 + c0 + CBK], cw_sb[:, s, 1:2], cb_sb[:, s:s + 1], ALU.mult, ALU.add,
                 reads=[pT, cw_sb, cb_sb], writes=[acc])
            k.stt(acc[:], pT[:, s, c0:c0 + CBK], cw_sb[:, s, 0:1], acc[:], ALU.mult, ALU.add,
                  reads=[pT, cw_sb, acc], writes=[acc])
            k.stt(zb[:], pT[:, s, 2 + c0:2 + c0 + CBK], cw_sb[:, s, 2:3], acc[:], ALU.mult, ALU.add,
                  reads=[pT, cw_sb, acc], writes=[zb])
            k.dma("sp", zT[s, :, c0:c0 + CBK], zb[:], reads=[zb], writes=[zT])
    k.pop()
    k.push()
    cs_sb = k.sb("cs", [128, 256], BF16)
    tbr_sb = k.sb("tbr", [NHI, 2 * NHI], BF16)
    tbi_sb = k.sb("tbi", [NHI, 2 * NHI], BF16)
    ec_sb = k.sb("ec", [128, NHI, 128], BF16)
    es_sb = k.sb("es", [128, NHI, 128], BF16)
    k.dma("pool", cs_sb[:], cs128[:], writes=[cs_sb])
    k.dma("pool", tbr_sb[:], tbr[:], writes=[tbr_sb])
    k.dma("pool", tbi_sb[:], tbi[:], writes=[tbi_sb])
    ecv = ec.h.rearrange("p (a b) -> p a b", b=128)
    esv = es.h.rearrange("p (a b) -> p a b", b=128)
    st = max(1, NHI // 4)
    for a0 in range(0, NHI, st):
        k.dma("pool", ec_sb[:, a0:a0 + st, :], ecv[:, a0:a0 + st, :], writes=[ec_sb])
        k.dma("pool", es_sb[:, a0:a0 + st, :], esv[:, a0:a0 + st, :], writes=[es_sb])
    A1 = k.sb("A1", [NHI, 2, 128, 128], BF16)
    for l0 in range(0, 128, 2):
        p = cx.psum()
        for j in range(2):
            k.mm(p[:NHI, j * 256:(j + 1) * 256], gT[:, l0 + j, :], cs_sb[:], True, True, reads=[gT, cs_sb], writes=[p])
        k.copy("dve" if (l0 // 2) % 2 == 0 else "act",
               A1[:, :, :, l0:l0 + 2].rearrange("p r c l -> p l r c"),
               p[:NHI, :512].rearrange("p (l r c) -> p l r c", l=2, r=2), reads=[p], writes=[A1])
    B1 = k.sb("B1", [128, 2, NHI, 128], BF16)
    W2 = 2 * NHI
    per = max(1, 512 // W2)
    per = min(per, 128)
    for c0 in range(0, 128, per):
        p = cx.psum()
        for j in range(per):
            kc_ = c0 + j
            k.mm(p[:, j * W2:(j + 1) * W2], A1[:, 0, kc_, :], tbr_sb[:], True, False, reads=[A1, tbr_sb], writes=[p])
            k.mm(p[:, j * W2:(j + 1) * W2], A1[:, 1, kc_, :], tbi_sb[:], False, True, reads=[A1, tbi_sb], writes=[p])
        k.copy("dve" if (c0 // per) % 2 == 0 else "act",
               B1[:, :, :, c0:c0 + per].rearrange("p r k c -> p c r k"),
               p[:, :per * W2].rearrange("p (c r k) -> p c r k", c=per, r=2), reads=[p], writes=[B1])
    yf = k.sb("yf", [128, 128, NHI], BF16)
    scale = 1.0 / float(np.sqrt(L * 128.0))
    per = min(4, NHI)
    for q0 in range(0, NHI, per):
        p = cx.psum()
        for j in range(per):
            kl = q0 + j
            k.mm(p[:, j * 128:(j + 1) * 128], B1[:, 0, kl, :], ec_sb[:, kl, :], True, False, reads=[B1, ec_sb], writes=[p])
            k.mm(p[:, j * 128:(j + 1) * 128], B1[:, 1, kl, :], es_sb[:, kl, :], False, True, reads=[B1, es_sb], writes=[p])
        k.act(yf[:, :, q0:q0 + per].rearrange("p h l -> p l h"),
              p[:, :per * 128].rearrange("p (l h) -> p l h", l=per), AF.Copy, reads=[p], writes=[yf], scale=scale)
    k.dma("sp", yfT[:, :], yf[:].rearrange("p h l -> p (h l)"), reads=[yf], writes=[yfT])
    k.pop()
    return k.finish()


def dft_tables_mix1(L):
    NHI = L // 128
    c = np.arange(128)
    ang = 2 * np.pi * np.outer(c, c) / 128.0
    cs128 = np.concatenate([np.cos(ang), -np.sin(ang)], 1)
    h = np.arange(NHI)
    angb = 2 * np.pi * np.outer(h, h) / NHI
    tbr = np.concatenate([np.cos(angb), -np.sin(angb)], 1)
    tbi = np.concatenate([np.sin(angb), np.cos(angb)], 1)
    nlo = np.arange(128)[:, None, None]
    klo = np.arange(NHI)[None, :, None]
    khi = np.arange(128)[None, None, :]
    ange = 2 * np.pi * ((nlo * (klo + NHI * khi)) % L) / L
    ec = np.cos(ange).reshape(128, L)
    es = np.sin(ange).reshape(128, L)
    f = lambda a: np.ascontiguousarray(a, dtype=np.float32)
    return dict(cs128=f(cs128), tbr=f(tbr), tbi=f(tbi), ec=f(ec), es=f(es))


def run_mix1(L, xT_list, mod_list, I):
    tabs = _get("tab1", dft_tables_mix1, L)
    n1g = fm_chunks(I["norm1_g"][0]).astype(np.float32)
    maps = []
    for i in range(NCORE):
        g = i % 4
        cols = np.concatenate([np.arange(128) + 128 * g] + [512 + s * 512 + 128 * g + np.arange(128) for s in range(3)])
        w = np.ascontiguousarray(I["ab_w_in"][0][:, cols])
        hc = [s * 512 + 128 * g + np.arange(128) for s in range(3)]
        cw = np.stack([I["hy_conv_w"][0][:, c_].T for c_ in hc], axis=1)
        cb = np.stack([I["hy_conv_b"][0][c_] for c_ in hc], axis=1)
        m = dict(xT=np.ascontiguousarray(xT_list[i], dtype=np.float32),
                 modT=np.ascontiguousarray(fm_chunks(mod_list[i].reshape(6, 1024)), dtype=np.float32),
                 n1g=n1g, w_in=w, cw=np.ascontiguousarray(cw, dtype=np.float32),
                 cb=np.ascontiguousarray(cb, dtype=np.float32))
        m.update(tabs)
        maps.append(m)
    res = _launch(_get("mix1", build_mix1, L), maps)
    return [r["yfT"] for r in res], [r["zT"] for r in res]


PI = float(np.pi)


def build_mix2(L):
    NHI = L // 128
    NK = 2 * NHI
    CB = 32 if L > 256 else 128
    NB = 128 // CB
    TW = min(512, L)
    k = KB()
    zT = k.dram("zT", [3, 128, L], BF16, "ExternalInput")
    zf = k.dram("zf", [33, L], F32, "ExternalInput")
    zb = k.dram("zb", [33, L], F32, "ExternalInput")
    w1 = k.dram("w1", [33, 64], F32, "ExternalInput")
    w2 = k.dram("w2", [64, 64], F32, "ExternalInput")
    w3 = k.dram("w3", [64, 2, 2, 128], F32, "ExternalInput")
    fb = k.dram("fb", [64, 3], F32, "ExternalInput")
    decf = k.dram("decf", [NHI, 128, 128], F32, "ExternalInput")
    decb = k.dram("decb", [NHI, 128, 128], F32, "ExternalInput")
    skipb = k.dram("skipb", [128, 2, 128], F32, "ExternalInput")
    t1f = k.dram("t1f", [NHI, 3 * NK], BF16, "ExternalInput")
    t1b = k.dram("t1b", [NHI, 3 * NK], BF16, "ExternalInput")
    e2c = k.dram("e2c", [128, NK, 128], BF16, "ExternalInput")
    e2s = k.dram("e2s", [128, NK, 128], BF16, "ExternalInput")
    tir = k.dram("tir", [128, 256], BF16, "ExternalInput")
    tii = k.dram("tii", [128, 256], BF16, "ExternalInput")
    eic = k.dram("eic", [NK, 128, NHI], BF16, "ExternalInput")
    eis = k.dram("eis", [NK, 128, NHI], BF16, "ExternalInput")
    yhT = k.dram("yhT", [128, L], BF16, "ExternalOutput")
    cx = Ctx(k)
    w1_sb = k.sb("w1", [33, 64], F32)
    w2_sb = k.sb("w2", [64, 64], F32)
    w3_sb = k.sb("w3", [64, 2, 2, 128], BF16)
    fb_sb = k.sb("fb", [64, 3], F32)
    skip_sb = k.sb("skip", [128, 2, 128], F32)
    t1f_sb = k.sb("t1f", [NHI, 3 * NK], BF16)
    t1b_sb = k.sb("t1b", [NHI, 3 * NK], BF16)
    tir_sb = k.sb("tir", [128, 256], BF16)
    tii_sb = k.sb("tii", [128, 256], BF16)
    onesf = k.sb("onesf", [128, 128], F32)
    k.memset("dve", onesf[:], 1.0, [onesf])
    negpi = k.sb("negpi", [128, 1], F32)
    k.memset("dve", negpi[:], 0.0, [negpi])
    for dst, src in ((w1_sb, w1), (w2_sb, w2), (fb_sb, fb), (skip_sb, skipb), (t1f_sb, t1f), (t1b_sb, t1b),
                     (tir_sb, tir), (tii_sb, tii)):
        k.dma("sp", dst[:], src[:], writes=[dst])
    k.dma("pool", w3_sb[:], w3[:], writes=[w3_sb])
    fbb = k.sb("fbb", [64, 2], F32)
    k.ts("dve", fbb[:], fb_sb[:, 1:3], fb_sb[:, 0:1], None, ALU.mult, None, reads=[fb_sb], writes=[fbb])
    h2 = [k.sb("h2", [64, 128, NHI], BF16) for _ in range(2)]
    k.push()
    zt2 = [k.sb("zt", [33, TW], F32) for _ in range(2)]
    y1 = k.sb("y1", [64, TW], F32)
    mk = k.sb("mk", [64, TW], F32)
    h1 = k.sb("h1", [64, TW], F32)

    def sin_layer(ps, col, out_ap, out_dep, in_view=None):
        k.ts("dve", y1[:], ps[:64, :TW], fb_sb[:, 0:1], fbb[:, col:col + 1], ALU.mult, ALU.add,
             reads=[ps, fb_sb, fbb], writes=[y1])
        k.ts("dve", mk[:], y1[:], -PI, None, ALU.is_lt, None, reads=[y1], writes=[mk])
        k.stt(y1[:], mk[:], 2 * PI, y1[:], ALU.mult, ALU.add, reads=[mk, y1], writes=[y1])
        k.ts("dve", mk[:], y1[:], PI, None, ALU.is_gt, None, reads=[y1], writes=[mk])
        k.stt(y1[:], mk[:], -2 * PI, y1[:], ALU.mult, ALU.add, reads=[mk, y1], writes=[y1])
        k.act(out_ap, y1[:] if in_view is None else in_view(y1), AF.Sin, reads=[y1], writes=[out_dep])

    it = 0
    for d, zsrc in enumerate((zf, zb)):
        for t in range(L // TW):
            t0 = t * TW
            zt = zt2[it % 2]
            it += 1
            k.dma("sp", zt[:], zsrc[:, t0:t0 + TW], writes=[zt])
            p = cx.psum()
            k.mm(p[:64, :TW], w1_sb[:], zt[:], True, True, reads=[w1_sb, zt], writes=[p])
            sin_layer(p, 0, h1[:], h1)
            p = cx.psum()
            k.mm(p[:64, :TW], w2_sb[:], h1[:], True, True, reads=[w2_sb, h1], writes=[p])
            nh = TW // 128
            h0 = t0 // 128
            sin_layer(p, 1, h2[d][:, :, h0:h0 + nh].rearrange("p l h -> p h l"), h2[d],
                      in_view=lambda y: y[:].rearrange("p (h l) -> p h l", l=128))
    k.pop()
    S1f = k.sb("S1f", [128, NK, 3, CB], BF16)
    S1v = k.sb("S1v", [128, NK, 3, CB], BF16)
    Y = k.sb("Y", [128, 2, CB, NK], BF16)
    FZ = k.sb("FZ", [128, 2 * 128 * CB], BF16)
    ftv = FZ[:NHI, :].rearrange("p (d c l) -> p d c l", d=2, c=CB)
    Zv = FZ[:NK, :].rearrange("p (r a c) -> p r a c", r=2, a=128)
    tin = k.sb("tin", [NHI, CB, 128], BF16)
    tx1 = k.sb("tx1", [NHI, CB, 128], BF16)
    tv1 = k.sb("tv1", [NHI, CB, 128], BF16)
    x2b = k.sb("x2b", [CB, NHI, 128], BF16)
    yhb = k.sb("yhb", [CB, NHI, 128], BF16)
    GN = min(8, 512 // CB)
    dec2 = [k.sb("dec", [NHI, GN, CB], F32) for _ in range(2)]
    hd2 = [k.sb("hd", [NHI, GN, CB], F32) for _ in range(2)]
    acc = k.sb("acc", [NHI, GN, CB], F32)
    hab2 = [k.sb("hab", [NHI, GN, CB], F32) for _ in range(2)]
    rn = k.sb("rn", [128, CB], F32)
    rn8 = k.sb("rn8", [128, 8, CB], F32)
    KG = min(4, NK, 512 // (2 * CB))
    e2c2 = [k.sb("e2c", [128, KG, 128], BF16) for _ in range(2)]
    e2s2 = [k.sb("e2s", [128, KG, 128], BF16) for _ in range(2)]
    Hg2 = [k.sb("Hg", [128, KG, 2, CB], F32) for _ in range(2)]
    m4 = [k.sb("m4", [128, KG, CB], F32) for _ in range(4)]
    AG = 512 // CB
    AG2 = min(512 // NHI, 128) if NHI <= 512 else 1
    AG2 = min(AG2, 8)
    eic2 = [k.sb("eic", [NK, AG, NHI], BF16) for _ in range(2)]
    eis2 = [k.sb("eis", [NK, AG, NHI], BF16) for _ in range(2)]
    zTv = zT.h
    cnt = [0]

    def alt():
        cnt[0] += 1
        return "dve" if cnt[0] % 2 == 0 else "act"

    def stage1(dst, srcs):
        for c in range(CB):
            p = cx.psum()
            for i, (vf, tab, deps) in enumerate(srcs):
                k.mm(p[:, :3 * NK], vf(c), tab[:], i == 0, i == len(srcs) - 1, reads=deps + [tab], writes=[p])
            k.copy(alt(), dst[:, :, :, c], p[:, :3 * NK].rearrange("p (r q) -> p q r", r=3), reads=[p], writes=[dst])

    for cb in range(NB):
        c0 = cb * CB
        k.dma("sp", tin[:], zTv[2, c0:c0 + CB, :].rearrange("c (h l) -> h c l", l=128), writes=[tin])
        k.dma("sp", tx1[:], zTv[0, c0:c0 + CB, :].rearrange("c (h l) -> h c l", l=128), writes=[tx1])
        k.dma("sp", x2b[:], zTv[1, c0:c0 + CB, :].rearrange("c (h l) -> c h l", l=128), writes=[x2b])
        for o in range(2):
            sig = tin if o == 0 else tv1
            k.memset("dve", acc[:], 0.0, [acc])
            gi = 0
            for d, dsrc in enumerate((decf, decb)):
                for l0 in range(0, 128, GN):
                    dec, hd = dec2[gi % 2], hd2[gi % 2]
                    gi += 1
                    k.dma("sp", dec[:], dsrc[:, l0:l0 + GN, c0:c0 + CB], writes=[dec])
                    p = cx.psum()
                    for j in range(GN):
                        k.mm(p[:NHI, j * CB:(j + 1) * CB], h2[d][:, l0 + j, :], w3_sb[:, d, o, c0:c0 + CB], True, True,
                             reads=[h2[d], w3_sb], writes=[p])
                    k.tt("dve", hd[:], p[:NHI, :GN * CB].rearrange("p (l c) -> p l c", l=GN), dec[:], ALU.mult,
                         reads=[p, dec], writes=[hd])
                    k.copy("pool", ftv[:, d, :, l0:l0 + GN].rearrange("p c l -> p l c"), hd[:], reads=[hd], writes=[FZ])
                    hab = hab2[gi % 2]
                    k.act(hab[:], hd[:], AF.Abs, reads=[hd], writes=[hab])
                    k.tt("pool", acc[:], acc[:], hab[:], ALU.add, reads=[hab, acc], writes=[acc])
            k.memset("pool", ftv[0:1, 1, :, 0:1], 0.0, [FZ])
            p = cx.psum()
            k.mm(p[:, :GN * CB], onesf[:NHI, :], acc[:].rearrange("p l c -> p (l c)"), True, True,
                 reads=[onesf, acc], writes=[p])
            k.op("dve", lambda e: e.tensor_reduce(out=rn[:], in_=p[:, :GN * CB].rearrange("p (l c) -> p c l", l=GN),
                                                  axis=AX.X, op=ALU.add), reads=[p], writes=[rn])
            k.op("dve", lambda e: e.reciprocal(out=rn[:], in_=rn[:]), reads=[rn], writes=[rn])
            for j in range(8):
                k.copy("pool", rn8[:, j, :], rn[:], reads=[rn], writes=[rn8])
            stage1(S1f, [(lambda c: ftv[:, 0, c, :], t1f_sb, [FZ]), (lambda c: ftv[:, 1, c, :], t1b_sb, [FZ])])
            stage1(S1v, [(lambda c: sig[:, c, :], t1f_sb, [sig])])
            for gq, q0 in enumerate(range(0, NK, KG)):
                ecg, esg = e2c2[gq % 2], e2s2[gq % 2]
                k.dma("sp", ecg[:], e2c[:, q0:q0 + KG, :], writes=[ecg])
                k.dma("sp", esg[:], e2s[:, q0:q0 + KG, :], writes=[esg])
                ph = cx.psum()
                px = cx.psum()
                for (pp, S1) in ((ph, S1f), (px, S1v)):
                    for j in range(KG):
                        kl = q0 + j
                        k.mm(pp[:, j * 2 * CB:(j + 1) * 2 * CB], ecg[:, j, :],
                             S1[:, kl, 0:2, :].rearrange("p r c -> p (r c)"), True, False, reads=[ecg, S1], writes=[pp])
                        k.mm(pp[:, j * 2 * CB:(j + 1) * 2 * CB], esg[:, j, :],
                             S1[:, kl, 1:3, :].rearrange("p r c -> p (r c)"), False, True, reads=[esg, S1], writes=[pp])
                Hg = Hg2[gq % 2]
                k.tt("dve", Hg[:].rearrange("p q r c -> p (q r) c"),
                     ph[:, :KG * 2 * CB].rearrange("p (q c) -> p q c", c=CB), rn8[:, :KG * 2, :], ALU.mult,
                     reads=[ph, rn8], writes=[Hg])
                for j in range(KG):
                    k.tt("pool", Hg[:, j, 0, :], Hg[:, j, 0, :], skip_sb[:, o, c0:c0 + CB], ALU.add,
                         reads=[Hg, skip_sb], writes=[Hg])
                pxv = px[:, :KG * 2 * CB].rearrange("p (q r c) -> p q r c", q=KG, r=2)
                k.tt("dve", m4[0][:], pxv[:, :, 0, :], Hg[:, :, 0, :], ALU.mult, reads=[px, Hg], writes=[m4[0]])
                k.tt("dve", m4[1][:], pxv[:, :, 1, :], Hg[:, :, 1, :], ALU.mult, reads=[px, Hg], writes=[m4[1]])
                k.tt("dve", m4[2][:], pxv[:, :, 0, :], Hg[:, :, 1, :], ALU.mult, reads=[px, Hg], writes=[m4[2]])
                k.tt("dve", m4[3][:], pxv[:, :, 1, :], Hg[:, :, 0, :], ALU.mult, reads=[px, Hg], writes=[m4[3]])
                k.tt("pool", Y[:, 0, :, q0:q0 + KG].rearrange("p c q -> p q c"), m4[0][:], m4[1][:], ALU.subtract,
                     reads=[m4[0], m4[1]], writes=[Y])
                k.tt("pool", Y[:, 1, :, q0:q0 + KG].rearrange("p c q -> p q c"), m4[2][:], m4[3][:], ALU.add,
                     reads=[m4[2], m4[3]], writes=[Y])
            for c in range(0, CB, 2):
                p = cx.psum()
                for j in range(2):
                    k.mm(p[:NK, j * 256:(j + 1) * 256], Y[:, 0, c + j, :], tir_sb[:], True, False, reads=[Y, tir_sb], writes=[p])
                    k.mm(p[:NK, j * 256:(j + 1) * 256], Y[:, 1, c + j, :], tii_sb[:], False, True, reads=[Y, tii_sb], writes=[p])
                k.copy(alt(), Zv[:, :, :, c:c + 2].rearrange("p r a c -> p c r a"),
                       p[:NK, :512].rearrange("p (c r a) -> p c r a", c=2, r=2), reads=[p], writes=[FZ])
            if o == 0:
                for ga, a0 in enumerate(range(0, 128, AG)):
                    ec_, es_ = eic2[ga % 2], eis2[ga % 2]
                    k.dma("sp", ec_[:], eic[:, a0:a0 + AG, :], writes=[ec_])
                    k.dma("sp", es_[:], eis[:, a0:a0 + AG, :], writes=[es_])
                    p = cx.psum()
                    for j in range(AG):
                        k.mm(p[:NHI, j * CB:(j + 1) * CB], ec_[:, j, :], Zv[:, 0, a0 + j, :], True, False, reads=[ec_, FZ], writes=[p])
                        k.mm(p[:NHI, j * CB:(j + 1) * CB], es_[:, j, :], Zv[:, 1, a0 + j, :], False, True, reads=[es_, FZ], writes=[p])
                    k.tt("dve", tv1[:, :, a0:a0 + AG].rearrange("p c a -> p a c"),
                         p[:NHI, :AG * CB].rearrange("p (a c) -> p a c", a=AG),
                         tx1[:, :, a0:a0 + AG].rearrange("p c a -> p a c"), ALU.mult, reads=[p, tx1], writes=[tv1])
            else:
                per = max(1, min(AG, 512 // NHI))
                for ga, a0 in enumerate(range(0, 128, AG)):
                    ec_, es_ = eic2[ga % 2], eis2[ga % 2]
                    k.dma("sp", ec_[:], eic[:, a0:a0 + AG, :], writes=[ec_])
                    k.dma("sp", es_[:], eis[:, a0:a0 + AG, :], writes=[es_])
                    for a1 in range(0, AG, per):
                        p = cx.psum()
                        for j in range(per):
                            a = a0 + a1 + j
                            k.mm(p[:CB, j * NHI:(j + 1) * NHI], Zv[:, 0, a, :], ec_[:, a1 + j, :], True, False, reads=[ec_, FZ], writes=[p])
                            k.mm(p[:CB, j * NHI:(j + 1) * NHI], Zv[:, 1, a, :], es_[:, a1 + j, :], False, True, reads=[es_, FZ], writes=[p])
                        aa = a0 + a1
                        k.tt("dve", yhb[:, :, aa:aa + per].rearrange("p b a -> p a b"),
                             p[:CB, :per * NHI].rearrange("p (a b) -> p a b", a=per),
                             x2b[:, :, aa:aa + per].rearrange("p b a -> p a b"), ALU.mult, reads=[p, x2b], writes=[yhb])
                k.dma("sp", yhT[c0:c0 + CB, :], yhb[:].rearrange("p b a -> p (b a)"), reads=[yhb], writes=[yhT])
    return k.finish()


def tables_mix2(L, g):
    NHI = L // 128
    NK = 2 * NHI
    N = 2 * L
    f32 = lambda a: np.ascontiguousarray(a, dtype=np.float32)
    bf = lambda a: np.ascontiguousarray(np.asarray(a, dtype=np.float32).astype(NPBF))

    def ztab(t):
        t = t.astype(np.float64)
        t01 = t / (L - 1)
        bands = np.linspace(1e-4, 15.0, 16)
        ang = (2 * np.pi / L) * t[:, None] * bands[None, :]
        return np.concatenate([t01[:, None], np.cos(ang), -np.sin(ang)], axis=1).T

    tf = np.arange(L)
    tb = L - np.arange(L)
    tb[0] = 0
    max_decay = np.log(1e-2) / 0.3
    min_decay = np.log(1e-2) / 1.5
    delta = np.abs(np.linspace(min_decay, max_decay, 512))[g * 128:(g + 1) * 128]

    def dtab(t):
        t01 = t.astype(np.float64) / (L - 1)
        return np.exp(-t01[:, None] * delta[None, :]).reshape(NHI, 128, 128)

    nh = np.arange(NHI)[:, None]
    kl = np.arange(NK)[None, :]
    a = 2 * np.pi * (nh * kl % NK) / NK
    t1f = np.concatenate([np.cos(a), -np.sin(a), -np.cos(a)], 1)
    a = 2 * np.pi * ((nh + NHI) * kl % NK) / NK
    t1b = np.concatenate([np.cos(a), -np.sin(a), -np.cos(a)], 1)
    nlo = np.arange(128)[:, None, None]
    klo = np.arange(NK)[None, :, None]
    khi = np.arange(128)[None, None, :]
    a = 2 * np.pi * ((nlo * (klo + NK * khi)) % N) / N
    e2c, e2s = np.cos(a), np.sin(a)
    kk = np.arange(128)
    a = 2 * np.pi * np.outer(kk, kk) / 128.0
    tir = np.concatenate([np.cos(a), np.sin(a)], 1)
    tii = np.concatenate([-np.sin(a), np.cos(a)], 1)
    klo = np.arange(NK)[:, None, None]
    na = np.arange(128)[None, :, None]
    nb = np.arange(NHI)[None, None, :]
    a = 2 * np.pi * ((klo * (na + 128 * nb)) % N) / N
    eic, eis = np.cos(a) / N, -np.sin(a) / N
    return dict(zf=f32(ztab(tf)), zb=f32(ztab(tb)), decf=f32(dtab(tf)), decb=f32(dtab(tb)),
                t1f=bf(t1f), t1b=bf(t1b), e2c=bf(e2c), e2s=bf(e2s), tir=bf(tir), tii=bf(tii), eic=bf(eic), eis=bf(eis))


def run_mix2(L, zT_list, I):
    maps = []
    for i in range(NCORE):
        g = i % 4
        tabs = _get("tab2", tables_mix2, L, g)
        w3 = I["hy_f_w3"][0].reshape(64, 2, 2, 512)[:, :, :, g * 128:(g + 1) * 128]
        skipb = np.broadcast_to(I["hy_bias"][0][None, :, g * 128:(g + 1) * 128], (128, 2, 128))
        fb = np.stack([I["hy_f_freq"][0], I["hy_f_b1"][0], I["hy_f_b2"][0]], axis=1)
        m = dict(zT=np.ascontiguousarray(zT_list[i]), w1=np.ascontiguousarray(I["hy_f_w1"][0]),
                 w2=np.ascontiguousarray(I["hy_f_w2"][0]), w3=np.ascontiguousarray(w3, dtype=np.float32),
                 fb=np.ascontiguousarray(fb, dtype=np.float32), skipb=np.ascontiguousarray(skipb, dtype=np.float32))
        m.update(tabs)
        maps.append(m)
    res = _launch(_get("mix2", build_mix2, L), maps)
    return [r["yhT"] for r in res]


FLEX_ROWS = [0, 1, 2, 3, 28, 29, 30, 31]
NTAB = 512 + 8 * 768
NEG = -30000.0


def build_attn(nj=8, lrs=None, nbuf=2, ubar=0):
    k = KB()
    lrs = list(range(32)) if lrs is None else list(lrs)
    NT = 2560
    hT = k.dram("hT", [D, NT], F32, "ExternalInput")
    hcT = k.dram("hcT", [D, 256], F32, "ExternalInput")
    modT = k.dram("modT", [128, 2, 6, 8], F32, "ExternalInput")
    n1g = k.dram("n1g", [128, 8], F32, "ExternalInput")
    w_qkv = k.dram("w_qkv", [D, 3 * D], F32, "ExternalInput")
    BT = k.dram("BT", [16, 64, NTAB], F32, "ExternalInput")
    MK = k.dram("MK", [64, NTAB], F32, "ExternalInput")
    ident = k.dram("ident", [64, 64], F32, "ExternalInput")
    attnT = k.dram("attnT", [2048, D], BF16, "ExternalOutput")
    cx = Ctx(k, npsum=3)
    sc2 = [k.ps("sc", [128, 1024], F32) for _ in range(2)]
    ptp = k.ps("pt", [128, 1024], BF16)
    mod_sb = k.sb("mod", [128, 2, 6, 8], F32)
    n1g_sb = k.sb("n1g", [128, 8], F32)
    idf = k.sb("idf", [64, 64], F32)
    idb = k.sb("idb", [64, 64], BF16)
    onesf = k.sb("onesf", [64, 64], F32)
    k.memset("dve", onesf[:], 1.0, [onesf])
    k.dma("sp", mod_sb[:], modT[:], writes=[mod_sb])
    k.dma("sp", n1g_sb[:], n1g[:], writes=[n1g_sb])
    k.dma("sp", idf[:], ident[:], writes=[idf])
    k.dma("pool", idb[:], ident[:], writes=[idb])
    gm = k.sb("gm", [128, 2, 8], F32)
    for s in range(2):
        k.ts("dve", gm[:, s, :], mod_sb[:, s, 1, :], 1.0, None, ALU.add, None, reads=[mod_sb], writes=[gm])
        k.tt("dve", gm[:, s, :], gm[:, s, :], n1g_sb[:], ALU.mult, reads=[gm, n1g_sb], writes=[gm])
    mk_sb = k.sb("mk", [64, NTAB], BF16)
    for qq in range(4):
        k.dma("pool", mk_sb[:, qq * (NTAB // 4):(qq + 1) * (NTAB // 4)], MK[:, qq * (NTAB // 4):(qq + 1) * (NTAB // 4)],
              writes=[mk_sb])
    uT = k.sb("uT", [128, 8, NT + 256], BF16)
    ud = [Dep() for _ in range(8)]
    k.push()
    x2 = [k.sb("x", [128, 8, 512], F32) for _ in range(2)]
    xd = [[Dep() for _ in range(8)] for _ in range(2)]
    rstd = k.sb("rstd", [128, 512], F32)
    sq2 = [k.sb("sq", [128, 512], BF16) for _ in range(2)]
    tmp2 = [k.sb("tmp", [128, 512], F32) for _ in range(2)]
    hTc = chunked(hT.h)
    hcTc = chunked(hcT.h)
    tiles = [(hTc, t * 512, 512, 0, t * 512) for t in range(5)] + [(hcTc, 0, 256, 1, NT)]
    for ti, (src, s0, n, seg, d0) in enumerate(tiles):
        xs = x2[ti % 2]
        for c in range(8):
            k.dma("sp", xs[:, c, :n], src[:, c, s0:s0 + n], writes=[xd[ti % 2][c]])
        norm_mod(k, cx, lambda c: xs[:, c, :n], xd[ti % 2], n, lambda c: gm[:, seg, c:c + 1],
                 lambda c: mod_sb[:, seg, 0, c:c + 1], lambda c: uT[:, c, d0:d0 + n], ud, rstd, sq2, tmp2,
                 moddeps=[gm, mod_sb])
    k.pop()
    w2 = [k.sb("w", [128, 8, 384], BF16) for _ in range(2)]
    qT2 = [k.sb("qT", [64, 2048], BF16) for _ in range(2)]
    kT2 = [k.sb("kT", [64, NT + 256], BF16) for _ in range(2)]
    V = k.sb("V", [64, 44, 2, 65], BF16)
    k.memset("dve", V[:], 1.0, [V])
    tab2 = [k.sb("tab", [64, NTAB], BF16) for _ in range(2)]
    P2 = [k.sb("P", [64, 1024], BF16) for _ in range(2)]
    PT2 = [k.sb("PT", [64, 1024], BF16) for _ in range(2)]
    at2 = [k.sb("at", [64, 32, 64], BF16) for _ in range(2)]
    nmx2 = [k.sb("nmx", [64, 1], F32) for _ in range(2)]
    rs2 = [k.sb("rs", [64, 1], F32) for _ in range(2)]
    dg2 = [k.sb("dg", [64, 64], F32) for _ in range(2)]
    rc2 = [k.sb("rc", [64, 64], F32) for _ in range(2)]
    wv = w_qkv.h.rearrange("(c p) f -> p c f", p=128)
    un = 0
    for j in range(nj):
        w = w2[j % 2]
        for part in range(3):
            k.dma("pool", w[:, :, part * 128:(part + 1) * 128], wv[:, :, part * D + j * 128: part * D + (j + 1) * 128],
                  writes=[w])
        for hh_ in range(2):
            for t in range(4):
                p = cx.psum()
                for kc in range(8):
                    k.mm(p[:64, :512], w[:, kc, hh_ * 64:(hh_ + 1) * 64], uT[:, kc, 256 + t * 512:256 + (t + 1) * 512],
                         kc == 0, kc == 7, reads=[w, ud[kc]], writes=[p])
                k.act(qT2[hh_][:, t * 512:(t + 1) * 512], p[:64, :512], AF.Copy, reads=[p], writes=[qT2[hh_]], scale=0.125)
            for t in range(6):
                n = 512 if t < 5 else 256
                p = cx.psum()
                for kc in range(8):
                    k.mm(p[:64, :n], w[:, kc, 128 + hh_ * 64:128 + (hh_ + 1) * 64], uT[:, kc, t * 512:t * 512 + n],
                         kc == 0, kc == 7, reads=[w, ud[kc]], writes=[p])
                k.copy("act", kT2[hh_][:, t * 512:t * 512 + n], p[:64, :n], reads=[p], writes=[kT2[hh_]])
        for r0 in range(0, 44, 4):
            p = cx.psum()
            for rr in range(4):
                for kc in range(8):
                    k.mm(p[:64, rr * 128:(rr + 1) * 128], uT[:, kc, (r0 + rr) * 64:(r0 + rr + 1) * 64], w[:, kc, 256:384],
                         kc == 0, kc == 7, reads=[w, ud[kc]], writes=[p])
            for hh_ in range(2):
                k.copy("dve", V[:, r0:r0 + 4, hh_, 0:64],
                       p[:64, :512].rearrange("p (r h c) -> p r h c", r=4, h=2)[:, :, hh_, :], reads=[p], writes=[V])
        units = []
        for hh in range(2):
            for lr in lrs:
                flex = lr in FLEX_ROWS
                if flex:
                    W0, nW, tc0 = (0 if lr < 4 else 28), 12, 512 + FLEX_ROWS.index(lr) * 768
                else:
                    W0, nW, tc0 = lr, 8, 0
                units.append(dict(hh=hh, lr=lr, h=2 * j + hh, flex=flex, W0=W0, nW=nW, tc0=tc0,
                                  ncol=nW * 64 + 256, first=(lr == lrs[0]), last=(lr == lrs[-1])))

        def stage_A(U, ub):
            hh, lr, h = U["hh"], U["lr"], U["h"]
            qT, kT = qT2[hh], kT2[hh]
            tab = tab2[h % 2]
            if U["first"]:
                for qq in range(4):
                    k.dma("pool", tab[:, qq * (NTAB // 4):(qq + 1) * (NTAB // 4)],
                          BT[h, :, qq * (NTAB // 4):(qq + 1) * (NTAB // 4)], writes=[tab])
                k.tt("pool", tab[:], tab[:], mk_sb[:], ALU.add, reads=[tab, mk_sb], writes=[tab])
                if len(lrs) < 32:
                    k.memset("pool", at2[h % 2][:], 0.0, [at2[h % 2]])
            W0, tc0, ncol = U["W0"], U["tc0"], U["ncol"]
            sc, P, nmx = sc2[ub], P2[ub], nmx2[ub]
            qv = qT[:, lr * 64:(lr + 1) * 64]
            k.mm(sc[:64, 0:512], qv, kT[:, W0 * 64:W0 * 64 + 512], True, False, reads=[qT, kT], writes=[sc])
            k.mm(sc[:64, 0:512], idb[:], tab[:, tc0:tc0 + 512], False, True, reads=[idb, tab], writes=[sc])
            c1 = 512
            if U["flex"]:
                k.mm(sc[:64, 512:768], qv, kT[:, (W0 + 8) * 64:(W0 + 12) * 64], True, False, reads=[qT, kT], writes=[sc])
                k.mm(sc[:64, 512:768], idb[:], tab[:, tc0 + 512:tc0 + 768], False, True, reads=[idb, tab], writes=[sc])
                c1 = 768
            k.mm(sc[:64, c1:c1 + 256], qv, kT[:, NT:NT + 256], True, True, reads=[qT, kT], writes=[sc])
            k.op("dve", lambda e: e.tensor_reduce(out=nmx[:], in_=sc[:64, :ncol], axis=AX.X, op=ALU.max, negate=True),
                 reads=[sc], writes=[nmx])
            k.act(P[:, :ncol], sc[:64, :ncol], AF.Exp, reads=[sc, nmx], writes=[P], bias=nmx[:, 0:1], scale=1.0)

        def stage_B(U, ub):
            ncol = U["ncol"]
            P, PT = P2[ub], PT2[ub]
            for ch in range(ncol // 64):
                k.op("pe", lambda e: e.transpose(ptp[:64, ch * 64:(ch + 1) * 64], P[:, ch * 64:(ch + 1) * 64], idb[:]),
                     reads=[P, idb], writes=[ptp], inc=(ch == ncol // 64 - 1))
            k.copy("act", PT[:, :ncol], ptp[:64, :ncol], reads=[ptp], writes=[PT])

        def stage_C(U, ub):
            hh, lr, h, W0, nW = U["hh"], U["lr"], U["h"], U["W0"], U["nW"]
            PT, rc = PT2[ub], rc2[ub]
            at = at2[h % 2]
            nch = U["ncol"] // 64
            po = cx.psum()
            for ch in range(nch):
                vrow = (W0 + ch) if ch < nW else (40 + ch - nW)
                k.mm(po[:64, 0:65], PT[:, ch * 64:(ch + 1) * 64], V[:, vrow, hh, :], ch == 0, ch == nch - 1,
                     reads=[V, PT], writes=[po])
            k.op("dve", lambda e: e.reciprocal(out=rc[:, 0:1], in_=po[:64, 64:65]), reads=[po], writes=[rc])
            k.ts("dve", at[:, lr, :], po[:64, 0:64], rc[:, 0:1], None, ALU.mult, None, reads=[po, rc], writes=[at])
            if U["last"]:
                k.dma("sp", attnT[:, h * 64:(h + 1) * 64].rearrange("(r q) d -> q r d", q=64), at[:],
                      reads=[at], writes=[attnT])

        nu = len(units)
        for st in range(nu + 2):
            if st < nu:
                stage_A(units[st], st % 2)
            if 0 <= st - 1 < nu:
                stage_B(units[st - 1], (st - 1) % 2)
            if 0 <= st - 2 < nu:
                stage_C(units[st - 2], (st - 2) % 2)
    return k.finish()


def attn_tables(rpb, c):
    q = np.arange(64)[:, None]
    col = np.arange(64)[None, :]
    dc = np.clip(col - q + 15, 0, 30)
    wstart = np.clip(q - 8, 0, 48)
    colok = (col >= wstart) & (col < wstart + 16)
    BT = np.zeros((16, 64, NTAB), np.float32)
    MK = np.full((64, NTAB), NEG, np.float32)

    def fill(c0, drs):
        for i, dr in enumerate(drs):
            if dr is None:
                continue
            sl = slice(c0 + i * 64, c0 + (i + 1) * 64)
            BT[:, :, sl] = np.where(colok[None], rpb[:, dr][:, dc], 0.0)
            MK[:, sl] = np.where(colok, 0.0, NEG)

    fill(0, [i + 3 for i in range(8)])
    for f, lr in enumerate(FLEX_ROWS):
        r = 32 * c + lr
        rs = min(max(r - 4, 0), 120)
        base = (32 * c - 4) if lr < 4 else (32 * c + 24)
        drs = []
        for i in range(12):
            rho = base + i
            drs.append(rho - r + 7 if rs <= rho < rs + 8 else None)
        fill(512 + f * 768, drs)
    return BT, MK


def run_attn(h_lat0, h_ctx0, mod, I, ncore=NCORE, **dbg):
    ident = np.eye(64, dtype=np.float32)
    n1g = fm_chunks(I["norm1_g"][1]).astype(np.float32)
    maps = []
    for i in range(ncore):
        b, c = i // 4, i % 4
        hh = np.zeros((40 * 64, D), np.float32)
        r0 = 32 * c - 4
        lo, hi = max(r0, 0), min(r0 + 40, 128)
        hh[(lo - r0) * 64:(hi - r0) * 64] = h_lat0[b, lo * 64:hi * 64]
        modT = np.stack([fm_chunks(mod[1, cnd].reshape(6, 1024)) for cnd in (b, 2)], axis=1)
        BT, MK = _get("attntab%d" % id(I), attn_tables, I["na_rpb"][0], c) if False else attn_tables(I["na_rpb"][0], c)
        maps.append(dict(hT=np.ascontiguousarray(hh.T), hcT=np.ascontiguousarray(h_ctx0[b].T, dtype=np.float32),
                         modT=np.ascontiguousarray(modT, dtype=np.float32), n1g=n1g,
                         w_qkv=np.ascontiguousarray(I["na_w_qkv"][0]), BT=BT, MK=MK, ident=ident))
    res = _launch(_get("attn", build_attn, *dbg.values()), maps)
    return [r["attnT"] for r in res]


def kernel(**inputs):
    I = {k_: np.asarray(v) for k_, v in inputs.items()}
    x = I["x"].astype(np.float32, copy=False)
    ctx = I["ctx"].astype(np.float32, copy=False)
    mod = run_ada(I)
    xT = [np.ascontiguousarray(x[i // 4].T) for i in range(NCORE)]
    yf, z = run_mix1(8192, xT, [mod[0, i // 4] for i in range(NCORE)], I)
    yh = run_mix2(8192, z, I)
    cT = [np.ascontiguousarray(ctx[i // 4].T) for i in range(NCORE)]
    yfc, zc = run_mix1(256, cT, [mod[0, 2] for _ in range(NCORE)], I)
    yhc = run_mix2(256, zc, I)
    yT = [np.concatenate([yf[b * 4 + g] for g in range(4)] + [yh[b * 4 + g] for g in range(4)], axis=0) for b in range(2)]
    yTc = [np.concatenate([yfc[b * 4 + g] for g in range(4)] + [yhc[b * 4 + g] for g in range(4)], axis=0) for b in range(2)]
    hT_l, yT_l = [], []
    for i in range(NCORE):
        b, j = i // 4, i % 4
        hT_l.append(np.concatenate([xT[i][:, 2048 * j:2048 * (j + 1)], cT[i][:, 64 * j:64 * (j + 1)]], axis=1))
        yT_l.append(np.concatenate([yT[b][:, 2048 * j:2048 * (j + 1)], yTc[b][:, 64 * j:64 * (j + 1)]], axis=1))
    o0 = run_tail(0, hT_l, yT_l, mod, I, False)
    h_lat0 = np.empty((2, 8192, D), np.float32)
    h_ctx0 = np.empty((2, 256, D), np.float32)
    for i in range(NCORE):
        b, j = i // 4, i % 4
        h_lat0[b, 2048 * j:2048 * (j + 1)] = o0[i][:, :2048].T
        h_ctx0[b, 64 * j:64 * (j + 1)] = o0[i][:, 2048:].T
    at = run_attn(h_lat0, h_ctx0, mod, I)
    hT_l = [np.ascontiguousarray(h_lat0[i // 4, 2048 * (i % 4):2048 * (i % 4 + 1)].T) for i in range(NCORE)]
    atT = [np.ascontiguousarray(a.T) for a in at]
    o1 = run_tail(1, hT_l, atT, mod, I, True)
    out = np.empty((2, 8192, D), np.float32)
    for i in range(NCORE):
        b, j = i // 4, i % 4
        out[b, 2048 * j:2048 * (j + 1)] = o1[i].T
    return out
```
